# Optimizing a Trainium2 kernel written in Bass

```python
import math
import jax, jax.numpy as jnp
from jax import lax
import numpy as np

D_MODEL = 1024
BATCH = 8
SEQ = 4096
DEPTH = 1

CHUNK = 64
PLE_DIM = 256
EPS = 1e-6

RW_HEADS = 8
RW_HEAD_DIM = 64
RW_DIM = RW_HEADS * RW_HEAD_DIM
RW_LORA_W = 64
RW_LORA_A = 64
RW_LORA_G = 128
RW_GN_EPS = 64e-5

MLA_HEADS = 8
QK_NOPE = 64
QK_ROPE = 32
V_HEAD = 64
Q_LORA = 384
KV_LORA = 256
MLA_DIM = MLA_HEADS * V_HEAD
ROPE_THETA = 10000.0
Q_BLOCK = 128
NEG_INF = -1e30

PEER_HEADS = 8
N_KEYS = 128
N_EXPERTS = N_KEYS * N_KEYS
PEER_TOPK = 16
D_QUERY = 256
HALF_Q = D_QUERY // 2
PEER_TOKEN_BLOCK = 128

RW_COLS = 3 * RW_DIM + RW_LORA_W + RW_LORA_A + RW_LORA_G
MLA_COLS = Q_LORA + KV_LORA + QK_ROPE
GATE_COLS = 2 * D_MODEL
IN_COLS = RW_COLS + MLA_COLS + GATE_COLS

kernel_name = 'hybrid_rwkv7_mla_peer_block'


def rms_norm(x, gain, eps=EPS):
    xf = x.astype(jnp.float32)
    y = xf * lax.rsqrt(jnp.mean(xf * xf, axis=-1, keepdims=True) + eps)
    return (y * gain.astype(jnp.float32)).astype(x.dtype)


def rwkv7_scan(r, decay, k, v, a, b):
    def step(state, inp):
        r_t, w_t, k_t, v_t, a_t, b_t = inp
        sa = jnp.einsum('bhvk,bhk->bhv', state, a_t)
        state = (state * w_t[:, :, None, :] + sa[..., None] * b_t[:, :, None, :]
                 + v_t[..., None] * k_t[:, :, None, :])
        return state, jnp.einsum('bhvk,bhk->bhv', state, r_t)
    xs = tuple(jnp.moveaxis(t.astype(jnp.float32), 1, 0) for t in (r, decay, k, v, a, b))
    bsz, _, heads, n = r.shape
    s0 = jnp.zeros((bsz, heads, n, n), jnp.float32)
    _, ys = lax.scan(step, s0, xs)
    return jnp.moveaxis(ys, 0, 1)


def rwkv7_branch(z, mu, w0, w2, a0, a2, g2, k_k, k_a, r_k, gn_w, gn_b, w_o):
    bsz, seq = z.shape[:2]
    z_prev = jnp.pad(z, ((0, 0), (1, 0), (0, 0)))[:, :-1]
    z = z + mu * (z_prev - z)
    o1, o2, o3 = RW_DIM, 2 * RW_DIM, 3 * RW_DIM
    o4, o5 = o3 + RW_LORA_W, o3 + RW_LORA_W + RW_LORA_A
    r, k, v = z[..., :o1], z[..., o1:o2], z[..., o2:o3]
    zw, za, zg = z[..., o3:o4], z[..., o4:o5], z[..., o5:]
    w = -jax.nn.softplus(-(w0 + jnp.tanh(zw) @ w2)) - 0.5
    iclr = jax.nn.sigmoid(a0 + za @ a2)
    g = jax.nn.sigmoid(zg) @ g2
    hd = lambda t: t.reshape(bsz, seq, RW_HEADS, RW_HEAD_DIM)
    kk = hd(k * k_k).astype(jnp.float32)
    kk = kk / jnp.maximum(jnp.sqrt(jnp.sum(kk * kk, axis=-1, keepdims=True)), 1e-12)
    k = k * (1 + (iclr - 1) * k_a)
    decay = jnp.exp(-jnp.exp(w.astype(jnp.float32)))
    r_h, k_h, v_h, iclr_h = hd(r), hd(k), hd(v), hd(iclr).astype(jnp.float32)
    y = rwkv7_scan(r_h, hd(decay), k_h, v_h, -kk, kk * iclr_h)
    mean = jnp.mean(y, axis=-1, keepdims=True)
    var = jnp.mean(jnp.square(y - mean), axis=-1, keepdims=True)
    y = ((y - mean) * lax.rsqrt(var + RW_GN_EPS)).reshape(bsz, seq, RW_DIM) * gn_w + gn_b
    bonus = jnp.sum(r_h * k_h * r_k, axis=-1, keepdims=True) * v_h
    y = y + bonus.reshape(bsz, seq, RW_DIM)
    return (y * g).astype(z.dtype) @ w_o


def rope_tables(positions):
    inv_freq = ROPE_THETA ** (-jnp.arange(0, QK_ROPE, 2, dtype=jnp.float32) / QK_ROPE)
    ang = positions.astype(jnp.float32)[..., None] * inv_freq
    return jnp.cos(ang), jnp.sin(ang)


def apply_rope(x, cos, sin):
    half = x.shape[-1] // 2
    xf = x.astype(jnp.float32)
    x1, x2 = xf[..., :half], xf[..., half:]
    return jnp.concatenate([x1 * cos - x2 * sin, x2 * cos + x1 * sin], axis=-1).astype(x.dtype)


def chunk_causal_attention(q_nope, q_rope, k_nope, k_rope, v):
    bsz, seq, heads, _ = q_nope.shape
    n_blk = seq // Q_BLOCK
    scale = 1.0 / math.sqrt(QK_NOPE + QK_ROPE)
    key_chunk = jnp.arange(seq) // CHUNK

    def to_blocks(t):
        return jnp.moveaxis(t.reshape(bsz, n_blk, Q_BLOCK, *t.shape[2:]), 1, 0)

    def one_block(args):
        qn, qr, blk = args
        s = (jnp.einsum('bqhd,bkhd->bhqk', qn, k_nope)
             + jnp.einsum('bqhr,bkr->bhqk', qr, k_rope)).astype(jnp.float32) * scale
        q_chunk = (blk * Q_BLOCK + jnp.arange(Q_BLOCK)) // CHUNK
        mask = key_chunk[None, :] <= q_chunk[:, None]
        s = jnp.where(mask, s, NEG_INF)
        prob = jax.nn.softmax(s, axis=-1).astype(v.dtype)
        return jnp.einsum('bhqk,bkhd->bqhd', prob, v)

    out = lax.map(one_block, (to_blocks(q_nope), to_blocks(q_rope), jnp.arange(n_blk)))
    return jnp.moveaxis(out, 0, 1).reshape(bsz, seq, heads, V_HEAD)


def mla_branch(z, positions, q_norm, w_uq, kv_norm, w_ukv, w_o):
    bsz, seq = z.shape[:2]
    c_q = rms_norm(z[..., :Q_LORA], q_norm)
    c_kv = rms_norm(z[..., Q_LORA:Q_LORA + KV_LORA], kv_norm)
    k_rope = z[..., Q_LORA + KV_LORA:]
    q = (c_q @ w_uq).reshape(bsz, seq, MLA_HEADS, QK_NOPE + QK_ROPE)
    kv = (c_kv @ w_ukv).reshape(bsz, seq, MLA_HEADS, QK_NOPE + V_HEAD)
    q_nope, q_rope = q[..., :QK_NOPE], q[..., QK_NOPE:]
    k_nope, v = kv[..., :QK_NOPE], kv[..., QK_NOPE:]
    cos, sin = rope_tables(positions)
    q_rope = apply_rope(q_rope, cos[:, :, None, :], sin[:, :, None, :])
    k_rope = apply_rope(k_rope, cos, sin)
    o = chunk_causal_attention(q_nope, q_rope, k_nope, k_rope, v)
    return o.reshape(bsz, seq, MLA_DIM) @ w_o


def peer_ffn(h, w_q, sub_keys, u_tab, v_tab):
    bsz, seq, d = h.shape
    hb = h.reshape(bsz * seq // PEER_TOKEN_BLOCK, PEER_TOKEN_BLOCK, d)

    def one_block(xb):
        t = xb.shape[0]
        q = (xb @ w_q).reshape(t, PEER_HEADS, 2, HALF_Q)
        s = jnp.einsum('thcd,hcnd->thcn', q, sub_keys).astype(jnp.float32)
        top_s, top_i = lax.top_k(s, PEER_TOPK)
        cand_s = (top_s[:, :, 0, :, None] + top_s[:, :, 1, None, :]).reshape(t, PEER_HEADS, PEER_TOPK * PEER_TOPK)
        cand_i = (top_i[:, :, 0, :, None] * N_KEYS + top_i[:, :, 1, None, :]).reshape(t, PEER_HEADS, PEER_TOPK * PEER_TOPK)
        best_s, best_pos = lax.top_k(cand_s, PEER_TOPK)
        idx = jnp.take_along_axis(cand_i, best_pos, axis=-1)
        gate = jax.nn.softmax(best_s, axis=-1).astype(xb.dtype)
        act = jax.nn.gelu(jnp.einsum('td,thkd->thk', xb, u_tab[idx]), approximate=False)
        return jnp.einsum('thk,thkd->td', gate * act, v_tab[idx])

    return lax.map(one_block, hb).reshape(bsz, seq, d)


def setup_inputs(seed: int = 0) -> dict:
    key = jax.random.key(seed)
    ks = iter(jax.random.split(key, 40))
    nrm = lambda shape, scale: jax.random.normal(next(ks), shape, jnp.float32) * scale
    gain = lambda shape: 1.0 + nrm(shape, 0.02)
    L = DEPTH
    offset = jax.random.randint(next(ks), (BATCH, 1), 0, 10000, jnp.int32)
    return {
        'x': nrm((BATCH, SEQ, D_MODEL), 1.0),
        'p': nrm((DEPTH, BATCH, SEQ, PLE_DIM), 1.0),
        'positions': offset + jnp.arange(SEQ, dtype=jnp.int32)[None, :],
        'norm_mix': gain((L, D_MODEL)),
        'w_in': nrm((L, D_MODEL, IN_COLS), D_MODEL ** -0.5),
        'rw_mu': jax.random.uniform(next(ks), (L, RW_COLS), jnp.float32),
        'rw_w0': jax.random.uniform(next(ks), (L, RW_DIM), jnp.float32, -6.0, -1.0),
        'rw_w2': nrm((L, RW_LORA_W, RW_DIM), 0.1),
        'rw_a0': nrm((L, RW_DIM), 0.1),
        'rw_a2': nrm((L, RW_LORA_A, RW_DIM), RW_LORA_A ** -0.5),
        'rw_g2': nrm((L, RW_LORA_G, RW_DIM), RW_LORA_G ** -0.5),
        'rw_k_k': 0.85 + nrm((L, RW_DIM), 0.02),
        'rw_k_a': gain((L, RW_DIM)),
        'rw_r_k': nrm((L, RW_HEADS, RW_HEAD_DIM), 0.1),
        'rw_gn_w': gain((L, RW_DIM)),
        'rw_gn_b': nrm((L, RW_DIM), 0.01),
        'rw_w_o': nrm((L, RW_DIM, D_MODEL), RW_DIM ** -0.5),
        'mla_q_norm': gain((L, Q_LORA)),
        'mla_w_uq': nrm((L, Q_LORA, MLA_HEADS * (QK_NOPE + QK_ROPE)), Q_LORA ** -0.5),
        'mla_kv_norm': gain((L, KV_LORA)),
        'mla_w_ukv': nrm((L, KV_LORA, MLA_HEADS * (QK_NOPE + V_HEAD)), KV_LORA ** -0.5),
        'mla_w_o': nrm((L, MLA_DIM, D_MODEL), MLA_DIM ** -0.5),
        'w_out': nrm((L, D_MODEL, D_MODEL), D_MODEL ** -0.5),
        'norm_ffn': gain((L, D_MODEL)),
        'peer_w_q': nrm((L, D_MODEL, PEER_HEADS * D_QUERY), D_MODEL ** -0.5),
        'peer_sub_keys': nrm((L, PEER_HEADS, 2, N_KEYS, HALF_Q), HALF_Q ** -0.5),
        'peer_u': nrm((L, N_EXPERTS, D_MODEL), D_MODEL ** -0.5),
        'peer_v': nrm((L, N_EXPERTS, D_MODEL), D_MODEL ** -0.5),
        'norm_ple': gain((L, D_MODEL)),
        'ple_w_gate': nrm((L, D_MODEL, D_MODEL), D_MODEL ** -0.5),
        'ple_w_proj': nrm((L, PLE_DIM, D_MODEL), 0.5 * PLE_DIM ** -0.5),
        'norm_final': gain((D_MODEL,)),
    }


def reference(x, p, positions, norm_mix, w_in, rw_mu, rw_w0, rw_w2, rw_a0, rw_a2, rw_g2,
              rw_k_k, rw_k_a, rw_r_k, rw_gn_w, rw_gn_b, rw_w_o, mla_q_norm, mla_w_uq,
              mla_kv_norm, mla_w_ukv, mla_w_o, w_out, norm_ffn, peer_w_q, peer_sub_keys,
              peer_u, peer_v, norm_ple, ple_w_gate, ple_w_proj, norm_final):
    for i in range(DEPTH):
        h = rms_norm(x, norm_mix[i])
        z = h @ w_in[i]
        z_rw = z[..., :RW_COLS]
        z_mla = z[..., RW_COLS:RW_COLS + MLA_COLS]
        gates = jax.nn.sigmoid(z[..., RW_COLS + MLA_COLS:])
        gate_a, gate_b = gates[..., :D_MODEL], gates[..., D_MODEL:]
        y_a = rwkv7_branch(z_rw, rw_mu[i], rw_w0[i], rw_w2[i], rw_a0[i], rw_a2[i], rw_g2[i],
                           rw_k_k[i], rw_k_a[i], rw_r_k[i], rw_gn_w[i], rw_gn_b[i], rw_w_o[i])
        y_b = mla_branch(z_mla, positions, mla_q_norm[i], mla_w_uq[i], mla_kv_norm[i],
                         mla_w_ukv[i], mla_w_o[i])
        x = x + (gate_a * y_a + gate_b * y_b) @ w_out[i]
        x = x + peer_ffn(rms_norm(x, norm_ffn[i]), peer_w_q[i], peer_sub_keys[i], peer_u[i], peer_v[i])
        x = x + jax.nn.sigmoid(rms_norm(x, norm_ple[i]) @ ple_w_gate[i]) * (p[i] @ ple_w_proj[i])
    return rms_norm(x, norm_final)
```

```python
import math
from contextlib import ExitStack
import numpy as np
import concourse.bass as bass
import concourse.mybir as mybir
from concourse.bass_utils import run_bass_kernel_spmd

F32 = mybir.dt.float32
BF16 = mybir.dt.bfloat16
I32 = mybir.dt.int32
AF = mybir.ActivationFunctionType
ALU = mybir.AluOpType
AX = mybir.AxisListType

D = 1024
NCOL = 4512
NG = 36
EPS = 1e-6
GN_EPS = 64e-5
NDSEM = 12


class Buf:
    __slots__ = ("name", "w", "r")

    def __init__(self, name):
        self.name = name
        self.w = None
        self.r = []


class TT:
    def __init__(self, t, buf):
        self.t = t
        self.buf = buf

    def __getitem__(self, k):
        return self.t[k]


class Prog:
    def __init__(self, nc, es):
        self.nc = nc
        self.es = es
        self.engs = ["pe", "act", "dve", "pool", "sp"]
        self.q = {e: [] for e in self.engs}
        self.cnt = {e: 0 for e in self.engs}
        self.sem = {}
        for e in ["pe", "act", "dve", "pool"]:
            self.sem[e] = es.enter_context(nc.semaphore("s_" + e))
        self.dsem = [es.enter_context(nc.semaphore("d%d" % i)) for i in range(NDSEM)]
        self.dcnt = [0] * NDSEM
        self.dnext = 0
        self.seen = {e: {} for e in self.engs}
        self.nb = 0
        self.epoch = 0
        self.semtab = {(e, 0): self.sem[e] for e in self.sem}

    def buf(self, name=None):
        self.nb += 1
        return Buf(name or "b%d" % self.nb)

    def sb(self, es, name, shape, dt):
        t = es.enter_context(self.nc.sbuf_tensor(name, list(shape), dt))
        return TT(t, self.buf(name))

    def ps(self, es, name, shape, dt=F32):
        t = es.enter_context(self.nc.psum_tensor(name, list(shape), dt))
        return TT(t, self.buf(name))

    def _semobj(self, key):
        return self.semtab[key] if isinstance(key, tuple) else self.dsem[key]

    def _need(self, eng, ev, waits):
        if ev is None:
            return
        key, val, peng = ev
        if eng == "pe" and peng == "pe":
            return
        if isinstance(key, tuple) and key[1] < self.epoch:
            return
        if self.seen[eng].get(key, 0) >= val:
            return
        if waits.get(key, 0) < val:
            waits[key] = val

    def op(self, eng, fn, reads=(), writes=(), dma=False):
        waits = {}
        for b in reads:
            b = b.buf if isinstance(b, TT) else b
            self._need(eng, b.w, waits)
        for b in writes:
            b = b.buf if isinstance(b, TT) else b
            self._need(eng, b.w, waits)
            for ev in b.r:
                self._need(eng, ev, waits)
        if dma:
            j = self.dnext
            self.dnext = (self.dnext + 1) % NDSEM
            if self.dcnt[j] > 0:
                ev = (j, 16 * self.dcnt[j], "dma")
                self._need(eng, ev, waits)
            self.dcnt[j] += 1
            ev = (j, 16 * self.dcnt[j], "dma")
            semo, inc = self.dsem[j], 16
        else:
            self.cnt[eng] += 1
            ev = ((eng, self.epoch), self.cnt[eng], eng)
            semo, inc = self.sem[eng], 1
        for k, v in waits.items():
            self.seen[eng][k] = v
        wl = [(self._semobj(k), v) for k, v in waits.items()]

        def thunk(e, wl=wl, fn=fn, semo=semo, inc=inc):
            for s, v in wl:
                e.wait_ge(s, v)
            fn(e).then_inc(semo, inc)

        self.q[eng].append(thunk)
        for b in reads:
            b = b.buf if isinstance(b, TT) else b
            b.r.append(ev)
            if len(b.r) > 64:
                mx = {}
                for (k, v, pe) in b.r:
                    if k not in mx or mx[k][1] < v:
                        mx[k] = (k, v, pe)
                b.r = list(mx.values())
        for b in writes:
            b = b.buf if isinstance(b, TT) else b
            b.w = ev
            b.r = []
        return ev

    def dma(self, out, in_, reads=(), writes=(), eng="sp", **kw):
        return self.op(eng, lambda e: e.dma_start(out=out, in_=in_, **kw), reads, writes, dma=True)

    def barrier(self):
        tot = {(e, self.epoch): self.cnt[e] for e in ["pe", "act", "dve", "pool"]}
        dt = {j: 16 * self.dcnt[j] for j in range(NDSEM)}
        for eng in self.engs:
            wl = []
            for k, v in list(tot.items()) + list(dt.items()):
                if v > 0 and self.seen[eng].get(k, 0) < v and not (isinstance(k, tuple) and k[0] == eng):
                    wl.append((self._semobj(k), v))
                    self.seen[eng][k] = v

            def thunk(e, wl=wl):
                for s, v in wl:
                    e.wait_ge(s, v)

            self.q[eng].append(thunk)
        if max(self.cnt.values()) > 6000:
            self.epoch += 1
            for e in ["pe", "act", "dve", "pool"]:
                self.sem[e] = self.es.enter_context(self.nc.semaphore("s_%s_%d" % (e, self.epoch)))
                self.semtab[(e, self.epoch)] = self.sem[e]
                self.cnt[e] = 0

    def emit(self):
        nc = self.nc
        with nc.Block() as block:
            @block.tensor
            def _(e):
                for f in self.q["pe"]:
                    f(e)

            @block.scalar
            def _(e):
                for f in self.q["act"]:
                    f(e)

            @block.vector
            def _(e):
                for f in self.q["dve"]:
                    f(e)

            @block.gpsimd
            def _(e):
                for f in self.q["pool"]:
                    f(e)

            @block.sync
            def _(e):
                for f in self.q["sp"]:
                    f(e)


def bc(ap, shape):
    return ap.to_broadcast(list(shape))


class K:
    def __init__(self, S, debug=False):
        self.S = S
        self.debug = debug
        self.nc = bass.Bass("TRN2", target_bir_lowering=False)
        self.inp = {}
        self.rr = 0
        self.split_delta = True
        self.prep_done = 0

    def din(self, name, shape, dt=F32):
        a = self.nc.dram_tensor(name, list(shape), dt, kind="ExternalInput").ap()
        self.inp[name] = a
        return a

    def dscr(self, name, shape, dt=F32):
        return TT(self.nc.dram_tensor(name, list(shape), dt, kind="Internal").ap(), Buf(name))

    def ev_eng(self):
        self.rr += 1
        return ["act", "dve"][self.rr % 2]

    def act(self, out, in_, func, reads, writes, bias=None, scale=1.0, accum=None):
        kw = {}
        if bias is not None:
            kw["bias"] = bias
        if accum is not None:
            kw["accum_out"] = accum
        return self.P.op("act", lambda e: e.activation(out=out, in_=in_, func=func, scale=scale, **kw),
                         reads, writes)

    def tt(self, eng, out, in0, in1, op, reads, writes):
        return self.P.op(eng, lambda e: e.tensor_tensor(out=out, in0=in0, in1=in1, op=op), reads, writes)

    def ts(self, eng, out, in0, s1, op0, reads, writes, s2=None, op1=None):
        if op1 is None:
            return self.P.op(eng, lambda e: e.tensor_scalar(out=out, in0=in0, scalar1=s1, scalar2=None, op0=op0),
                             reads, writes)
        return self.P.op(eng, lambda e: e.tensor_scalar(out=out, in0=in0, scalar1=s1, scalar2=s2, op0=op0, op1=op1),
                         reads, writes)

    def stt(self, eng, out, in0, scalar, in1, op0, op1, reads, writes):
        return self.P.op(eng, lambda e: e.scalar_tensor_tensor(out=out, in0=in0, scalar=scalar, in1=in1,
                                                               op0=op0, op1=op1), reads, writes)

    def copy(self, eng, out, in_, reads, writes):
        if eng == "act":
            return self.act(out, in_, AF.Copy, reads, writes)
        return self.P.op(eng, lambda e: e.tensor_copy(out=out, in_=in_), reads, writes)

    def mm(self, out, lhsT, rhs, start, stop, reads, writes):
        return self.P.op("pe", lambda e: e.matmul(out, lhsT, rhs, start=start, stop=stop), reads, writes)

    def tr(self, out, in_, ident, reads, writes):
        return self.P.op("pe", lambda e: e.transpose(out, in_, ident), reads, writes)

    def memset(self, eng, ap, val, writes):
        return self.P.op(eng, lambda e: e.memset(ap, val), (), writes)

    def rsqrt(self, out, in_, mul, add, reads, writes, eng="dve"):
        self.ts(eng, out, in_, mul, ALU.mult, reads, writes, s2=add, op1=ALU.add)
        self.act(out, out, AF.Sqrt, writes, writes)
        self.P.op("dve", lambda e: e.reciprocal(out=out, in_=out), writes, writes)

    def next_ps(self):
        self.psi = (self.psi + 1) % len(self.pspool)
        return self.pspool[self.psi]


def host_consts():
    c = {}
    c["ident"] = np.eye(128, dtype=np.float32)
    bo = np.zeros((128, 128), np.float32)
    bo[:64, :64] = 1.0
    bo[64:, 64:] = 1.0
    c["blockones"] = bo
    c["ones"] = np.ones((128, 128), np.float32)
    s = np.arange(64)[:, None]
    t = np.arange(64)[None, :]
    m = np.zeros((64, 3, 8, 64), np.float32)
    m[:, 0] = (s < t).astype(np.float32)[:, None, :]
    m[:, 1] = (s <= t).astype(np.float32)[:, None, :]
    m[:, 2] = (t < s).astype(np.float32)[:, None, :]
    c["masks"] = m.reshape(64, 3 * 8 * 64)
    invf = (10000.0 ** (-np.arange(0, 32, 2, dtype=np.float32) / 32)).astype(np.float32)
    rc = np.zeros((128, 4), np.float32)
    rc[64:80, 0] = invf
    rc[80:96, 0] = invf
    rc[64:80, 1] = -1.0
    rc[80:96, 1] = 1.0
    rc[:, 2] = math.pi / 2
    rc[:, 3] = 0.0
    c["ropec"] = rc
    return c


def _build_phase_a(self):
    P, nc, S = self.P, self.nc, self.S
    with ExitStack() as es:
        win = P.sb(es, "win", [128, 8, NCOL], BF16)
        stg = [P.sb(es, "wstg%d" % i, [128, NCOL], F32) for i in range(2)]
        gcol = P.sb(es, "gcol", [128, 8], F32)
        xt = [P.sb(es, "xt%d" % i, [128, D], F32) for i in range(2)]
        junk = P.sb(es, "junk", [128, D], F32)
        hb = [P.sb(es, "hb%d" % i, [128, D], BF16) for i in range(2)]
        ss = [P.sb(es, "ss%d" % i, [128, 1], F32) for i in range(2)]
        hT = [P.sb(es, "hT%d" % i, [128, 8, 512], BF16) for i in range(2)]
        zst = [P.sb(es, "zst%d" % i, [128, 512], F32) for i in range(4)]
        zero = P.sb(es, "zero", [128, 16], F32)
        pst = [P.ps(es, "pst%d" % i, [128, 8, 128], BF16) for i in range(2)]
        psz = [P.ps(es, "psz%d" % i, [128, 512], F32) for i in range(4)]
        w_in = self.inp["w_in"]
        P.dma(gcol[:, :], self.inp["norm_mix"].rearrange("(k p) -> p k", p=128), writes=[gcol],
              allow_slow_non_contiguous=True)
        self.memset("pool", zero[:, :], 0.0, [zero])
        P.dma(self.zT.t[0:14 * 128, 0:1].rearrange("(g p) o -> p (g o)", p=128), zero[:, 0:14],
              reads=[zero], writes=[self.zT], allow_slow_non_contiguous=True)
        for kc in range(8):
            st = stg[kc % 2]
            P.dma(st[:, :], w_in[kc * 128:(kc + 1) * 128, :], writes=[st])
            self.copy(["act", "dve", "pool"][kc % 3], win[:, kc, :], st[:, :], [st], [win])
        nsub = S // 128
        for ti in range(S // 512):
            h_T = hT[ti % 2]
            for sj in range(4):
                si = ti * 4 + sj
                x_t, h_b, s_s, p_t = xt[si % 2], hb[si % 2], ss[si % 2], pst[si % 2]
                P.dma(x_t[:, :], self.inp["x"][si * 128:(si + 1) * 128, :], writes=[x_t])
                self.act(junk[:, :], x_t[:, :], AF.Square, [x_t], [junk, s_s], accum=s_s[:, :])
                self.rsqrt(s_s[:, :], s_s[:, :], 1.0 / D, EPS, [s_s], [s_s])
                self.ts("dve", h_b[:, :], x_t[:, :], s_s[:, 0:1], ALU.mult, [x_t, s_s], [h_b])
                for kc in range(8):
                    self.tr(p_t[:, kc, :], h_b[:, kc * 128:(kc + 1) * 128], self.identb[:, :],
                            [h_b, self.identb], [p_t])
                self.tt("dve", h_T[:, :, sj * 128:(sj + 1) * 128], p_t[:, :, :],
                        bc(gcol[:, :].unsqueeze(2), [128, 8, 128]), ALU.mult, [p_t, gcol], [h_T])
            for g in range(NG):
                c0 = g * 128 if g < 19 else (2432 if g == 19 else 2464 + (g - 20) * 128)
                cw = 32 if g == 19 else 128
                pz = psz[g % 4]
                zs = zst[g % 4]
                for kc in range(8):
                    self.mm(pz[0:cw, :], win[:, kc, c0:c0 + cw], h_T[:, kc, :], kc == 0, kc == 7,
                            [win, h_T], [pz])
                if g >= 20:
                    self.act(zs[0:cw, :], pz[0:cw, :], AF.Sigmoid, [pz], [zs])
                else:
                    self.copy(self.ev_eng(), zs[0:cw, :], pz[0:cw, :], [pz], [zs])
                P.dma(self.zT.t[g * 128:g * 128 + cw, 1 + ti * 512:1 + (ti + 1) * 512], zs[0:cw, :],
                      reads=[zs], writes=[self.zT])
    P.barrier()


K.phase_a = _build_phase_a


def _build(self):
    nc, S = self.nc, self.S
    inp = self.din
    inp("x", [S, D]); inp("p", [S, 256]); inp("pos", [1, S], I32)
    inp("norm_mix", [D]); inp("w_in", [D, NCOL]); inp("rw_mu", [1792]); inp("rw_w0", [512])
    inp("rw_w2", [64, 512]); inp("rw_a0", [512]); inp("rw_a2", [64, 512]); inp("rw_g2", [128, 512])
    inp("rw_k_k", [512]); inp("rw_k_a", [512]); inp("rw_r_k", [512]); inp("rw_gn_w", [512])
    inp("rw_gn_b", [512]); inp("rw_w_o", [512, D]); inp("mla_q_norm", [384]); inp("mla_w_uq", [384, 768])
    inp("mla_kv_norm", [256]); inp("mla_w_ukv", [256, 1024]); inp("mla_w_o", [512, D]); inp("w_out", [D, D])
    inp("norm_ffn", [D]); inp("peer_w_q", [D, 2048]); inp("peer_sub_keys", [16, 128, 128])
    inp("peer_u", [16384, D]); inp("peer_v", [16384, D]); inp("norm_ple", [D]); inp("ple_w_gate", [D, D])
    inp("ple_w_proj", [256, D]); inp("norm_final", [D])
    for k, v in host_consts().items():
        inp("c_" + k, list(v.shape))
    self.out = nc.dram_tensor("out", [S, D], F32, kind="ExternalOutput").ap()
    self.zT = self.dscr("zT", [NG * 128, S + 1])
    self.ygT = self.dscr("ygT", [512, S], BF16)
    self.x1 = self.dscr("x1", [S, D])
    self.oTs = self.dscr("oTs", [512, S], BF16)
    self.wqs = self.dscr("wqs", [1024, 2048], BF16)
    self.uTs = self.dscr("uTs", [1024, 16384], BF16)
    self.vs = self.dscr("vs", [16384, 1024], BF16)
    if self.debug:
        self.dbg = {}
    with ExitStack() as es:
        self.P = P = Prog(nc, es)
        self.ident = P.sb(es, "ident", [128, 128], F32)
        self.identb = P.sb(es, "identb", [128, 128], BF16)
        self.blockones = P.sb(es, "blockones", [128, 128], F32)
        self.ones = P.sb(es, "onesf", [128, 128], F32)
        P.dma(self.ident[:, :], self.inp["c_ident"], writes=[self.ident])
        P.dma(self.blockones[:, :], self.inp["c_blockones"], writes=[self.blockones])
        P.dma(self.ones[:, :], self.inp["c_ones"], writes=[self.ones])
        self.copy("dve", self.identb[:, :], self.ident[:, :], [self.ident], [self.identb])
        P.barrier()
        self.phase_a()
        if self.debug != "a":
            self.phase_b()
        if self.debug not in ("a", "b"):
            self.phase_c()
        if self.debug not in ("a", "b", "c"):
            self.phase_d()
        if self.debug == "c":
            P.dma(self.out[:, :], self.x1.t[:, :], reads=[self.x1])
        if self.debug == "b":
            with ExitStack() as es2:
                d1 = P.sb(es2, "dbg1", [128, 4, 512], BF16)
                d2 = P.sb(es2, "dbg2", [128, 4, 512], F32)
                P.dma(d1[:, :, :], self.ygT.t[0:512, 0:512].rearrange("(g p) t -> p g t", p=128), reads=[self.ygT], writes=[d1])
                self.copy("dve", d2[:, :, :], d1[:, :, :], [d1], [d2])
                P.dma(self.out[0:512, 0:512].rearrange("(g p) t -> p g t", p=128), d2[:, :, :], reads=[d2])
                P.barrier()
        if self.debug == "a":
            P.dma(self.out[0:512, 0:512], self.zT.t[0:512, 1:513], reads=[self.zT], eng="sp")
            P.dma(self.out[0:512, 512:1024], self.zT.t[2560:3072, 1:513], reads=[self.zT], eng="sp")
        P.barrier()
        P.emit()
    return nc


K.build = _build


def make_inputs(S, b, x, p, positions, **w):
    m = {"x": np.ascontiguousarray(x[b]), "p": np.ascontiguousarray(p[0, b]),
         "pos": np.ascontiguousarray(positions[b].reshape(1, S).astype(np.int32))}
    for k, v in w.items():
        a = np.asarray(v)
        if k == "norm_final":
            m[k] = np.ascontiguousarray(a)
        elif k == "rw_r_k":
            m[k] = np.ascontiguousarray(a[0].reshape(512))
        elif k == "peer_sub_keys":
            m[k] = np.ascontiguousarray(a[0].reshape(16, 128, 128))
        else:
            m[k] = np.ascontiguousarray(a[0])
    for k, v in host_consts().items():
        m["c_" + k] = v
    return m


def kernel(x, p, positions, **w):
    x = np.asarray(x); p = np.asarray(p); positions = np.asarray(positions)
    B, S = x.shape[0], x.shape[1]
    kb = K(S)
    nc = kb.build()
    in_maps = [make_inputs(S, b, x, p, positions, **w) for b in range(B)]
    res = run_bass_kernel_spmd(nc, in_maps, core_ids=list(range(B)))
    return np.stack([r["out"] for r in res.results], axis=0).astype(np.float32)


def _col(self, es, name, src, ng):
    t = self.P.sb(es, name, [128, ng], F32)
    self.P.dma(t[:, :], self.inp[src].rearrange("(g p) -> p g", p=128), writes=[t],
               allow_slow_non_contiguous=True)
    return t


K.col = _col


def _build_phase_b(self):
    P, nc, S = self.P, self.nc, self.S
    with ExitStack() as es:
        sb = lambda n, sh, dt=F32: P.sb(es, n, sh, dt)
        mu = self.col(es, "mu", "rw_mu", 14)
        w0 = self.col(es, "w0c", "rw_w0", 4)
        a0 = self.col(es, "a0c", "rw_a0", 4)
        kkc = self.col(es, "kkc", "rw_k_k", 4)
        kac = self.col(es, "kac", "rw_k_a", 4)
        rkc = self.col(es, "rkc", "rw_r_k", 4)
        gnw = self.col(es, "gnw", "rw_gn_w", 4)
        gnb = self.col(es, "gnb", "rw_gn_b", 4)
        w2 = sb("w2", [64, 512]); P.dma(w2[:, :], self.inp["rw_w2"], writes=[w2])
        a2 = sb("a2", [128, 512]); P.dma(a2[64:128, :], self.inp["rw_a2"], writes=[a2])
        g2 = sb("g2", [128, 512]); P.dma(g2[:, :], self.inp["rw_g2"], writes=[g2])
        masks = sb("masks", [64, 3, 8, 64])
        P.dma(masks[:, :, :, :].rearrange("p a h t -> p (a h t)"), self.inp["c_masks"], writes=[masks])
        zin = sb("zin", [128, 14, 513])
        zs = sb("zs", [128, 14, 512])
        big = [sb("big%d" % i, [128, 4, 512]) for i in range(8)]
        tw = sb("tw", [64, 512]); sg = sb("sgz", [128, 512])
        gC = sb("gC", [128, 4, 8])
        Hc = sb("Hc", [128, 4, 64])
        Ht = sb("Ht", [128, 4, 64])
        ygo = sb("ygo", [128, 4, 512], BF16)
        c64 = lambda n: sb(n, [64, 8, 64])
        Np = [c64("Np0"), c64("Np1")]; Ntp = [c64("Ntp0"), c64("Ntp1")]
        AkT = c64("AkT"); ArbT = c64("ArbT"); ArkT = c64("ArkT")
        U = [c64("U0"), c64("U1")]
        Vtm = c64("Vtm"); Btm = sb("Btm", [64, 4, 128]); Ktm = sb("Ktm", [64, 4, 128])
        Ysb = c64("Ysb"); Ysq = c64("Ysq")
        st = sb("st", [64, 4, 8])
        eo = sb("eo", [128, 2])
        self.memset("dve", eo[:, :], 0.0, [eo])
        self.memset("dve", eo[0:64, 0:1], 1.0, [eo])
        self.memset("dve", eo[64:128, 1:2], 1.0, [eo])
        mk = {}
        for nm in ("b", "a", "k", "r"):
            mk[nm] = [sb("mk_%s%d" % (nm, i), [128, 4, 64]) for i in range(2)]
        self.pspool = [P.ps(es, "pb%d" % i, [128, 512], F32) for i in range(8)]
        self.psi = 0
        self.memset("dve", Hc[:, :, :], 0.0, [Hc])
        self.prep_alloc(es)
        prep_per_chunk = -(-128 // (S // 64))
        for ti in range(S // 512):
            t0 = ti * 512
            P.dma(zin[:, :, :], self.zT.t[0:1792, t0:t0 + 513].rearrange("(g p) t -> p g t", p=128),
                  reads=[self.zT], writes=[zin])
            self.tt("dve", zs[:, :, :], zin[:, :, 0:512], zin[:, :, 1:513], ALU.subtract, [zin], [zs])
            self.tt("pool", zs[:, :, :], zs[:, :, :], bc(mu[:, :].unsqueeze(2), [128, 14, 512]), ALU.mult,
                    [zs, mu], [zs])
            self.tt("dve", zs[:, :, :], zs[:, :, :], zin[:, :, 1:513], ALU.add, [zs, zin], [zs])
            r_, k_, v_ = zs[:, 0:4, :], zs[:, 4:8, :], zs[:, 8:12, :]
            A, B, C, Dd, E, Fb, G, H = big
            self.act(tw[:, :], zs[0:64, 12, :], AF.Tanh, [zs], [tw])
            for g in range(4):
                ps = self.next_ps()
                self.mm(ps[:, :], w2[:, g * 128:(g + 1) * 128], tw[:, :], True, True, [w2, tw], [ps])
                self.act(E[:, g, :], ps[:, :], AF.Sigmoid, [ps, w0], [E], bias=w0[:, g:g + 1])
            self.ts("pool", E[:, :, :], E[:, :, :], -math.exp(-0.5), ALU.mult, [E], [E])
            src = E
            pp = [A, B]
            k = 0
            for sh in (1, 2, 4, 8, 16, 32):
                dst = pp[k % 2]
                sv = src[:, :, :].rearrange("p g (c t) -> p (g c) t", t=64)
                dv = dst[:, :, :].rearrange("p g (c t) -> p (g c) t", t=64)
                self.tt("dve", dv[:, :, sh:64], sv[:, :, sh:64], sv[:, :, 0:64 - sh], ALU.add, [src], [dst])
                self.copy("pool", dv[:, :, 0:sh], sv[:, :, 0:sh], [src], [dst])
                src = dst
                k += 1
            X = src
            Y = A if X is B else B
            self.act(C[:, :, :], X[:, :, :], AF.Exp, [X], [C])
            self.act(Dd[:, :, :], X[:, :, :], AF.Exp, [X], [Dd], scale=-1.0)
            self.tt("dve", Y[:, :, :], X[:, :, :], E[:, :, :], ALU.subtract, [X, E], [Y])
            self.act(Y[:, :, :], Y[:, :, :], AF.Exp, [Y], [Y])
            self.copy("pool", gC[:, :, :], C[:, :, :].rearrange("p g (c t) -> p g c t", t=64)[:, :, :, 63],
                      [C], [gC])
            for g in range(4):
                self.ts("dve", Fb[:, g, :], k_[:, g, :], kkc[:, g:g + 1], ALU.mult, [zs, kkc], [Fb])
            self.tt("pool", G[:, :, :], Fb[:, :, :], Fb[:, :, :], ALU.mult, [Fb], [G])
            for g in range(4):
                ps = self.next_ps()
                self.mm(ps[:, :], self.blockones[:, :], G[:, g, :], True, True, [self.blockones, G], [ps])
                self.ts("dve", E[:, g, :], ps[:, :], 1e-24, ALU.add, [ps], [E])
            self.act(E[:, :, :], E[:, :, :], AF.Sqrt, [E], [E])
            P.op("dve", lambda e: e.reciprocal(out=E[:, :, :], in_=E[:, :, :]), [E], [E])
            self.tt("dve", Fb[:, :, :], Fb[:, :, :], E[:, :, :], ALU.mult, [Fb, E], [Fb])
            for g in range(4):
                ps = self.next_ps()
                self.mm(ps[:, :], a2[64:128, g * 128:(g + 1) * 128], zs[64:128, 12, :], True, True, [a2, zs], [ps])
                self.act(G[:, g, :], ps[:, :], AF.Sigmoid, [ps, a0], [G], bias=a0[:, g:g + 1])
            self.stt("dve", Y[:, :, :], Fb[:, :, :], -1.0, Y[:, :, :], ALU.mult, ALU.mult, [Fb, Y], [Y])
            self.tt("pool", H[:, :, :], Fb[:, :, :], G[:, :, :], ALU.mult, [Fb, G], [H])
            self.tt("dve", H[:, :, :], H[:, :, :], Dd[:, :, :], ALU.mult, [H, Dd], [H])
            for g in range(4):
                self.ts("dve", G[:, g, :], G[:, g, :], -1.0, ALU.add, [G, kac], [G], s2=kac[:, g:g + 1], op1=ALU.mult)
            self.stt("dve", G[:, :, :], G[:, :, :], 1.0, k_, ALU.add, ALU.mult, [G, zs], [G])
            self.tt("pool", Fb[:, :, :], r_, G[:, :, :], ALU.mult, [zs, G], [Fb])
            for g in range(4):
                self.ts("dve", Fb[:, g, :], Fb[:, g, :], rkc[:, g:g + 1], ALU.mult, [Fb, rkc], [Fb])
            for g in range(4):
                ps = self.next_ps()
                self.mm(ps[:, :], self.blockones[:, :], Fb[:, g, :], True, True, [self.blockones, Fb], [ps])
                self.tt("dve", E[:, g, :], ps[:, :], v_[:, g, :], ALU.mult, [ps, zs], [E])
            bonus = E
            self.tt("dve", Dd[:, :, :], Dd[:, :, :], G[:, :, :], ALU.mult, [Dd, G], [Dd])
            self.tt("pool", C[:, :, :], C[:, :, :], r_, ALU.mult, [C, zs], [C])
            self.act(sg[:, :], zs[:, 13, :], AF.Sigmoid, [zs], [sg])
            for g in range(4):
                ps = self.next_ps()
                self.mm(ps[:, :], g2[:, g * 128:(g + 1) * 128], sg[:, :], True, True, [g2, sg], [ps])
                self.copy("act", G[:, g, :], ps[:, :], [ps], [G])
            rt, at, bt, kt, ynT = C, Y, H, Dd, X
            for ch in range(8 if getattr(self, "stopb", 9) > 2 else 0):
                cs = slice(ch * 64, (ch + 1) * 64)
                for _ in range(prep_per_chunk):
                    if self.prep_done < 128:
                        self.prep_step()
                for (srcT, dstT, rd) in ((bt, Btm, [bt]), (kt, Ktm, [kt]), (v_, Vtm, [zs])):
                    ps = self.next_ps()
                    for g in range(4):
                        self.tr(ps[0:64, g * 128:(g + 1) * 128], srcT[:, g, cs], self.ident[:, :],
                                rd + [self.ident], [ps])
                    dv = dstT[:, :, :].rearrange("p a b -> p (a b)")
                    self.copy(self.ev_eng(), dv, ps[0:64, :], [ps], [dstT])
                for nm, srcT in (("b", bt), ("a", at), ("k", kt), ("r", rt)):
                    for e2 in range(2):
                        self.ts(["dve", "pool"][e2], mk[nm][e2][:, :, :], srcT[:, :, cs], eo[:, e2:e2 + 1], ALU.mult,
                                [srcT, eo], [mk[nm][e2]])

                def amat(dst, lT, lrd, rT, rrd, mi):
                    ps = self.next_ps()
                    for h in range(8):
                        self.mm(ps[0:64, h * 64:(h + 1) * 64], mk[lT][h % 2][:, h // 2, :], rT[:, h // 2, cs],
                                True, True, [mk[lT][h % 2], rrd], [ps])
                    self.tt("dve", dst[:, :, :], ps[0:64, :].rearrange("p (h t) -> p h t", h=8),
                            masks[:, mi, :, :], ALU.mult, [ps, masks], [dst])
                if getattr(self, "stopb", 9) == 3:
                    continue
                amat(Np[0], "b", bt, at, at, 0)
                amat(Ntp[0], "a", at, bt, bt, 2)
                amat(AkT, "k", kt, at, at, 0)
                amat(ArbT, "b", bt, rt, rt, 1)
                amat(ArkT, "k", kt, rt, rt, 1)
                if getattr(self, "stopb", 9) == 4:
                    continue
                ps = self.next_ps()
                for h in range(8):
                    o = ps[0:64, h * 64:(h + 1) * 64]
                    self.mm(o, mk["a"][h % 2][:, h // 2, :], Hc[:, h // 2, :], True, False, [mk["a"][h % 2], Hc], [ps])
                    self.mm(o, AkT[:, h, :], Vtm[:, h, :], False, True, [AkT, Vtm], [ps])
                self.copy("act", U[0][:, :, :], ps[0:64, :].rearrange("p (h t) -> p h t", h=8), [ps], [U[0]])
                cur = 0
                for lvl in range(6):
                    Nn, Nt = Np[lvl % 2], Ntp[lvl % 2]
                    ps = self.next_ps()
                    for h in range(8):
                        self.mm(ps[0:64, h * 64:(h + 1) * 64], Nn[:, h, :], U[cur][:, h, :], True, True,
                                [Nn, U[cur]], [ps])
                    self.tt("dve", U[1 - cur][:, :, :], U[cur][:, :, :],
                            ps[0:64, :].rearrange("p (h t) -> p h t", h=8), ALU.add, [ps, U[cur]], [U[1 - cur]])
                    cur = 1 - cur
                    if lvl < 5:
                        N2, Nt2 = Np[(lvl + 1) % 2], Ntp[(lvl + 1) % 2]
                        ps = self.next_ps()
                        for h in range(8):
                            self.mm(ps[0:64, h * 64:(h + 1) * 64], Nt[:, h, :], Nn[:, h, :], True, True,
                                    [Nt, Nn], [ps])
                        ps2 = None
                        if lvl < 4:
                            ps2 = self.next_ps()
                            for h in range(8):
                                self.mm(ps2[0:64, h * 64:(h + 1) * 64], Nn[:, h, :], Nt[:, h, :], True, True,
                                        [Nt, Nn], [ps2])
                        self.copy("act", N2[:, :, :], ps[0:64, :].rearrange("p (h t) -> p h t", h=8), [ps], [N2])
                        if ps2 is not None:
                            self.copy("pool" if False else "dve", Nt2[:, :, :],
                                      ps2[0:64, :].rearrange("p (h t) -> p h t", h=8), [ps2], [Nt2])
                Uf = U[cur]
                if getattr(self, "stopb", 9) == 5:
                    continue
                psy = self.next_ps()
                for h in range(8):
                    o = psy[0:64, h * 64:(h + 1) * 64]
                    self.mm(o, mk["r"][h % 2][:, h // 2, :], Hc[:, h // 2, :], True, False, [mk["r"][h % 2], Hc], [psy])
                    self.mm(o, ArbT[:, h, :], Uf[:, h, :], False, False, [ArbT, Uf], [psy])
                    self.mm(o, ArkT[:, h, :], Vtm[:, h, :], False, True, [ArkT, Vtm], [psy])
                psh = self.next_ps()
                for h in range(8):
                    o = psh[:, h * 64:(h + 1) * 64]
                    self.mm(o, Btm[:, h // 2, :], Uf[:, h, :], True, False, [Btm, Uf], [psh])
                    self.mm(o, Ktm[:, h // 2, :], Vtm[:, h, :], False, True, [Ktm, Vtm], [psh])
                phv = psh[:, :].rearrange("p (g e v) -> p g e v", g=4, e=2)
                for e2 in range(2):
                    rs = slice(e2 * 64, e2 * 64 + 64)
                    self.tt("dve", Ht[rs, :, :], Hc[rs, :, :], phv[rs, :, e2, :], ALU.add, [Hc, psh], [Ht])
                    self.tt("dve", Hc[rs, :, :], Ht[rs, :, :], bc(gC[rs, :, ch:ch + 1], [64, 4, 64]), ALU.mult,
                            [Ht, gC], [Hc])
                if getattr(self, "stopb", 9) == 6:
                    continue
                yv = psy[0:64, :].rearrange("p (h t) -> p h t", h=8)
                self.copy("act", Ysb[:, :, :], yv, [psy], [Ysb])
                self.act(Ysq[:, :, :], yv, AF.Square, [psy], [Ysq])
                P.op("dve", lambda e: e.reduce_sum(out=st[:, 0, :], in_=Ysb[:, :, :], axis=AX.X), [Ysb], [st])
                P.op("dve", lambda e: e.reduce_sum(out=st[:, 1, :], in_=Ysq[:, :, :], axis=AX.X), [Ysq], [st])
                self.ts("dve", st[:, 0, :], st[:, 0, :], 1.0 / 64, ALU.mult, [st], [st])
                self.tt("dve", st[:, 2, :], st[:, 0, :], st[:, 0, :], ALU.mult, [st], [st])
                self.stt("dve", st[:, 1, :], st[:, 1, :], 1.0 / 64, st[:, 2, :], ALU.mult, ALU.subtract, [st], [st])
                self.ts("dve", st[:, 1, :], st[:, 1, :], GN_EPS, ALU.add, [st], [st])
                self.act(st[:, 1, :], st[:, 1, :], AF.Sqrt, [st], [st])
                P.op("dve", lambda e: e.reciprocal(out=st[:, 1, :], in_=st[:, 1, :]), [st], [st])
                self.tt("dve", Ysb[:, :, :], Ysb[:, :, :], bc(st[:, 0, :].unsqueeze(2), [64, 8, 64]), ALU.subtract,
                        [Ysb, st], [Ysb])
                self.tt("dve", Ysb[:, :, :], Ysb[:, :, :], bc(st[:, 1, :].unsqueeze(2), [64, 8, 64]), ALU.mult,
                        [Ysb, st], [Ysb])
                ps = self.next_ps()
                yf = Ysb[:, :, :].rearrange("p h v -> p (h v)")
                for g in range(4):
                    self.tr(ps[:, g * 64:(g + 1) * 64], yf[:, g * 128:(g + 1) * 128], self.ident[0:64, 0:64],
                            [Ysb, self.ident], [ps])
                for g in range(4):
                    self.act(ynT[:, g, cs], ps[:, g * 64:(g + 1) * 64], AF.Identity, [ps, gnw, gnb], [ynT],
                             bias=gnb[:, g:g + 1], scale=gnw[:, g:g + 1])
            self.tt("dve", ynT[:, :, :], ynT[:, :, :], bonus[:, :, :], ALU.add, [ynT, bonus], [ynT])
            self.tt("dve", ygo[:, :, :], ynT[:, :, :], G[:, :, :], ALU.mult, [ynT, G], [ygo])
            P.dma(self.ygT.t[:, t0:t0 + 512].rearrange("(g p) t -> p g t", p=128), ygo[:, :, :],
                  reads=[ygo], writes=[self.ygT])
            P.barrier()
    P.barrier()


K.phase_b = _build_phase_b


def _loadw(self, es, name, src_ap, rows, cols, stg):
    P = self.P
    nk = rows // 128
    t = name if isinstance(name, TT) else P.sb(es, name, [128, nk, cols], BF16)
    for kc in range(nk):
        st = stg[kc % 2]
        P.dma(st[:, 0:cols], src_ap[kc * 128:(kc + 1) * 128, :], writes=[st])
        self.copy(["act", "dve", "pool"][kc % 3], t[:, kc, :], st[:, 0:cols], [st], [t])
    return t


K.loadw = _loadw

MAGIC = 12582912.0
TWO_PI_1 = 6.28125
TWO_PI_2 = 2.0 * math.pi - 6.28125


def _build_phase_c(self):
    P, nc, S = self.P, self.nc, self.S
    NT = S // 512
    with ExitStack() as es:
        sb = lambda n, sh, dt=F32: P.sb(es, n, sh, dt)
        stg = [sb("cstg%d" % i, [128, 1024]) for i in range(2)]
        wuq = self.loadw(es, "wuq", self.inp["mla_w_uq"], 384, 768, stg)
        wkv = self.loadw(es, "wkv", self.inp["mla_w_ukv"], 256, 1024, stg)
        wrot = sb("wrot", [128, 3, 8, 96], BF16)
        self.memset("pool", wrot[:, :, :, :], 0.0, [wrot])
        wq4 = wuq[:, :, :].rearrange("p c (h e) -> p c h e", h=8)
        self.copy("dve", wrot[:, :, :, 64:80], wq4[:, :, :, 80:96], [wuq], [wrot])
        self.copy("dve", wrot[:, :, :, 80:96], wq4[:, :, :, 64:80], [wuq], [wrot])
        wk4 = wkv[:, :, :].rearrange("p c (h e) -> p c h e", h=8)
        wv = sb("wv", [128, 2, 8, 64], BF16)
        self.copy("dve", wv[:, :, :, :], wk4[:, :, :, 64:128], [wkv], [wv])
        qn = self.col(es, "qn", "mla_q_norm", 3)
        kvn = self.col(es, "kvn", "mla_kv_norm", 2)
        ropec = sb("ropec", [128, 4]); P.dma(ropec[:, :], self.inp["c_ropec"], writes=[ropec])
        onesb = sb("onesb", [128, 128], BF16)
        self.copy("dve", onesb[:, :], self.ones[:, :], [self.ones], [onesb])
        Kf = [sb("Kf%d" % h, [96, S], BF16) for h in range(8)]
        Vtm = sb("Vtm_a", [128, S // 128, 512], BF16)
        Qf = sb("Qf", [96, 8, 512], BF16)
        zc = sb("zc", [128, 3, 512]); zsq = sb("zsq", [128, 3, 512]); rstd = sb("rstd_c", [128, 512])
        cn = sb("cn", [128, 3, 512], BF16)
        posi = sb("posi", [96, 512], I32); ang = sb("ang", [96, 512]); kq = sb("kq", [96, 512])
        Ct = sb("Ct", [96, 512]); St = sb("St", [96, 512])
        kr = sb("kr", [96, 512]); krot = sb("krot", [96, 512]); krb = sb("krb", [96, 512], BF16)
        qa = sb("qa", [96, 512]); qb = sb("qb", [96, 512])
        Pt = [sb("Pt%d" % i, [128, 512], BF16) for i in range(3)]
        rec = sb("rec", [128, 512])
        oT = sb("oT", [128, 4, 512], BF16)
        self.pspool = [P.ps(es, "pc%d" % i, [128, 512], F32) for i in range(4)]
        self.psi = 0
        psO = [P.ps(es, "pO%d" % i, [128, 512], F32) for i in range(2)]
        psS = [P.ps(es, "pS%d" % i, [128, 512], F32) for i in range(2)]
        scale = 1.0 / math.sqrt(96.0)

        def rmsn(groups, ng, gcolt, nfeat):
            self.tt("pool", zsq[:, 0:ng, :], zc[:, 0:ng, :], zc[:, 0:ng, :], ALU.mult, [zc], [zsq])
            ps = self.next_ps()
            for c in range(ng):
                self.mm(ps[:, :], self.ones[:, :], zsq[:, c, :], c == 0, c == ng - 1, [self.ones, zsq], [ps])
            self.rsqrt(rstd[:, :], ps[:, :], 1.0 / nfeat, EPS, [ps], [rstd])
            for c in range(ng):
                self.stt("dve", cn[:, c, :], zc[:, c, :], gcolt[:, c:c + 1], rstd[:, :], ALU.mult, ALU.mult,
                         [zc, gcolt, rstd], [cn])

        for T in range(NT):
            t0 = T * 512
            P.dma(posi[:, :], self.inp["pos"][0:1, t0:t0 + 512].partition_broadcast(96), writes=[posi])
            self.copy("dve", ang[:, :], posi[:, :], [posi], [ang])
            self.ts("dve", ang[:, :], ang[:, :], ropec[0:96, 0:1], ALU.mult, [ang, ropec], [ang])
            self.ts("dve", kq[:, :], ang[:, :], 1.0 / (2 * math.pi), ALU.mult, [ang], [kq], s2=MAGIC, op1=ALU.add)
            self.ts("dve", kq[:, :], kq[:, :], -MAGIC, ALU.add, [kq], [kq])
            self.stt("dve", ang[:, :], kq[:, :], -TWO_PI_1, ang[:, :], ALU.mult, ALU.add, [kq, ang], [ang])
            self.stt("dve", ang[:, :], kq[:, :], -TWO_PI_2, ang[:, :], ALU.mult, ALU.add, [kq, ang], [ang])
            self.ts("dve", ang[:, :], ang[:, :], 3.14159, ALU.min, [ang], [ang], s2=-3.14159, op1=ALU.max)
            self.act(St[:, :], ang[:, :], AF.Sin, [ang], [St])
            self.ts("dve", St[:, :], St[:, :], ropec[0:96, 1:2], ALU.mult, [St, ropec], [St])
            self.act(kq[:, :], ang[:, :], AF.Abs, [ang], [kq])
            self.act(Ct[:, :], kq[:, :], AF.Sin, [kq, ropec], [Ct], bias=ropec[0:96, 2:3], scale=-1.0)
            P.dma(zc[:, 0:2, :], self.zT.t[17 * 128:19 * 128, 1 + t0:1 + t0 + 512].rearrange("(g p) t -> p g t", p=128),
                  reads=[self.zT], writes=[zc])
            rmsn(2, 2, kvn, 256.0)
            for h in range(8):
                ps = self.next_ps()
                for c in range(2):
                    self.mm(ps[0:64, :], wk4[:, c, h, 0:64], cn[:, c, :], c == 0, c == 1, [wkv, cn], [ps])
                self.copy(self.ev_eng(), Kf[h][0:64, t0:t0 + 512], ps[0:64, :], [ps], [Kf[h]])
            rz = 19 * 128
            P.dma(kr[64:96, :], self.zT.t[rz:rz + 32, 1 + t0:1 + t0 + 512], reads=[self.zT], writes=[kr])
            P.dma(krot[64:80, :], self.zT.t[rz + 16:rz + 32, 1 + t0:1 + t0 + 512], reads=[self.zT], writes=[krot])
            P.dma(krot[80:96, :], self.zT.t[rz:rz + 16, 1 + t0:1 + t0 + 512], reads=[self.zT], writes=[krot])
            self.tt("dve", kr[64:96, :], kr[64:96, :], Ct[64:96, :], ALU.mult, [kr, Ct], [kr])
            self.tt("dve", krot[64:96, :], krot[64:96, :], St[64:96, :], ALU.mult, [krot, St], [krot])
            self.tt("dve", krb[64:96, :], kr[64:96, :], krot[64:96, :], ALU.add, [kr, krot], [krb])
            for h in range(8):
                self.copy(["dve", "pool"][h % 2], Kf[h][64:96, t0:t0 + 512], krb[64:96, :], [krb], [Kf[h]])
            for b4 in range(4):
                ps = self.next_ps()
                for c in range(2):
                    self.mm(ps[:, :], cn[:, c, b4 * 128:(b4 + 1) * 128], wv[:, c, :, :].rearrange("p h e -> p (h e)"),
                            c == 0, c == 1, [cn, wv], [ps])
                self.copy(self.ev_eng(), Vtm[:, T * 4 + b4, :], ps[:, :], [ps], [Vtm])
            P.dma(zc[:, 0:3, :], self.zT.t[14 * 128:17 * 128, 1 + t0:1 + t0 + 512].rearrange("(g p) t -> p g t", p=128),
                  reads=[self.zT], writes=[zc])
            rmsn(3, 3, qn, 384.0)
            for h in range(8):
                ps = self.next_ps()
                ps2 = self.next_ps()
                for c in range(3):
                    self.mm(ps[0:96, :], wuq[:, c, h * 96:(h + 1) * 96], cn[:, c, :], c == 0, c == 2, [wuq, cn], [ps])
                for c in range(3):
                    self.mm(ps2[0:96, :], wrot[:, c, h, :], cn[:, c, :], c == 0, c == 2, [wrot, cn], [ps2])
                self.tt("dve", qa[:, :], ps[0:96, :], Ct[:, :], ALU.mult, [ps, Ct], [qa])
                self.tt("dve", qb[:, :], ps2[0:96, :], St[:, :], ALU.mult, [ps2, St], [qb])
                self.tt("pool", Qf[:, h, :], qa[:, :], qb[:, :], ALU.add, [qa, qb], [Qf])
            pi = 0
            for h in range(8):
                pO, pS = psO[h % 2], psS[h % 2]
                nkb = 4 * T + 4
                for kb in range(nkb):
                    nq0 = max(0, kb - 4 * T)
                    cl = slice(nq0 * 128, 512)
                    ps = self.next_ps()
                    self.mm(ps[:, cl], Kf[h][0:96, kb * 128:(kb + 1) * 128], Qf[0:96, h, cl], True, True,
                            [Kf[h], Qf], [ps])
                    pt = Pt[pi % 3]
                    pi += 1
                    self.act(pt[:, cl], ps[:, cl], AF.Exp, [ps], [pt], scale=scale)
                    if kb >= 4 * T:
                        self.memset("pool", pt[64:128, nq0 * 128:nq0 * 128 + 64], 0.0, [pt])
                    hp2 = (h // 2) * 128
                    self.mm(pO[:, cl], Vtm[:, kb, hp2:hp2 + 128], pt[:, cl], kb == 0, kb == nkb - 1, [Vtm, pt], [pO])
                    self.mm(pS[:, cl], onesb[:, :], pt[:, cl], kb == 0, kb == nkb - 1, [onesb, pt], [pS])
                P.op("dve", lambda e, o=rec[:, :], i=pS[:, :]: e.reciprocal(out=o, in_=i), [pS], [rec])
                rs = slice((h % 2) * 64, (h % 2) * 64 + 64)
                self.tt("dve", oT[rs, h // 2, :], pO[rs, :], rec[rs, :], ALU.mult, [pO, rec], [oT])
            P.dma(self.oTs.t[:, t0:t0 + 512].rearrange("(g p) t -> p g t", p=128), oT[:, :, :],
                  reads=[oT], writes=[self.oTs])
            P.barrier()
    P.barrier()
    with ExitStack() as es:
        sb = lambda n, sh, dt=F32: P.sb(es, n, sh, dt)
        wo_m = P.sb(es, "wo_m", [128, 4, 1024], BF16)
        wo_r = P.sb(es, "wo_r", [128, 4, 1024], BF16)
        w_o = P.sb(es, "w_o", [128, 8, 1024], BF16)
        stg = [sb("c2stg%d" % i, [128, 1024]) for i in range(2)]
        self.loadw(es, wo_m, self.inp["mla_w_o"], 512, 1024, stg)
        self.loadw(es, wo_r, self.inp["rw_w_o"], 512, 1024, stg)
        self.loadw(es, w_o, self.inp["w_out"], 1024, 1024, stg)
        oT = sb("oT2", [128, 4, 512], BF16)
        ygl = sb("ygl", [128, 4, 512], BF16)
        gA = [sb("gA%d" % i, [128, 512]) for i in range(2)]
        gB = [sb("gB%d" % i, [128, 512]) for i in range(2)]
        ta = sb("ta", [128, 512]); tb = sb("tb", [128, 512])
        mix = sb("mix", [128, 8, 512], BF16)
        xl = [sb("xl%d" % i, [128, 1024]) for i in range(2)]
        self.pspool = [P.ps(es, "pc2_%d" % i, [128, 512], F32) for i in range(6)]
        self.psi = 0
        for T in range(NT):
            t0 = T * 512
            P.dma(oT[:, :, :], self.oTs.t[:, t0:t0 + 512].rearrange("(g p) t -> p g t", p=128),
                  reads=[self.oTs], writes=[oT])
            P.dma(ygl[:, :, :], self.ygT.t[:, t0:t0 + 512].rearrange("(g p) t -> p g t", p=128),
                  reads=[self.ygT], writes=[ygl])
            for j in range(8):
                ga, gb = gA[j % 2], gB[j % 2]
                P.dma(ga[:, :], self.zT.t[(20 + j) * 128:(21 + j) * 128, 1 + t0:1 + t0 + 512], reads=[self.zT], writes=[ga])
                P.dma(gb[:, :], self.zT.t[(28 + j) * 128:(29 + j) * 128, 1 + t0:1 + t0 + 512], reads=[self.zT], writes=[gb])
                ps = self.next_ps()
                ps2 = self.next_ps()
                for c in range(4):
                    self.mm(ps[:, :], wo_r[:, c, j * 128:(j + 1) * 128], ygl[:, c, :], c == 0, c == 3, [wo_r, ygl], [ps])
                for c in range(4):
                    self.mm(ps2[:, :], wo_m[:, c, j * 128:(j + 1) * 128], oT[:, c, :], c == 0, c == 3, [wo_m, oT], [ps2])
                self.tt("dve", ta[:, :], ps[:, :], ga[:, :], ALU.mult, [ps, ga], [ta])
                self.tt("dve", tb[:, :], ps2[:, :], gb[:, :], ALU.mult, [ps2, gb], [tb])
                self.tt("pool", mix[:, j, :], ta[:, :], tb[:, :], ALU.add, [ta, tb], [mix])
            for b4 in range(4):
                x_l = xl[b4 % 2]
                r0 = t0 + b4 * 128
                P.dma(x_l[:, :], self.inp["x"][r0:r0 + 128, :], writes=[x_l])
                for hf in range(2):
                    ps = self.next_ps()
                    for j in range(8):
                        self.mm(ps[:, :], mix[:, j, b4 * 128:(b4 + 1) * 128], w_o[:, j, hf * 512:(hf + 1) * 512],
                                j == 0, j == 7, [mix, w_o], [ps])
                    self.tt("dve", x_l[:, hf * 512:(hf + 1) * 512], x_l[:, hf * 512:(hf + 1) * 512], ps[:, :], ALU.add,
                            [x_l, ps], [x_l])
                P.dma(self.x1.t[r0:r0 + 128, :], x_l[:, :], reads=[x_l], writes=[self.x1])
    P.barrier()


K.phase_c = _build_phase_c


def _build_phase_d(self):
    P, nc, S = self.P, self.nc, self.S
    NEG = -1e30
    with ExitStack() as es:
        sb = lambda n, sh, dt=F32: P.sb(es, n, sh, dt)
        stg = [sb("dstg%d" % i, [128, 1024]) for i in range(2)]
        self.pspool = [P.ps(es, "pdp%d" % i, [128, 512], F32) for i in range(4)]
        self.psi = 0
        if self.prep_done < 128:
            self.prep_alloc(es)
            while self.prep_done < 128:
                self.prep_step()
    P.barrier()
    with ExitStack() as es:
        sb = lambda n, sh, dt=F32: P.sb(es, n, sh, dt)
        self.pspool = [P.ps(es, "pd%d" % i, [128, 512], F32) for i in range(3)]
        self.psi = 0
        psOut = [[P.ps(es, "po%d%d" % (a, b), [128, 512], F32) for b in range(2)] for a in range(2)]
        uni = P.sb(es, "uni", [128, 8192], F32)
        wq = TT(uni[:, :].bitcast(BF16).rearrange("p (k c) -> p k c", k=8), P.buf("wqv"))
        wpg = P.sb(es, "wpg", [128, 8, 1024], BF16)
        wpp = P.sb(es, "wpp", [128, 2, 1024], BF16)
        skT = sb("skT", [128, 16, 128], BF16)
        with ExitStack() as es3:
            stg = [P.sb(es3, "dstg2_%d" % i, [128, 2048], F32) for i in range(2)]
            self.loadw(es, wq, self.inp["peer_w_q"], 1024, 2048, stg)
            P.dma(self.wqs.t[:, :].rearrange("(k p) c -> p k c", p=128), wq[:, :, :], reads=[wq], writes=[self.wqs])
            self.loadw(es, wpg, self.inp["ple_w_gate"], 1024, 1024, stg)
            self.loadw(es, wpp, self.inp["ple_w_proj"], 256, 1024, stg)
            self._skt(stg, skT)
            P.barrier()
        for hc in range(0):
            st = stg[hc % 2]
            P.dma(st[:, 0:128], self.inp["peer_sub_keys"][hc], writes=[st])
            ps = self.next_ps()
            self.tr(ps[:, 0:128], st[:, 0:128], self.ident[:, :], [st, self.ident], [ps])
            self.copy("dve", skT[:, hc, :], ps[:, 0:128], [ps], [skT])
        gf = self.col(es, "gf", "norm_ffn", 8)
        gp = self.col(es, "gp", "norm_ple", 8)
        gfin = sb("gfin", [128, 1024])
        P.dma(gfin[:, :], self.inp["norm_final"].rearrange("(o d) -> o d", o=1).partition_broadcast(128), writes=[gfin])
        x1t = [sb("x1t%d" % i, [128, 1024]) for i in range(2)]
        hb = sb("hb_d", [128, 1024], BF16)
        junk = hb
        ssd = sb("ssd", [128, 1])
        xnT = sb("xnT", [128, 8, 256], BF16)
        xn2T = sb("xn2T", [128, 8, 128], BF16)
        qT = sb("qT", [128, 16, 256], BF16)
        sS = [sb("sS%d" % i, [128, 16, 128]) for i in range(2)]
        s1pp = [sb("s1pp%d" % i, [128, 8, 128]) for i in range(2)]
        bE = [sb("bE%d" % i, [128, 8]) for i in range(2)]
        a16 = sb("a16", [128, 16]); b16 = sb("b16", [128, 16]); c16 = sb("c16", [128, 16]); e16 = sb("e16", [128, 16])
        tmpk = sb("tmpk", [128, 128]); cand = sb("cand", [128, 256]); cand2 = sb("cand2", [128, 256])
        sc = sb("scal", [128, 8])
        ubk = [sb("ubk%d" % i, [128, 8, 512], BF16) for i in range(2)]
        vbk = [sb("vbk%d" % i, [128, 4, 1024], BF16) for i in range(2)]
        dl = [TT(uni[:, i * 2048:(i + 1) * 2048].rearrange("p (a b c) -> p a b c", a=4, b=4), P.buf("dl%d" % i))
              for i in range(4)]
        Ee = [sb("Ee%d" % i, [128, 4, 4, 128], BF16) for i in range(4)]
        Gh = [sb("Gh%d" % i, [128, 4, 4, 128], BF16) for i in range(4)]
        dg = [sb("dg%d" % i, [128, 8, 128], BF16) for i in range(2)]
        wE = sb("wE", [128, 8])
        gel = [sb("gel%d" % i, [128, 512], BF16) for i in range(3)]
        Pm = [sb("Pm%d" % i, [128, 512], BF16) for i in range(3)]
        PTs = [sb("PTs%d" % i, [128, 4, 128], BF16) for i in range(3)]
        pst = P.ps(es, "pstd", [128, 8, 128], BF16)
        pl = sb("pl", [128, 256]); plb = sb("plb", [128, 256], BF16); pT = sb("pT", [128, 2, 128], BF16)
        gt = sb("gt", [128, 512])
        ib = self.identb

        def norm_T(xsrc, gcolt, dstT, csl):
            self.act(junk[:, :], xsrc[:, :], AF.Square, [xsrc], [junk, ssd], accum=ssd[:, :])
            self.rsqrt(ssd[:, :], ssd[:, :], 1.0 / D, EPS, [ssd], [ssd])
            self.ts("dve", hb[:, :], xsrc[:, :], ssd[:, 0:1], ALU.mult, [xsrc, ssd], [hb])
            for kc in range(8):
                self.tr(pst[:, kc, :], hb[:, kc * 128:(kc + 1) * 128], ib[:, :], [hb, ib], [pst])
            self.tt("dve", dstT[:, :, csl], pst[:, :, :], bc(gcolt[:, :].unsqueeze(2), [128, 8, 128]), ALU.mult,
                    [pst, gcolt], [dstT])

        def top16(dst, src, srcT, tmp, tmpT):
            P.op("dve", lambda e, o=dst[:, 0:8], i=src: e.max(out=o, in_=i), [dst, srcT], [dst])
            P.op("dve", lambda e, o=tmp, r=dst[:, 0:8], i=src: e.match_replace(out=o, in_to_replace=r, in_values=i,
                                                                                imm_value=NEG), [dst, srcT], [tmpT])
            P.op("dve", lambda e, o=dst[:, 8:16], i=tmp: e.max(out=o, in_=i), [tmpT], [dst])

        gi = 0
        mk = 0
        for tl in range(S // 256):
            P.dma(wq[:, :, :], self.wqs.t[:, :].rearrange("(k p) c -> p k c", p=128), reads=[self.wqs], writes=[wq])
            for sub in range(2):
                r0 = tl * 256 + sub * 128
                P.dma(x1t[sub][:, :], self.x1.t[r0:r0 + 128, :], reads=[self.x1], writes=[x1t[sub]])
                norm_T(x1t[sub], gf, xnT, slice(sub * 128, (sub + 1) * 128))
            for hc in range(16):
                ps = self.next_ps()
                for kc in range(8):
                    self.mm(ps[:, 0:256], wq[:, kc, hc * 128:(hc + 1) * 128], xnT[:, kc, :], kc == 0, kc == 7,
                            [wq, xnT], [ps])
                self.copy(self.ev_eng(), qT[:, hc, :], ps[:, 0:256], [ps], [qT])
            for sub in range(2):
                for g4 in range(4):
                    ps = self.next_ps()
                    for i4 in range(4):
                        hc = g4 * 4 + i4
                        self.mm(ps[:, i4 * 128:(i4 + 1) * 128], qT[:, hc, sub * 128:(sub + 1) * 128], skT[:, hc, :],
                                True, True, [qT, skT], [ps])
                    self.copy(self.ev_eng(), sS[sub][:, g4 * 4:(g4 + 1) * 4, :],
                              ps[:, :].rearrange("p (a n) -> p a n", a=4), [ps], [sS[sub]])
                for h in range(8):
                    s1 = sS[sub][:, 2 * h, :]
                    s2 = sS[sub][:, 2 * h + 1, :]
                    top16(a16, s1, sS[sub], tmpk[:, :], tmpk)
                    top16(b16, s2, sS[sub], tmpk[:, :], tmpk)
                    self.tt("dve", cand[:, :].rearrange("p (a b) -> p a b", a=16),
                            bc(a16[:, :].unsqueeze(2), [128, 16, 16]), bc(b16[:, :].unsqueeze(1), [128, 16, 16]),
                            ALU.add, [a16, b16], [cand])
                    top16(c16, cand[:, :], cand, cand2[:, :], cand2)
                    P.op("dve", lambda e, o=sc[:, 0:1], i=c16[:, :]: e.tensor_reduce(out=o, in_=i, axis=AX.X, op=ALU.min),
                         [c16], [sc])
                    P.op("dve", lambda e, o=sc[:, 1:2], i=c16[:, :]: e.tensor_reduce(out=o, in_=i, axis=AX.X, op=ALU.max),
                         [c16], [sc])
                    self.ts("dve", sc[:, 0:1], sc[:, 0:1], -2e-6, ALU.add, [sc], [sc])
                    self.ts("dve", sc[:, 2:3], sc[:, 1:2], -1.0, ALU.mult, [sc], [sc])
                    self.act(e16[:, :], c16[:, :], AF.Exp, [c16, sc], [e16, sc], bias=sc[:, 2:3], accum=sc[:, 3:4])
                    self.act(sc[:, 4:5], sc[:, 3:4], AF.Ln, [sc], [sc])
                    self.tt("dve", sc[:, 5:6], sc[:, 0:1], sc[:, 1:2], ALU.subtract, [sc], [sc])
                    self.tt("dve", bE[sub][:, h:h + 1], sc[:, 5:6], sc[:, 4:5], ALU.subtract, [sc], [bE[sub]])
                    self.ts("dve", s1pp[sub][:, h, :], s1, sc[:, 0:1], ALU.subtract, [sS[sub], sc], [s1pp[sub]])
                self.act(wE[:, :], bE[sub][:, :], AF.Exp, [bE[sub]], [wE])
                for h in range(8):
                    self.ts(["dve", "pool"][h % 2], dg[sub][:, h, :], ib[:, :], wE[:, h:h + 1], ALU.mult, [ib, wE], [dg[sub]])
            NK = 64
            P.barrier()

            def S1(k):
                blk, sub = k // 2, k % 2
                ub = ubk[blk % 2]
                if sub == 0:
                    vb = vbk[blk % 2]
                    P.dma(ub[:, :, :], self.uTs.t[:, blk * 512:(blk + 1) * 512].rearrange("(k p) e -> p k e", p=128),
                          reads=[self.uTs], writes=[ub])
                    P.dma(vb[:, :, :], self.vs.t[blk * 512:(blk + 1) * 512, :].rearrange("(c p) d -> p c d", p=128),
                          reads=[self.vs], writes=[vb])
                psA = self.next_ps()
                for kc in range(8):
                    self.mm(psA[:, :], xnT[:, kc, sub * 128:(sub + 1) * 128], ub[:, kc, :], kc == 0, kc == 7,
                            [xnT, ub], [psA])
                for hg in range(2):
                    d_, e_, g_ = dl[(k % 2) * 2 + hg], Ee[(k % 2) * 2 + hg], Gh[(k % 2) * 2 + hg]
                    s2v = sS[sub][:, :, :].rearrange("p (h c) n -> p h c n", c=2)[:, hg * 4:(hg + 1) * 4, 1, :]
                    self.tt("dve", d_[:, :, :, :],
                            bc(s2v.unsqueeze(2), [128, 4, 4, 128]),
                            bc(s1pp[sub][:, hg * 4:(hg + 1) * 4, blk * 4:(blk + 1) * 4].unsqueeze(3), [128, 4, 4, 128]),
                            ALU.add, [sS[sub], s1pp[sub]], [d_])
                    self.act(e_[:, :, :, :], d_[:, :, :, :], AF.Exp, [d_], [e_])
                    self.stt("dve", g_[:, :, :, :], d_[:, :, :, :], 0.0, e_[:, :, :, :], ALU.is_ge, ALU.mult,
                             [d_, e_], [g_])
                self.psA_k[k % 2] = psA

            def S1b(k):
                psA = self.psA_k[k % 2]
                self.act(gel[k % 3][:, :], psA[:, :], AF.Gelu, [psA], [gel[k % 3]])

            def S2(k):
                sub = k % 2
                psM = self.next_ps()
                for hg in range(2):
                    g_ = Gh[(k % 2) * 2 + hg]
                    for h4 in range(4):
                        h = hg * 4 + h4
                        self.mm(psM[:, :], dg[sub][:, h, :], g_[:, h4, :, :].rearrange("p a b -> p (a b)"),
                                h == 0, h == 7, [dg[sub], g_], [psM])
                self.psM_k[k % 2] = psM

            def S2b(k):
                psM = self.psM_k[k % 2]
                self.tt("dve", Pm[k % 3][:, :], gel[k % 3][:, :], psM[:, :], ALU.mult, [gel[k % 3], psM], [Pm[k % 3]])

            def S3(k):
                pm = Pm[k % 3]
                for ec in range(4):
                    self.tr(pst[:, ec, :], pm[:, ec * 128:(ec + 1) * 128], ib[:, :], [pm, ib], [pst])
                self.copy("act", PTs[k % 3][:, :, :], pst[:, 0:4, :], [pst], [PTs[k % 3]])

            def S4(k):
                blk, sub = k // 2, k % 2
                vb = vbk[blk % 2]
                pts = PTs[k % 3]
                for hf in range(2):
                    for ec in range(4):
                        self.mm(psOut[sub][hf][:, :], pts[:, ec, :], vb[:, ec, hf * 512:(hf + 1) * 512],
                                blk == 0 and ec == 0, blk == 31 and ec == 3, [pts, vb], [psOut[sub][hf]])

            self.psA_k = [None, None]
            self.psM_k = [None, None]
            for r in range(NK + 3):
                if 0 <= r - 3 < NK:
                    S4(r - 3)
                if r < NK:
                    S1(r)
                if 0 <= r - 1 < NK:
                    S2(r - 1)
                if 0 <= r - 2 < NK:
                    S3(r - 2)
                if r < NK:
                    S1b(r)
                if 0 <= r - 1 < NK:
                    S2b(r - 1)
            for sub in range(2):
                r0 = tl * 256 + sub * 128
                xx = x1t[sub]
                for hf in range(2):
                    self.tt("dve", xx[:, hf * 512:(hf + 1) * 512], xx[:, hf * 512:(hf + 1) * 512], psOut[sub][hf][:, :],
                            ALU.add, [xx, psOut[sub][hf]], [xx])
                norm_T(xx, gp, xn2T, slice(0, 128))
                P.dma(pl[:, :], self.inp["p"][r0:r0 + 128, :], writes=[pl])
                self.copy("pool", plb[:, :], pl[:, :], [pl], [plb])
                for c in range(2):
                    self.tr(pst[:, c, :], plb[:, c * 128:(c + 1) * 128], ib[:, :], [plb, ib], [pst])
                self.copy("act", pT[:, :, :], pst[:, 0:2, :], [pst], [pT])
                for hf in range(2):
                    hs = slice(hf * 512, (hf + 1) * 512)
                    ps = self.next_ps()
                    for kc in range(8):
                        self.mm(ps[:, :], xn2T[:, kc, :], wpg[:, kc, hs], kc == 0, kc == 7, [xn2T, wpg], [ps])
                    self.act(gt[:, :], ps[:, :], AF.Sigmoid, [ps], [gt])
                    ps2 = self.next_ps()
                    for c in range(2):
                        self.mm(ps2[:, :], pT[:, c, :], wpp[:, c, hs], c == 0, c == 1, [pT, wpp], [ps2])
                    self.tt("dve", gt[:, :], gt[:, :], ps2[:, :], ALU.mult, [gt, ps2], [gt])
                    self.tt("dve", xx[:, hs], xx[:, hs], gt[:, :], ALU.add, [xx, gt], [xx])
                self.act(junk[:, :], xx[:, :], AF.Square, [xx], [junk, ssd], accum=ssd[:, :])
                self.rsqrt(ssd[:, :], ssd[:, :], 1.0 / D, EPS, [ssd], [ssd])
                self.stt("dve", xx[:, :], xx[:, :], ssd[:, 0:1], gfin[:, :], ALU.mult, ALU.mult, [xx, ssd, gfin], [xx])
                P.dma(self.out[r0:r0 + 128, :], xx[:, :], reads=[xx])
            P.barrier()
    P.barrier()


K.phase_d = _build_phase_d


def _prep_alloc(self, es):
    P = self.P
    self.pp_u32 = [P.sb(es, "ub32_%d" % i, [128, 1024], F32) for i in range(2)]
    self.pp_ucv = [P.sb(es, "ucv%d" % i, [128, 8, 128], BF16) for i in range(2)]
    self.pp_vcv = [P.sb(es, "vcv%d" % i, [128, 1024], BF16) for i in range(2)]
    self.pp_v32 = [P.sb(es, "pv32_%d" % i, [128, 1024], F32) for i in range(2)]


def _prep_step(self):
    P = self.P
    c = self.prep_done
    self.prep_done += 1
    u32 = self.pp_u32[c % 2]
    P.dma(u32[:, :], self.inp["peer_u"][c * 128:(c + 1) * 128, :], writes=[u32])
    uc = self.pp_ucv[c % 2]
    for half in range(2):
        ps = self.next_ps()
        for k4 in range(4):
            kc = half * 4 + k4
            self.tr(ps[:, k4 * 128:(k4 + 1) * 128], u32[:, kc * 128:(kc + 1) * 128], self.ident[:, :],
                    [u32, self.ident], [ps])
        self.copy(self.ev_eng(), uc[:, half * 4:(half + 1) * 4, :],
                  ps[:, :].rearrange("p (k e) -> p k e", k=4), [ps], [uc])
    P.dma(self.uTs.t[:, c * 128:(c + 1) * 128].rearrange("(k p) e -> p k e", p=128), uc[:, :, :],
          reads=[uc], writes=[self.uTs])
    v32 = self.pp_v32[c % 2]
    P.dma(v32[:, :], self.inp["peer_v"][c * 128:(c + 1) * 128, :], writes=[v32])
    vc = self.pp_vcv[c % 2]
    self.copy("pool", vc[:, :], v32[:, :], [v32], [vc])
    P.dma(self.vs.t[c * 128:(c + 1) * 128, :], vc[:, :], reads=[vc], writes=[self.vs])


K.prep_alloc = _prep_alloc
K.prep_step = _prep_step


def _skt(self, stg, skT):
    P = self.P
    for hc in range(16):
        st = stg[hc % 2]
        P.dma(st[:, 0:128], self.inp["peer_sub_keys"][hc], writes=[st])
        ps = self.next_ps()
        self.tr(ps[:, 0:128], st[:, 0:128], self.ident[:, :], [st, self.ident], [ps])
        self.copy("dve", skT[:, hc, :], ps[:, 0:128], [ps], [skT])


K._skt = _skt
```

```python
import math
from contextlib import ExitStack
import numpy as np
import concourse.bass as bass
import concourse.mybir as mybir
from concourse.bass_utils import run_bass_kernel_spmd

F32 = mybir.dt.float32
BF16 = mybir.dt.bfloat16
I32 = mybir.dt.int32
AF = mybir.ActivationFunctionType
ALU = mybir.AluOpType
AX = mybir.AxisListType

D = 1024
NCOL = 4512
NG = 36
EPS = 1e-6
GN_EPS = 64e-5
NDSEM = 12


class Buf:
    __slots__ = ("name", "w", "r")

    def __init__(self, name):
        self.name = name
        self.w = None
        self.r = []


class TT:
    def __init__(self, t, buf):
        self.t = t
        self.buf = buf

    def __getitem__(self, k):
        return self.t[k]


class Prog:
    def __init__(self, nc, es):
        self.nc = nc
        self.es = es
        self.engs = ["pe", "act", "dve", "pool", "sp"]
        self.q = {e: [] for e in self.engs}
        self.cnt = {e: 0 for e in self.engs}
        self.sem = {}
        for e in ["pe", "act", "dve", "pool"]:
            self.sem[e] = es.enter_context(nc.semaphore("s_" + e))
        self.dsem = [es.enter_context(nc.semaphore("d%d" % i)) for i in range(NDSEM)]
        self.dcnt = [0] * NDSEM
        self.dnext = 0
        self.seen = {e: {} for e in self.engs}
        self.nb = 0
        self.epoch = 0
        self.semtab = {(e, 0): self.sem[e] for e in self.sem}

    def buf(self, name=None):
        self.nb += 1
        return Buf(name or "b%d" % self.nb)

    def sb(self, es, name, shape, dt):
        t = es.enter_context(self.nc.sbuf_tensor(name, list(shape), dt))
        return TT(t, self.buf(name))

    def ps(self, es, name, shape, dt=F32):
        t = es.enter_context(self.nc.psum_tensor(name, list(shape), dt))
        return TT(t, self.buf(name))

    def _semobj(self, key):
        return self.semtab[key] if isinstance(key, tuple) else self.dsem[key]

    def _need(self, eng, ev, waits):
        if ev is None:
            return
        key, val, peng = ev
        if eng == "pe" and peng == "pe":
            return
        if isinstance(key, tuple) and key[1] < self.epoch:
            return
        if self.seen[eng].get(key, 0) >= val:
            return
        if waits.get(key, 0) < val:
            waits[key] = val

    def op(self, eng, fn, reads=(), writes=(), dma=False):
        waits = {}
        for b in reads:
            b = b.buf if isinstance(b, TT) else b
            self._need(eng, b.w, waits)
        for b in writes:
            b = b.buf if isinstance(b, TT) else b
            self._need(eng, b.w, waits)
            for ev in b.r:
                self._need(eng, ev, waits)
        if dma:
            j = self.dnext
            self.dnext = (self.dnext + 1) % NDSEM
            if self.dcnt[j] > 0:
                ev = (j, 16 * self.dcnt[j], "dma")
                self._need(eng, ev, waits)
            self.dcnt[j] += 1
            ev = (j, 16 * self.dcnt[j], "dma")
            semo, inc = self.dsem[j], 16
        else:
            self.cnt[eng] += 1
            ev = ((eng, self.epoch), self.cnt[eng], eng)
            semo, inc = self.sem[eng], 1
        for k, v in waits.items():
            self.seen[eng][k] = v
        wl = [(self._semobj(k), v) for k, v in waits.items()]

        def thunk(e, wl=wl, fn=fn, semo=semo, inc=inc):
            for s, v in wl:
                e.wait_ge(s, v)
            fn(e).then_inc(semo, inc)

        self.q[eng].append(thunk)
        for b in reads:
            b = b.buf if isinstance(b, TT) else b
            b.r.append(ev)
            if len(b.r) > 64:
                mx = {}
                for (k, v, pe) in b.r:
                    if k not in mx or mx[k][1] < v:
                        mx[k] = (k, v, pe)
                b.r = list(mx.values())
        for b in writes:
            b = b.buf if isinstance(b, TT) else b
            b.w = ev
            b.r = []
        return ev

    def dma(self, out, in_, reads=(), writes=(), eng="sp", **kw):
        return self.op(eng, lambda e: e.dma_start(out=out, in_=in_, **kw), reads, writes, dma=True)

    def barrier(self):
        tot = {(e, self.epoch): self.cnt[e] for e in ["pe", "act", "dve", "pool"]}
        dt = {j: 16 * self.dcnt[j] for j in range(NDSEM)}
        for eng in self.engs:
            wl = []
            for k, v in list(tot.items()) + list(dt.items()):
                if v > 0 and self.seen[eng].get(k, 0) < v and not (isinstance(k, tuple) and k[0] == eng):
                    wl.append((self._semobj(k), v))
                    self.seen[eng][k] = v

            def thunk(e, wl=wl):
                for s, v in wl:
                    e.wait_ge(s, v)

            self.q[eng].append(thunk)
        if max(self.cnt.values()) > 6000:
            self.epoch += 1
            for e in ["pe", "act", "dve", "pool"]:
                self.sem[e] = self.es.enter_context(self.nc.semaphore("s_%s_%d" % (e, self.epoch)))
                self.semtab[(e, self.epoch)] = self.sem[e]
                self.cnt[e] = 0

    def emit(self):
        nc = self.nc
        with nc.Block() as block:
            @block.tensor
            def _(e):
                for f in self.q["pe"]:
                    f(e)

            @block.scalar
            def _(e):
                for f in self.q["act"]:
                    f(e)

            @block.vector
            def _(e):
                for f in self.q["dve"]:
                    f(e)

            @block.gpsimd
            def _(e):
                for f in self.q["pool"]:
                    f(e)

            @block.sync
            def _(e):
                for f in self.q["sp"]:
                    f(e)


def bc(ap, shape):
    return ap.to_broadcast(list(shape))


class K:
    def __init__(self, S, debug=False):
        self.S = S
        self.debug = debug
        self.nc = bass.Bass("TRN2", target_bir_lowering=False)
        self.inp = {}
        self.rr = 0
        self.split_delta = True
        self.prep_done = 0

    def din(self, name, shape, dt=F32):
        a = self.nc.dram_tensor(name, list(shape), dt, kind="ExternalInput").ap()
        self.inp[name] = a
        return a

    def dscr(self, name, shape, dt=F32):
        return TT(self.nc.dram_tensor(name, list(shape), dt, kind="Internal").ap(), Buf(name))

    def ev_eng(self):
        self.rr += 1
        return ["act", "dve"][self.rr % 2]

    def act(self, out, in_, func, reads, writes, bias=None, scale=1.0, accum=None):
        kw = {}
        if bias is not None:
            kw["bias"] = bias
        if accum is not None:
            kw["accum_out"] = accum
        return self.P.op("act", lambda e: e.activation(out=out, in_=in_, func=func, scale=scale, **kw),
                         reads, writes)

    def tt(self, eng, out, in0, in1, op, reads, writes):
        return self.P.op(eng, lambda e: e.tensor_tensor(out=out, in0=in0, in1=in1, op=op), reads, writes)

    def ts(self, eng, out, in0, s1, op0, reads, writes, s2=None, op1=None):
        if op1 is None:
            return self.P.op(eng, lambda e: e.tensor_scalar(out=out, in0=in0, scalar1=s1, scalar2=None, op0=op0),
                             reads, writes)
        return self.P.op(eng, lambda e: e.tensor_scalar(out=out, in0=in0, scalar1=s1, scalar2=s2, op0=op0, op1=op1),
                         reads, writes)

    def stt(self, eng, out, in0, scalar, in1, op0, op1, reads, writes):
        return self.P.op(eng, lambda e: e.scalar_tensor_tensor(out=out, in0=in0, scalar=scalar, in1=in1,
                                                               op0=op0, op1=op1), reads, writes)

    def copy(self, eng, out, in_, reads, writes):
        if eng == "act":
            return self.act(out, in_, AF.Copy, reads, writes)
        return self.P.op(eng, lambda e: e.tensor_copy(out=out, in_=in_), reads, writes)

    def mm(self, out, lhsT, rhs, start, stop, reads, writes):
        return self.P.op("pe", lambda e: e.matmul(out, lhsT, rhs, start=start, stop=stop), reads, writes)

    def tr(self, out, in_, ident, reads, writes):
        return self.P.op("pe", lambda e: e.transpose(out, in_, ident), reads, writes)

    def memset(self, eng, ap, val, writes):
        return self.P.op(eng, lambda e: e.memset(ap, val), (), writes)

    def rsqrt(self, out, in_, mul, add, reads, writes, eng="dve"):
        self.ts(eng, out, in_, mul, ALU.mult, reads, writes, s2=add, op1=ALU.add)
        self.act(out, out, AF.Sqrt, writes, writes)
        self.P.op("dve", lambda e: e.reciprocal(out=out, in_=out), writes, writes)

    def next_ps(self):
        self.psi = (self.psi + 1) % len(self.pspool)
        return self.pspool[self.psi]


def host_consts():
    c = {}
    c["ident"] = np.eye(128, dtype=np.float32)
    bo = np.zeros((128, 128), np.float32)
    bo[:64, :64] = 1.0
    bo[64:, 64:] = 1.0
    c["blockones"] = bo
    c["ones"] = np.ones((128, 128), np.float32)
    s = np.arange(64)[:, None]
    t = np.arange(64)[None, :]
    m = np.zeros((64, 3, 8, 64), np.float32)
    m[:, 0] = (s < t).astype(np.float32)[:, None, :]
    m[:, 1] = (s <= t).astype(np.float32)[:, None, :]
    m[:, 2] = (t < s).astype(np.float32)[:, None, :]
    c["masks"] = m.reshape(64, 3 * 8 * 64)
    invf = (10000.0 ** (-np.arange(0, 32, 2, dtype=np.float32) / 32)).astype(np.float32)
    rc = np.zeros((128, 4), np.float32)
    rc[64:80, 0] = invf
    rc[80:96, 0] = invf
    rc[64:80, 1] = -1.0
    rc[80:96, 1] = 1.0
    rc[:, 2] = math.pi / 2
    rc[:, 3] = 0.0
    c["ropec"] = rc
    return c


def _build_phase_a(self):
    P, nc, S = self.P, self.nc, self.S
    with ExitStack() as es:
        win = P.sb(es, "win", [128, 8, NCOL], BF16)
        stg = [P.sb(es, "wstg%d" % i, [128, NCOL], F32) for i in range(2)]
        gcol = P.sb(es, "gcol", [128, 8], F32)
        xt = [P.sb(es, "xt%d" % i, [128, D], F32) for i in range(2)]
        junk = P.sb(es, "junk", [128, D], F32)
        hb = [P.sb(es, "hb%d" % i, [128, D], BF16) for i in range(2)]
        ss = [P.sb(es, "ss%d" % i, [128, 1], F32) for i in range(2)]
        hT = [P.sb(es, "hT%d" % i, [128, 8, 512], BF16) for i in range(2)]
        zst = [P.sb(es, "zst%d" % i, [128, 512], F32) for i in range(4)]
        zero = P.sb(es, "zero", [128, 16], F32)
        pst = [P.ps(es, "pst%d" % i, [128, 8, 128], BF16) for i in range(2)]
        psz = [P.ps(es, "psz%d" % i, [128, 512], F32) for i in range(4)]
        w_in = self.inp["w_in"]
        P.dma(gcol[:, :], self.inp["norm_mix"].rearrange("(k p) -> p k", p=128), writes=[gcol],
              allow_slow_non_contiguous=True)
        self.memset("pool", zero[:, :], 0.0, [zero])
        P.dma(self.zT.t[0:14 * 128, 0:1].rearrange("(g p) o -> p (g o)", p=128), zero[:, 0:14],
              reads=[zero], writes=[self.zT], allow_slow_non_contiguous=True)
        for kc in range(8):
            st = stg[kc % 2]
            P.dma(st[:, :], w_in[kc * 128:(kc + 1) * 128, :], writes=[st])
            self.copy(["act", "dve", "pool"][kc % 3], win[:, kc, :], st[:, :], [st], [win])
        nsub = S // 128
        for ti in range(S // 512):
            h_T = hT[ti % 2]
            for sj in range(4):
                si = ti * 4 + sj
                x_t, h_b, s_s, p_t = xt[si % 2], hb[si % 2], ss[si % 2], pst[si % 2]
                P.dma(x_t[:, :], self.inp["x"][si * 128:(si + 1) * 128, :], writes=[x_t])
                self.act(junk[:, :], x_t[:, :], AF.Square, [x_t], [junk, s_s], accum=s_s[:, :])
                self.rsqrt(s_s[:, :], s_s[:, :], 1.0 / D, EPS, [s_s], [s_s])
                self.ts("dve", h_b[:, :], x_t[:, :], s_s[:, 0:1], ALU.mult, [x_t, s_s], [h_b])
                for kc in range(8):
                    self.tr(p_t[:, kc, :], h_b[:, kc * 128:(kc + 1) * 128], self.identb[:, :],
                            [h_b, self.identb], [p_t])
                self.tt("dve", h_T[:, :, sj * 128:(sj + 1) * 128], p_t[:, :, :],
                        bc(gcol[:, :].unsqueeze(2), [128, 8, 128]), ALU.mult, [p_t, gcol], [h_T])
            for g in range(NG):
                c0 = g * 128 if g < 19 else (2432 if g == 19 else 2464 + (g - 20) * 128)
                cw = 32 if g == 19 else 128
                pz = psz[g % 4]
                zs = zst[g % 4]
                for kc in range(8):
                    self.mm(pz[0:cw, :], win[:, kc, c0:c0 + cw], h_T[:, kc, :], kc == 0, kc == 7,
                            [win, h_T], [pz])
                if g >= 20:
                    self.act(zs[0:cw, :], pz[0:cw, :], AF.Sigmoid, [pz], [zs])
                else:
                    self.copy(self.ev_eng(), zs[0:cw, :], pz[0:cw, :], [pz], [zs])
                P.dma(self.zT.t[g * 128:g * 128 + cw, 1 + ti * 512:1 + (ti + 1) * 512], zs[0:cw, :],
                      reads=[zs], writes=[self.zT])
    P.barrier()


K.phase_a = _build_phase_a


def _build(self):
    nc, S = self.nc, self.S
    inp = self.din
    inp("x", [S, D]); inp("p", [S, 256]); inp("pos", [1, S], I32)
    inp("norm_mix", [D]); inp("w_in", [D, NCOL]); inp("rw_mu", [1792]); inp("rw_w0", [512])
    inp("rw_w2", [64, 512]); inp("rw_a0", [512]); inp("rw_a2", [64, 512]); inp("rw_g2", [128, 512])
    inp("rw_k_k", [512]); inp("rw_k_a", [512]); inp("rw_r_k", [512]); inp("rw_gn_w", [512])
    inp("rw_gn_b", [512]); inp("rw_w_o", [512, D]); inp("mla_q_norm", [384]); inp("mla_w_uq", [384, 768])
    inp("mla_kv_norm", [256]); inp("mla_w_ukv", [256, 1024]); inp("mla_w_o", [512, D]); inp("w_out", [D, D])
    inp("norm_ffn", [D]); inp("peer_w_q", [D, 2048]); inp("peer_sub_keys", [16, 128, 128])
    inp("peer_u", [16384, D]); inp("peer_v", [16384, D]); inp("norm_ple", [D]); inp("ple_w_gate", [D, D])
    inp("ple_w_proj", [256, D]); inp("norm_final", [D])
    for k, v in host_consts().items():
        inp("c_" + k, list(v.shape))
    self.out = nc.dram_tensor("out", [S, D], F32, kind="ExternalOutput").ap()
    self.zT = self.dscr("zT", [NG * 128, S + 1])
    self.ygT = self.dscr("ygT", [512, S], BF16)
    self.x1 = self.dscr("x1", [S, D])
    self.oTs = self.dscr("oTs", [512, S], BF16)
    self.wqs = self.dscr("wqs", [1024, 2048], BF16)
    self.uTs = self.dscr("uTs", [1024, 16384], BF16)
    self.vs = self.dscr("vs", [16384, 1024], BF16)
    if self.debug:
        self.dbg = {}
    with ExitStack() as es:
        self.P = P = Prog(nc, es)
        self.ident = P.sb(es, "ident", [128, 128], F32)
        self.identb = P.sb(es, "identb", [128, 128], BF16)
        self.blockones = P.sb(es, "blockones", [128, 128], F32)
        self.ones = P.sb(es, "onesf", [128, 128], F32)
        P.dma(self.ident[:, :], self.inp["c_ident"], writes=[self.ident])
        P.dma(self.blockones[:, :], self.inp["c_blockones"], writes=[self.blockones])
        P.dma(self.ones[:, :], self.inp["c_ones"], writes=[self.ones])
        self.copy("dve", self.identb[:, :], self.ident[:, :], [self.ident], [self.identb])
        P.barrier()
        self.phase_a()
        if self.debug != "a":
            self.phase_b()
        if self.debug not in ("a", "b"):
            self.phase_c()
        if self.debug not in ("a", "b", "c"):
            self.phase_d()
        if self.debug == "c":
            P.dma(self.out[:, :], self.x1.t[:, :], reads=[self.x1])
        if self.debug == "b":
            with ExitStack() as es2:
                d1 = P.sb(es2, "dbg1", [128, 4, 512], BF16)
                d2 = P.sb(es2, "dbg2", [128, 4, 512], F32)
                P.dma(d1[:, :, :], self.ygT.t[0:512, 0:512].rearrange("(g p) t -> p g t", p=128), reads=[self.ygT], writes=[d1])
                self.copy("dve", d2[:, :, :], d1[:, :, :], [d1], [d2])
                P.dma(self.out[0:512, 0:512].rearrange("(g p) t -> p g t", p=128), d2[:, :, :], reads=[d2])
                P.barrier()
        if self.debug == "a":
            P.dma(self.out[0:512, 0:512], self.zT.t[0:512, 1:513], reads=[self.zT], eng="sp")
            P.dma(self.out[0:512, 512:1024], self.zT.t[2560:3072, 1:513], reads=[self.zT], eng="sp")
        P.barrier()
        P.emit()
    return nc


K.build = _build


def make_inputs(S, b, x, p, positions, **w):
    m = {"x": np.ascontiguousarray(x[b]), "p": np.ascontiguousarray(p[0, b]),
         "pos": np.ascontiguousarray(positions[b].reshape(1, S).astype(np.int32))}
    for k, v in w.items():
        a = np.asarray(v)
        if k == "norm_final":
            m[k] = np.ascontiguousarray(a)
        elif k == "rw_r_k":
            m[k] = np.ascontiguousarray(a[0].reshape(512))
        elif k == "peer_sub_keys":
            m[k] = np.ascontiguousarray(a[0].reshape(16, 128, 128))
        else:
            m[k] = np.ascontiguousarray(a[0])
    for k, v in host_consts().items():
        m["c_" + k] = v
    return m


def kernel(x, p, positions, **w):
    x = np.asarray(x); p = np.asarray(p); positions = np.asarray(positions)
    B, S = x.shape[0], x.shape[1]
    kb = K(S)
    nc = kb.build()
    in_maps = [make_inputs(S, b, x, p, positions, **w) for b in range(B)]
    res = run_bass_kernel_spmd(nc, in_maps, core_ids=list(range(B)))
    return np.stack([r["out"] for r in res.results], axis=0).astype(np.float32)


def _col(self, es, name, src, ng):
    t = self.P.sb(es, name, [128, ng], F32)
    self.P.dma(t[:, :], self.inp[src].rearrange("(g p) -> p g", p=128), writes=[t],
               allow_slow_non_contiguous=True)
    return t


K.col = _col


def _build_phase_b(self):
    P, nc, S = self.P, self.nc, self.S
    with ExitStack() as es:
        sb = lambda n, sh, dt=F32: P.sb(es, n, sh, dt)
        mu = self.col(es, "mu", "rw_mu", 14)
        w0 = self.col(es, "w0c", "rw_w0", 4)
        a0 = self.col(es, "a0c", "rw_a0", 4)
        kkc = self.col(es, "kkc", "rw_k_k", 4)
        kac = self.col(es, "kac", "rw_k_a", 4)
        rkc = self.col(es, "rkc", "rw_r_k", 4)
        gnw = self.col(es, "gnw", "rw_gn_w", 4)
        gnb = self.col(es, "gnb", "rw_gn_b", 4)
        w2 = sb("w2", [64, 512]); P.dma(w2[:, :], self.inp["rw_w2"], writes=[w2])
        a2 = sb("a2", [128, 512]); P.dma(a2[64:128, :], self.inp["rw_a2"], writes=[a2])
        g2 = sb("g2", [128, 512]); P.dma(g2[:, :], self.inp["rw_g2"], writes=[g2])
        masks = sb("masks", [64, 3, 8, 64])
        P.dma(masks[:, :, :, :].rearrange("p a h t -> p (a h t)"), self.inp["c_masks"], writes=[masks])
        zin = sb("zin", [128, 14, 513])
        zs = sb("zs", [128, 14, 512])
        big = [sb("big%d" % i, [128, 4, 512]) for i in range(8)]
        tw = sb("tw", [64, 512]); sg = sb("sgz", [128, 512])
        gC = sb("gC", [128, 4, 8])
        Hc = sb("Hc", [128, 4, 64])
        Ht = sb("Ht", [128, 4, 64])
        ygo = sb("ygo", [128, 4, 512], BF16)
        c64 = lambda n: sb(n, [64, 8, 64])
        Np = [c64("Np0"), c64("Np1")]; Ntp = [c64("Ntp0"), c64("Ntp1")]
        AkT = c64("AkT"); ArbT = c64("ArbT"); ArkT = c64("ArkT")
        U = [c64("U0"), c64("U1")]
        Vtm = c64("Vtm"); Btm = sb("Btm", [64, 4, 128]); Ktm = sb("Ktm", [64, 4, 128])
        Ysb = c64("Ysb"); Ysq = c64("Ysq")
        st = sb("st", [64, 4, 8])
        eo = sb("eo", [128, 2])
        self.memset("dve", eo[:, :], 0.0, [eo])
        self.memset("dve", eo[0:64, 0:1], 1.0, [eo])
        self.memset("dve", eo[64:128, 1:2], 1.0, [eo])
        mk = {}
        for nm in ("b", "a", "k", "r"):
            mk[nm] = [sb("mk_%s%d" % (nm, i), [128, 4, 64]) for i in range(2)]
        self.pspool = [P.ps(es, "pb%d" % i, [128, 512], F32) for i in range(8)]
        self.psi = 0
        self.memset("dve", Hc[:, :, :], 0.0, [Hc])
        self.prep_alloc(es)
        prep_per_chunk = -(-128 // (S // 64))
        for ti in range(S // 512):
            t0 = ti * 512
            P.dma(zin[:, :, :], self.zT.t[0:1792, t0:t0 + 513].rearrange("(g p) t -> p g t", p=128),
                  reads=[self.zT], writes=[zin])
            self.tt("dve", zs[:, :, :], zin[:, :, 0:512], zin[:, :, 1:513], ALU.subtract, [zin], [zs])
            self.tt("pool", zs[:, :, :], zs[:, :, :], bc(mu[:, :].unsqueeze(2), [128, 14, 512]), ALU.mult,
                    [zs, mu], [zs])
            self.tt("dve", zs[:, :, :], zs[:, :, :], zin[:, :, 1:513], ALU.add, [zs, zin], [zs])
            r_, k_, v_ = zs[:, 0:4, :], zs[:, 4:8, :], zs[:, 8:12, :]
            A, B, C, Dd, E, Fb, G, H = big
            self.act(tw[:, :], zs[0:64, 12, :], AF.Tanh, [zs], [tw])
            for g in range(4):
                ps = self.next_ps()
                self.mm(ps[:, :], w2[:, g * 128:(g + 1) * 128], tw[:, :], True, True, [w2, tw], [ps])
                self.act(E[:, g, :], ps[:, :], AF.Sigmoid, [ps, w0], [E], bias=w0[:, g:g + 1])
            self.ts("pool", E[:, :, :], E[:, :, :], -math.exp(-0.5), ALU.mult, [E], [E])
            src = E
            pp = [A, B]
            k = 0
            for sh in (1, 2, 4, 8, 16, 32):
                dst = pp[k % 2]
                sv = src[:, :, :].rearrange("p g (c t) -> p (g c) t", t=64)
                dv = dst[:, :, :].rearrange("p g (c t) -> p (g c) t", t=64)
                self.tt("dve", dv[:, :, sh:64], sv[:, :, sh:64], sv[:, :, 0:64 - sh], ALU.add, [src], [dst])
                self.copy("pool", dv[:, :, 0:sh], sv[:, :, 0:sh], [src], [dst])
                src = dst
                k += 1
            X = src
            Y = A if X is B else B
            self.act(C[:, :, :], X[:, :, :], AF.Exp, [X], [C])
            self.act(Dd[:, :, :], X[:, :, :], AF.Exp, [X], [Dd], scale=-1.0)
            self.tt("dve", Y[:, :, :], X[:, :, :], E[:, :, :], ALU.subtract, [X, E], [Y])
            self.act(Y[:, :, :], Y[:, :, :], AF.Exp, [Y], [Y])
            self.copy("pool", gC[:, :, :], C[:, :, :].rearrange("p g (c t) -> p g c t", t=64)[:, :, :, 63],
                      [C], [gC])
            for g in range(4):
                self.ts("dve", Fb[:, g, :], k_[:, g, :], kkc[:, g:g + 1], ALU.mult, [zs, kkc], [Fb])
            self.tt("pool", G[:, :, :], Fb[:, :, :], Fb[:, :, :], ALU.mult, [Fb], [G])
            for g in range(4):
                ps = self.next_ps()
                self.mm(ps[:, :], self.blockones[:, :], G[:, g, :], True, True, [self.blockones, G], [ps])
                self.ts("dve", E[:, g, :], ps[:, :], 1e-24, ALU.add, [ps], [E])
            self.act(E[:, :, :], E[:, :, :], AF.Sqrt, [E], [E])
            P.op("dve", lambda e: e.reciprocal(out=E[:, :, :], in_=E[:, :, :]), [E], [E])
            self.tt("dve", Fb[:, :, :], Fb[:, :, :], E[:, :, :], ALU.mult, [Fb, E], [Fb])
            for g in range(4):
                ps = self.next_ps()
                self.mm(ps[:, :], a2[64:128, g * 128:(g + 1) * 128], zs[64:128, 12, :], True, True, [a2, zs], [ps])
                self.act(G[:, g, :], ps[:, :], AF.Sigmoid, [ps, a0], [G], bias=a0[:, g:g + 1])
            self.stt("dve", Y[:, :, :], Fb[:, :, :], -1.0, Y[:, :, :], ALU.mult, ALU.mult, [Fb, Y], [Y])
            self.tt("pool", H[:, :, :], Fb[:, :, :], G[:, :, :], ALU.mult, [Fb, G], [H])
            self.tt("dve", H[:, :, :], H[:, :, :], Dd[:, :, :], ALU.mult, [H, Dd], [H])
            for g in range(4):
                self.ts("dve", G[:, g, :], G[:, g, :], -1.0, ALU.add, [G, kac], [G], s2=kac[:, g:g + 1], op1=ALU.mult)
            self.stt("dve", G[:, :, :], G[:, :, :], 1.0, k_, ALU.add, ALU.mult, [G, zs], [G])
            self.tt("pool", Fb[:, :, :], r_, G[:, :, :], ALU.mult, [zs, G], [Fb])
            for g in range(4):
                self.ts("dve", Fb[:, g, :], Fb[:, g, :], rkc[:, g:g + 1], ALU.mult, [Fb, rkc], [Fb])
            for g in range(4):
                ps = self.next_ps()
                self.mm(ps[:, :], self.blockones[:, :], Fb[:, g, :], True, True, [self.blockones, Fb], [ps])
                self.tt("dve", E[:, g, :], ps[:, :], v_[:, g, :], ALU.mult, [ps, zs], [E])
            bonus = E
            self.tt("dve", Dd[:, :, :], Dd[:, :, :], G[:, :, :], ALU.mult, [Dd, G], [Dd])
            self.tt("pool", C[:, :, :], C[:, :, :], r_, ALU.mult, [C, zs], [C])
            self.act(sg[:, :], zs[:, 13, :], AF.Sigmoid, [zs], [sg])
            for g in range(4):
                ps = self.next_ps()
                self.mm(ps[:, :], g2[:, g * 128:(g + 1) * 128], sg[:, :], True, True, [g2, sg], [ps])
                self.copy("act", G[:, g, :], ps[:, :], [ps], [G])
            rt, at, bt, kt, ynT = C, Y, H, Dd, X
            for ch in range(8 if getattr(self, "stopb", 9) > 2 else 0):
                cs = slice(ch * 64, (ch + 1) * 64)
                for _ in range(prep_per_chunk):
                    if self.prep_done < 128:
                        self.prep_step()
                for (srcT, dstT, rd) in ((bt, Btm, [bt]), (kt, Ktm, [kt]), (v_, Vtm, [zs])):
                    ps = self.next_ps()
                    for g in range(4):
                        self.tr(ps[0:64, g * 128:(g + 1) * 128], srcT[:, g, cs], self.ident[:, :],
                                rd + [self.ident], [ps])
                    dv = dstT[:, :, :].rearrange("p a b -> p (a b)")
                    self.copy(self.ev_eng(), dv, ps[0:64, :], [ps], [dstT])
                for nm, srcT in (("b", bt), ("a", at), ("k", kt), ("r", rt)):
                    for e2 in range(2):
                        self.ts(["dve", "pool"][e2], mk[nm][e2][:, :, :], srcT[:, :, cs], eo[:, e2:e2 + 1], ALU.mult,
                                [srcT, eo], [mk[nm][e2]])

                def amat(dst, lT, lrd, rT, rrd, mi):
                    ps = self.next_ps()
                    for h in range(8):
                        self.mm(ps[0:64, h * 64:(h + 1) * 64], mk[lT][h % 2][:, h // 2, :], rT[:, h // 2, cs],
                                True, True, [mk[lT][h % 2], rrd], [ps])
                    self.tt("dve", dst[:, :, :], ps[0:64, :].rearrange("p (h t) -> p h t", h=8),
                            masks[:, mi, :, :], ALU.mult, [ps, masks], [dst])
                if getattr(self, "stopb", 9) == 3:
                    continue
                amat(Np[0], "b", bt, at, at, 0)
                amat(Ntp[0], "a", at, bt, bt, 2)
                amat(AkT, "k", kt, at, at, 0)
                amat(ArbT, "b", bt, rt, rt, 1)
                amat(ArkT, "k", kt, rt, rt, 1)
                if getattr(self, "stopb", 9) == 4:
                    continue
                ps = self.next_ps()
                for h in range(8):
                    o = ps[0:64, h * 64:(h + 1) * 64]
                    self.mm(o, mk["a"][h % 2][:, h // 2, :], Hc[:, h // 2, :], True, False, [mk["a"][h % 2], Hc], [ps])
                    self.mm(o, AkT[:, h, :], Vtm[:, h, :], False, True, [AkT, Vtm], [ps])
                self.copy("act", U[0][:, :, :], ps[0:64, :].rearrange("p (h t) -> p h t", h=8), [ps], [U[0]])
                cur = 0
                for lvl in range(6):
                    Nn, Nt = Np[lvl % 2], Ntp[lvl % 2]
                    ps = self.next_ps()
                    for h in range(8):
                        self.mm(ps[0:64, h * 64:(h + 1) * 64], Nn[:, h, :], U[cur][:, h, :], True, True,
                                [Nn, U[cur]], [ps])
                    self.tt("dve", U[1 - cur][:, :, :], U[cur][:, :, :],
                            ps[0:64, :].rearrange("p (h t) -> p h t", h=8), ALU.add, [ps, U[cur]], [U[1 - cur]])
                    cur = 1 - cur
                    if lvl < 5:
                        N2, Nt2 = Np[(lvl + 1) % 2], Ntp[(lvl + 1) % 2]
                        ps = self.next_ps()
                        for h in range(8):
                            self.mm(ps[0:64, h * 64:(h + 1) * 64], Nt[:, h, :], Nn[:, h, :], True, True,
                                    [Nt, Nn], [ps])
                        ps2 = None
                        if lvl < 4:
                            ps2 = self.next_ps()
                            for h in range(8):
                                self.mm(ps2[0:64, h * 64:(h + 1) * 64], Nn[:, h, :], Nt[:, h, :], True, True,
                                        [Nt, Nn], [ps2])
                        self.copy("act", N2[:, :, :], ps[0:64, :].rearrange("p (h t) -> p h t", h=8), [ps], [N2])
                        if ps2 is not None:
                            self.copy("pool" if False else "dve", Nt2[:, :, :],
                                      ps2[0:64, :].rearrange("p (h t) -> p h t", h=8), [ps2], [Nt2])
                Uf = U[cur]
                if getattr(self, "stopb", 9) == 5:
                    continue
                psy = self.next_ps()
                for h in range(8):
                    o = psy[0:64, h * 64:(h + 1) * 64]
                    self.mm(o, mk["r"][h % 2][:, h // 2, :], Hc[:, h // 2, :], True, False, [mk["r"][h % 2], Hc], [psy])
                    self.mm(o, ArbT[:, h, :], Uf[:, h, :], False, False, [ArbT, Uf], [psy])
                    self.mm(o, ArkT[:, h, :], Vtm[:, h, :], False, True, [ArkT, Vtm], [psy])
                psh = self.next_ps()
                for h in range(8):
                    o = psh[:, h * 64:(h + 1) * 64]
                    self.mm(o, Btm[:, h // 2, :], Uf[:, h, :], True, False, [Btm, Uf], [psh])
                    self.mm(o, Ktm[:, h // 2, :], Vtm[:, h, :], False, True, [Ktm, Vtm], [psh])
                phv = psh[:, :].rearrange("p (g e v) -> p g e v", g=4, e=2)
                for e2 in range(2):
                    rs = slice(e2 * 64, e2 * 64 + 64)
                    self.tt("dve", Ht[rs, :, :], Hc[rs, :, :], phv[rs, :, e2, :], ALU.add, [Hc, psh], [Ht])
                    self.tt("dve", Hc[rs, :, :], Ht[rs, :, :], bc(gC[rs, :, ch:ch + 1], [64, 4, 64]), ALU.mult,
                            [Ht, gC], [Hc])
                if getattr(self, "stopb", 9) == 6:
                    continue
                yv = psy[0:64, :].rearrange("p (h t) -> p h t", h=8)
                self.copy("act", Ysb[:, :, :], yv, [psy], [Ysb])
                self.act(Ysq[:, :, :], yv, AF.Square, [psy], [Ysq])
                P.op("dve", lambda e: e.reduce_sum(out=st[:, 0, :], in_=Ysb[:, :, :], axis=AX.X), [Ysb], [st])
                P.op("dve", lambda e: e.reduce_sum(out=st[:, 1, :], in_=Ysq[:, :, :], axis=AX.X), [Ysq], [st])
                self.ts("dve", st[:, 0, :], st[:, 0, :], 1.0 / 64, ALU.mult, [st], [st])
                self.tt("dve", st[:, 2, :], st[:, 0, :], st[:, 0, :], ALU.mult, [st], [st])
                self.stt("dve", st[:, 1, :], st[:, 1, :], 1.0 / 64, st[:, 2, :], ALU.mult, ALU.subtract, [st], [st])
                self.ts("dve", st[:, 1, :], st[:, 1, :], GN_EPS, ALU.add, [st], [st])
                self.act(st[:, 1, :], st[:, 1, :], AF.Sqrt, [st], [st])
                P.op("dve", lambda e: e.reciprocal(out=st[:, 1, :], in_=st[:, 1, :]), [st], [st])
                self.tt("dve", Ysb[:, :, :], Ysb[:, :, :], bc(st[:, 0, :].unsqueeze(2), [64, 8, 64]), ALU.subtract,
                        [Ysb, st], [Ysb])
                self.tt("dve", Ysb[:, :, :], Ysb[:, :, :], bc(st[:, 1, :].unsqueeze(2), [64, 8, 64]), ALU.mult,
                        [Ysb, st], [Ysb])
                ps = self.next_ps()
                yf = Ysb[:, :, :].rearrange("p h v -> p (h v)")
                for g in range(4):
                    self.tr(ps[:, g * 64:(g + 1) * 64], yf[:, g * 128:(g + 1) * 128], self.ident[0:64, 0:64],
                            [Ysb, self.ident], [ps])
                for g in range(4):
                    self.act(ynT[:, g, cs], ps[:, g * 64:(g + 1) * 64], AF.Identity, [ps, gnw, gnb], [ynT],
                             bias=gnb[:, g:g + 1], scale=gnw[:, g:g + 1])
            self.tt("dve", ynT[:, :, :], ynT[:, :, :], bonus[:, :, :], ALU.add, [ynT, bonus], [ynT])
            self.tt("dve", ygo[:, :, :], ynT[:, :, :], G[:, :, :], ALU.mult, [ynT, G], [ygo])
            P.dma(self.ygT.t[:, t0:t0 + 512].rearrange("(g p) t -> p g t", p=128), ygo[:, :, :],
                  reads=[ygo], writes=[self.ygT])
            P.barrier()
    P.barrier()


K.phase_b = _build_phase_b


def _loadw(self, es, name, src_ap, rows, cols, stg):
    P = self.P
    nk = rows // 128
    t = name if isinstance(name, TT) else P.sb(es, name, [128, nk, cols], BF16)
    for kc in range(nk):
        st = stg[kc % 2]
        P.dma(st[:, 0:cols], src_ap[kc * 128:(kc + 1) * 128, :], writes=[st])
        self.copy(["act", "dve", "pool"][kc % 3], t[:, kc, :], st[:, 0:cols], [st], [t])
    return t


K.loadw = _loadw

MAGIC = 12582912.0
TWO_PI_1 = 6.28125
TWO_PI_2 = 2.0 * math.pi - 6.28125


def _build_phase_c(self):
    P, nc, S = self.P, self.nc, self.S
    NT = S // 512
    with ExitStack() as es:
        sb = lambda n, sh, dt=F32: P.sb(es, n, sh, dt)
        stg = [sb("cstg%d" % i, [128, 1024]) for i in range(2)]
        wuq = self.loadw(es, "wuq", self.inp["mla_w_uq"], 384, 768, stg)
        wkv = self.loadw(es, "wkv", self.inp["mla_w_ukv"], 256, 1024, stg)
        wrot = sb("wrot", [128, 3, 8, 96], BF16)
        self.memset("pool", wrot[:, :, :, :], 0.0, [wrot])
        wq4 = wuq[:, :, :].rearrange("p c (h e) -> p c h e", h=8)
        self.copy("dve", wrot[:, :, :, 64:80], wq4[:, :, :, 80:96], [wuq], [wrot])
        self.copy("dve", wrot[:, :, :, 80:96], wq4[:, :, :, 64:80], [wuq], [wrot])
        wk4 = wkv[:, :, :].rearrange("p c (h e) -> p c h e", h=8)
        wv = sb("wv", [128, 2, 8, 64], BF16)
        self.copy("dve", wv[:, :, :, :], wk4[:, :, :, 64:128], [wkv], [wv])
        qn = self.col(es, "qn", "mla_q_norm", 3)
        kvn = self.col(es, "kvn", "mla_kv_norm", 2)
        ropec = sb("ropec", [128, 4]); P.dma(ropec[:, :], self.inp["c_ropec"], writes=[ropec])
        onesb = sb("onesb", [128, 128], BF16)
        self.copy("dve", onesb[:, :], self.ones[:, :], [self.ones], [onesb])
        Kf = [sb("Kf%d" % h, [96, S], BF16) for h in range(8)]
        Vtm = sb("Vtm_a", [128, S // 128, 512], BF16)
        Qf = sb("Qf", [96, 8, 512], BF16)
        zc = sb("zc", [128, 3, 512]); zsq = sb("zsq", [128, 3, 512]); rstd = sb("rstd_c", [128, 512])
        cn = sb("cn", [128, 3, 512], BF16)
        posi = sb("posi", [96, 512], I32); ang = sb("ang", [96, 512]); kq = sb("kq", [96, 512])
        Ct = sb("Ct", [96, 512]); St = sb("St", [96, 512])
        kr = sb("kr", [96, 512]); krot = sb("krot", [96, 512]); krb = sb("krb", [96, 512], BF16)
        qa = sb("qa", [96, 512]); qb = sb("qb", [96, 512])
        Pt = [sb("Pt%d" % i, [128, 512], BF16) for i in range(3)]
        rec = sb("rec", [128, 512])
        oT = sb("oT", [128, 4, 512], BF16)
        self.pspool = [P.ps(es, "pc%d" % i, [128, 512], F32) for i in range(4)]
        self.psi = 0
        psO = [P.ps(es, "pO%d" % i, [128, 512], F32) for i in range(2)]
        psS = [P.ps(es, "pS%d" % i, [128, 512], F32) for i in range(2)]
        scale = 1.0 / math.sqrt(96.0)

        def rmsn(groups, ng, gcolt, nfeat):
            self.tt("pool", zsq[:, 0:ng, :], zc[:, 0:ng, :], zc[:, 0:ng, :], ALU.mult, [zc], [zsq])
            ps = self.next_ps()
            for c in range(ng):
                self.mm(ps[:, :], self.ones[:, :], zsq[:, c, :], c == 0, c == ng - 1, [self.ones, zsq], [ps])
            self.rsqrt(rstd[:, :], ps[:, :], 1.0 / nfeat, EPS, [ps], [rstd])
            for c in range(ng):
                self.stt("dve", cn[:, c, :], zc[:, c, :], gcolt[:, c:c + 1], rstd[:, :], ALU.mult, ALU.mult,
                         [zc, gcolt, rstd], [cn])

        for T in range(NT):
            t0 = T * 512
            P.dma(posi[:, :], self.inp["pos"][0:1, t0:t0 + 512].partition_broadcast(96), writes=[posi])
            self.copy("dve", ang[:, :], posi[:, :], [posi], [ang])
            self.ts("dve", ang[:, :], ang[:, :], ropec[0:96, 0:1], ALU.mult, [ang, ropec], [ang])
            self.ts("dve", kq[:, :], ang[:, :], 1.0 / (2 * math.pi), ALU.mult, [ang], [kq], s2=MAGIC, op1=ALU.add)
            self.ts("dve", kq[:, :], kq[:, :], -MAGIC, ALU.add, [kq], [kq])
            self.stt("dve", ang[:, :], kq[:, :], -TWO_PI_1, ang[:, :], ALU.mult, ALU.add, [kq, ang], [ang])
            self.stt("dve", ang[:, :], kq[:, :], -TWO_PI_2, ang[:, :], ALU.mult, ALU.add, [kq, ang], [ang])
            self.ts("dve", ang[:, :], ang[:, :], 3.14159, ALU.min, [ang], [ang], s2=-3.14159, op1=ALU.max)
            self.act(St[:, :], ang[:, :], AF.Sin, [ang], [St])
            self.ts("dve", St[:, :], St[:, :], ropec[0:96, 1:2], ALU.mult, [St, ropec], [St])
            self.act(kq[:, :], ang[:, :], AF.Abs, [ang], [kq])
            self.act(Ct[:, :], kq[:, :], AF.Sin, [kq, ropec], [Ct], bias=ropec[0:96, 2:3], scale=-1.0)
            P.dma(zc[:, 0:2, :], self.zT.t[17 * 128:19 * 128, 1 + t0:1 + t0 + 512].rearrange("(g p) t -> p g t", p=128),
                  reads=[self.zT], writes=[zc])
            rmsn(2, 2, kvn, 256.0)
            for h in range(8):
                ps = self.next_ps()
                for c in range(2):
                    self.mm(ps[0:64, :], wk4[:, c, h, 0:64], cn[:, c, :], c == 0, c == 1, [wkv, cn], [ps])
                self.copy(self.ev_eng(), Kf[h][0:64, t0:t0 + 512], ps[0:64, :], [ps], [Kf[h]])
            rz = 19 * 128
            P.dma(kr[64:96, :], self.zT.t[rz:rz + 32, 1 + t0:1 + t0 + 512], reads=[self.zT], writes=[kr])
            P.dma(krot[64:80, :], self.zT.t[rz + 16:rz + 32, 1 + t0:1 + t0 + 512], reads=[self.zT], writes=[krot])
            P.dma(krot[80:96, :], self.zT.t[rz:rz + 16, 1 + t0:1 + t0 + 512], reads=[self.zT], writes=[krot])
            self.tt("dve", kr[64:96, :], kr[64:96, :], Ct[64:96, :], ALU.mult, [kr, Ct], [kr])
            self.tt("dve", krot[64:96, :], krot[64:96, :], St[64:96, :], ALU.mult, [krot, St], [krot])
            self.tt("dve", krb[64:96, :], kr[64:96, :], krot[64:96, :], ALU.add, [kr, krot], [krb])
            for h in range(8):
                self.copy(["dve", "pool"][h % 2], Kf[h][64:96, t0:t0 + 512], krb[64:96, :], [krb], [Kf[h]])
            for b4 in range(4):
                ps = self.next_ps()
                for c in range(2):
                    self.mm(ps[:, :], cn[:, c, b4 * 128:(b4 + 1) * 128], wv[:, c, :, :].rearrange("p h e -> p (h e)"),
                            c == 0, c == 1, [cn, wv], [ps])
                self.copy(self.ev_eng(), Vtm[:, T * 4 + b4, :], ps[:, :], [ps], [Vtm])
            P.dma(zc[:, 0:3, :], self.zT.t[14 * 128:17 * 128, 1 + t0:1 + t0 + 512].rearrange("(g p) t -> p g t", p=128),
                  reads=[self.zT], writes=[zc])
            rmsn(3, 3, qn, 384.0)
            for h in range(8):
                ps = self.next_ps()
                ps2 = self.next_ps()
                for c in range(3):
                    self.mm(ps[0:96, :], wuq[:, c, h * 96:(h + 1) * 96], cn[:, c, :], c == 0, c == 2, [wuq, cn], [ps])
                for c in range(3):
                    self.mm(ps2[0:96, :], wrot[:, c, h, :], cn[:, c, :], c == 0, c == 2, [wrot, cn], [ps2])
                self.tt("dve", qa[:, :], ps[0:96, :], Ct[:, :], ALU.mult, [ps, Ct], [qa])
                self.tt("dve", qb[:, :], ps2[0:96, :], St[:, :], ALU.mult, [ps2, St], [qb])
                self.tt("pool", Qf[:, h, :], qa[:, :], qb[:, :], ALU.add, [qa, qb], [Qf])
            pi = 0
            for h in range(8):
                pO, pS = psO[h % 2], psS[h % 2]
                nkb = 4 * T + 4
                for kb in range(nkb):
                    nq0 = max(0, kb - 4 * T)
                    cl = slice(nq0 * 128, 512)
                    ps = self.next_ps()
                    self.mm(ps[:, cl], Kf[h][0:96, kb * 128:(kb + 1) * 128], Qf[0:96, h, cl], True, True,
                            [Kf[h], Qf], [ps])
                    pt = Pt[pi % 3]
                    pi += 1
                    self.act(pt[:, cl], ps[:, cl], AF.Exp, [ps], [pt], scale=scale)
                    if kb >= 4 * T:
                        self.memset("pool", pt[64:128, nq0 * 128:nq0 * 128 + 64], 0.0, [pt])
                    hp2 = (h // 2) * 128
                    self.mm(pO[:, cl], Vtm[:, kb, hp2:hp2 + 128], pt[:, cl], kb == 0, kb == nkb - 1, [Vtm, pt], [pO])
                    self.mm(pS[:, cl], onesb[:, :], pt[:, cl], kb == 0, kb == nkb - 1, [onesb, pt], [pS])
                P.op("dve", lambda e, o=rec[:, :], i=pS[:, :]: e.reciprocal(out=o, in_=i), [pS], [rec])
                rs = slice((h % 2) * 64, (h % 2) * 64 + 64)
                self.tt("dve", oT[rs, h // 2, :], pO[rs, :], rec[rs, :], ALU.mult, [pO, rec], [oT])
            P.dma(self.oTs.t[:, t0:t0 + 512].rearrange("(g p) t -> p g t", p=128), oT[:, :, :],
                  reads=[oT], writes=[self.oTs])
            P.barrier()
    P.barrier()
    with ExitStack() as es:
        sb = lambda n, sh, dt=F32: P.sb(es, n, sh, dt)
        wo_m = P.sb(es, "wo_m", [128, 4, 1024], BF16)
        wo_r = P.sb(es, "wo_r", [128, 4, 1024], BF16)
        w_o = P.sb(es, "w_o", [128, 8, 1024], BF16)
        stg = [sb("c2stg%d" % i, [128, 1024]) for i in range(2)]
        self.loadw(es, wo_m, self.inp["mla_w_o"], 512, 1024, stg)
        self.loadw(es, wo_r, self.inp["rw_w_o"], 512, 1024, stg)
        self.loadw(es, w_o, self.inp["w_out"], 1024, 1024, stg)
        oT = sb("oT2", [128, 4, 512], BF16)
        ygl = sb("ygl", [128, 4, 512], BF16)
        gA = [sb("gA%d" % i, [128, 512]) for i in range(2)]
        gB = [sb("gB%d" % i, [128, 512]) for i in range(2)]
        ta = sb("ta", [128, 512]); tb = sb("tb", [128, 512])
        mix = sb("mix", [128, 8, 512], BF16)
        xl = [sb("xl%d" % i, [128, 1024]) for i in range(2)]
        self.pspool = [P.ps(es, "pc2_%d" % i, [128, 512], F32) for i in range(6)]
        self.psi = 0
        for T in range(NT):
            t0 = T * 512
            P.dma(oT[:, :, :], self.oTs.t[:, t0:t0 + 512].rearrange("(g p) t -> p g t", p=128),
                  reads=[self.oTs], writes=[oT])
            P.dma(ygl[:, :, :], self.ygT.t[:, t0:t0 + 512].rearrange("(g p) t -> p g t", p=128),
                  reads=[self.ygT], writes=[ygl])
            for j in range(8):
                ga, gb = gA[j % 2], gB[j % 2]
                P.dma(ga[:, :], self.zT.t[(20 + j) * 128:(21 + j) * 128, 1 + t0:1 + t0 + 512], reads=[self.zT], writes=[ga])
                P.dma(gb[:, :], self.zT.t[(28 + j) * 128:(29 + j) * 128, 1 + t0:1 + t0 + 512], reads=[self.zT], writes=[gb])
                ps = self.next_ps()
                ps2 = self.next_ps()
                for c in range(4):
                    self.mm(ps[:, :], wo_r[:, c, j * 128:(j + 1) * 128], ygl[:, c, :], c == 0, c == 3, [wo_r, ygl], [ps])
                for c in range(4):
                    self.mm(ps2[:, :], wo_m[:, c, j * 128:(j + 1) * 128], oT[:, c, :], c == 0, c == 3, [wo_m, oT], [ps2])
                self.tt("dve", ta[:, :], ps[:, :], ga[:, :], ALU.mult, [ps, ga], [ta])
                self.tt("dve", tb[:, :], ps2[:, :], gb[:, :], ALU.mult, [ps2, gb], [tb])
                self.tt("pool", mix[:, j, :], ta[:, :], tb[:, :], ALU.add, [ta, tb], [mix])
            for b4 in range(4):
                x_l = xl[b4 % 2]
                r0 = t0 + b4 * 128
                P.dma(x_l[:, :], self.inp["x"][r0:r0 + 128, :], writes=[x_l])
                for hf in range(2):
                    ps = self.next_ps()
                    for j in range(8):
                        self.mm(ps[:, :], mix[:, j, b4 * 128:(b4 + 1) * 128], w_o[:, j, hf * 512:(hf + 1) * 512],
                                j == 0, j == 7, [mix, w_o], [ps])
                    self.tt("dve", x_l[:, hf * 512:(hf + 1) * 512], x_l[:, hf * 512:(hf + 1) * 512], ps[:, :], ALU.add,
                            [x_l, ps], [x_l])
                P.dma(self.x1.t[r0:r0 + 128, :], x_l[:, :], reads=[x_l], writes=[self.x1])
    P.barrier()


K.phase_c = _build_phase_c


def _build_phase_d(self):
    P, nc, S = self.P, self.nc, self.S
    NEG = -1e30
    with ExitStack() as es:
        sb = lambda n, sh, dt=F32: P.sb(es, n, sh, dt)
        stg = [sb("dstg%d" % i, [128, 1024]) for i in range(2)]
        self.pspool = [P.ps(es, "pdp%d" % i, [128, 512], F32) for i in range(4)]
        self.psi = 0
        if self.prep_done < 128:
            self.prep_alloc(es)
            while self.prep_done < 128:
                self.prep_step()
    P.barrier()
    with ExitStack() as es:
        sb = lambda n, sh, dt=F32: P.sb(es, n, sh, dt)
        self.pspool = [P.ps(es, "pd%d" % i, [128, 512], F32) for i in range(3)]
        self.psi = 0
        psOut = [[P.ps(es, "po%d%d" % (a, b), [128, 512], F32) for b in range(2)] for a in range(2)]
        uni = P.sb(es, "uni", [128, 8192], F32)
        wq = TT(uni[:, :].bitcast(BF16).rearrange("p (k c) -> p k c", k=8), P.buf("wqv"))
        wpg = P.sb(es, "wpg", [128, 8, 1024], BF16)
        wpp = P.sb(es, "wpp", [128, 2, 1024], BF16)
        skT = sb("skT", [128, 16, 128], BF16)
        with ExitStack() as es3:
            stg = [P.sb(es3, "dstg2_%d" % i, [128, 2048], F32) for i in range(2)]
            self.loadw(es, wq, self.inp["peer_w_q"], 1024, 2048, stg)
            P.dma(self.wqs.t[:, :].rearrange("(k p) c -> p k c", p=128), wq[:, :, :], reads=[wq], writes=[self.wqs])
            self.loadw(es, wpg, self.inp["ple_w_gate"], 1024, 1024, stg)
            self.loadw(es, wpp, self.inp["ple_w_proj"], 256, 1024, stg)
            self._skt(stg, skT)
            P.barrier()
        for hc in range(0):
            st = stg[hc % 2]
            P.dma(st[:, 0:128], self.inp["peer_sub_keys"][hc], writes=[st])
            ps = self.next_ps()
            self.tr(ps[:, 0:128], st[:, 0:128], self.ident[:, :], [st, self.ident], [ps])
            self.copy("dve", skT[:, hc, :], ps[:, 0:128], [ps], [skT])
        gf = self.col(es, "gf", "norm_ffn", 8)
        gp = self.col(es, "gp", "norm_ple", 8)
        gfin = sb("gfin", [128, 1024])
        P.dma(gfin[:, :], self.inp["norm_final"].rearrange("(o d) -> o d", o=1).partition_broadcast(128), writes=[gfin])
        x1t = [sb("x1t%d" % i, [128, 1024]) for i in range(2)]
        hb = sb("hb_d", [128, 1024], BF16)
        junk = hb
        ssd = sb("ssd", [128, 1])
        xnT = sb("xnT", [128, 8, 256], BF16)
        xn2T = sb("xn2T", [128, 8, 128], BF16)
        qT = sb("qT", [128, 16, 256], BF16)
        sS = [sb("sS%d" % i, [128, 16, 128]) for i in range(2)]
        s1pp = [sb("s1pp%d" % i, [128, 8, 128]) for i in range(2)]
        bE = [sb("bE%d" % i, [128, 8]) for i in range(2)]
        a16 = sb("a16", [128, 16]); b16 = sb("b16", [128, 16]); c16 = sb("c16", [128, 16]); e16 = sb("e16", [128, 16])
        tmpk = sb("tmpk", [128, 128]); cand = sb("cand", [128, 256]); cand2 = sb("cand2", [128, 256])
        sc = sb("scal", [128, 8])
        ubk = [sb("ubk%d" % i, [128, 8, 512], BF16) for i in range(2)]
        vbk = [sb("vbk%d" % i, [128, 4, 1024], BF16) for i in range(3)]
        dl = [TT(uni[:, i * 2048:(i + 1) * 2048].rearrange("p (a b c) -> p a b c", a=4, b=4), P.buf("dl%d" % i))
              for i in range(4)]
        Ee = [sb("Ee%d" % i, [128, 4, 4, 128], BF16) for i in range(4)]
        Gh = [sb("Gh%d" % i, [128, 4, 4, 128], BF16) for i in range(4)]
        dg = [sb("dg%d" % i, [128, 8, 128], BF16) for i in range(2)]
        wE = sb("wE", [128, 8])
        gel = [sb("gel%d" % i, [128, 512], BF16) for i in range(3)]
        Pm = [sb("Pm%d" % i, [128, 512], BF16) for i in range(3)]
        PTs = [sb("PTs%d" % i, [128, 4, 128], BF16) for i in range(3)]
        pst = P.ps(es, "pstd", [128, 8, 128], BF16)
        pl = sb("pl", [128, 256]); plb = sb("plb", [128, 256], BF16); pT = sb("pT", [128, 2, 128], BF16)
        gt = sb("gt", [128, 512])
        ib = self.identb

        def norm_T(xsrc, gcolt, dstT, csl):
            self.act(junk[:, :], xsrc[:, :], AF.Square, [xsrc], [junk, ssd], accum=ssd[:, :])
            self.rsqrt(ssd[:, :], ssd[:, :], 1.0 / D, EPS, [ssd], [ssd])
            self.ts("dve", hb[:, :], xsrc[:, :], ssd[:, 0:1], ALU.mult, [xsrc, ssd], [hb])
            for kc in range(8):
                self.tr(pst[:, kc, :], hb[:, kc * 128:(kc + 1) * 128], ib[:, :], [hb, ib], [pst])
            self.tt("dve", dstT[:, :, csl], pst[:, :, :], bc(gcolt[:, :].unsqueeze(2), [128, 8, 128]), ALU.mult,
                    [pst, gcolt], [dstT])

        def top16(dst, src, srcT, tmp, tmpT):
            P.op("dve", lambda e, o=dst[:, 0:8], i=src: e.max(out=o, in_=i), [dst, srcT], [dst])
            P.op("dve", lambda e, o=tmp, r=dst[:, 0:8], i=src: e.match_replace(out=o, in_to_replace=r, in_values=i,
                                                                                imm_value=NEG), [dst, srcT], [tmpT])
            P.op("dve", lambda e, o=dst[:, 8:16], i=tmp: e.max(out=o, in_=i), [tmpT], [dst])

        gi = 0
        mk = 0
        for tl in range(S // 256):
            P.dma(wq[:, :, :], self.wqs.t[:, :].rearrange("(k p) c -> p k c", p=128), reads=[self.wqs], writes=[wq])
            for sub in range(2):
                r0 = tl * 256 + sub * 128
                P.dma(x1t[sub][:, :], self.x1.t[r0:r0 + 128, :], reads=[self.x1], writes=[x1t[sub]])
                norm_T(x1t[sub], gf, xnT, slice(sub * 128, (sub + 1) * 128))
            for hc in range(16):
                ps = self.next_ps()
                for kc in range(8):
                    self.mm(ps[:, 0:256], wq[:, kc, hc * 128:(hc + 1) * 128], xnT[:, kc, :], kc == 0, kc == 7,
                            [wq, xnT], [ps])
                self.copy(self.ev_eng(), qT[:, hc, :], ps[:, 0:256], [ps], [qT])
            for sub in range(2):
                for g4 in range(4):
                    ps = self.next_ps()
                    for i4 in range(4):
                        hc = g4 * 4 + i4
                        self.mm(ps[:, i4 * 128:(i4 + 1) * 128], qT[:, hc, sub * 128:(sub + 1) * 128], skT[:, hc, :],
                                True, True, [qT, skT], [ps])
                    self.copy(self.ev_eng(), sS[sub][:, g4 * 4:(g4 + 1) * 4, :],
                              ps[:, :].rearrange("p (a n) -> p a n", a=4), [ps], [sS[sub]])
                for h in range(8):
                    s1 = sS[sub][:, 2 * h, :]
                    s2 = sS[sub][:, 2 * h + 1, :]
                    top16(a16, s1, sS[sub], tmpk[:, :], tmpk)
                    top16(b16, s2, sS[sub], tmpk[:, :], tmpk)
                    self.tt("dve", cand[:, :].rearrange("p (a b) -> p a b", a=16),
                            bc(a16[:, :].unsqueeze(2), [128, 16, 16]), bc(b16[:, :].unsqueeze(1), [128, 16, 16]),
                            ALU.add, [a16, b16], [cand])
                    top16(c16, cand[:, :], cand, cand2[:, :], cand2)
                    P.op("dve", lambda e, o=sc[:, 0:1], i=c16[:, :]: e.tensor_reduce(out=o, in_=i, axis=AX.X, op=ALU.min),
                         [c16], [sc])
                    P.op("dve", lambda e, o=sc[:, 1:2], i=c16[:, :]: e.tensor_reduce(out=o, in_=i, axis=AX.X, op=ALU.max),
                         [c16], [sc])
                    self.ts("dve", sc[:, 0:1], sc[:, 0:1], -2e-6, ALU.add, [sc], [sc])
                    self.ts("dve", sc[:, 2:3], sc[:, 1:2], -1.0, ALU.mult, [sc], [sc])
                    self.act(e16[:, :], c16[:, :], AF.Exp, [c16, sc], [e16, sc], bias=sc[:, 2:3], accum=sc[:, 3:4])
                    self.act(sc[:, 4:5], sc[:, 3:4], AF.Ln, [sc], [sc])
                    self.tt("dve", sc[:, 5:6], sc[:, 0:1], sc[:, 1:2], ALU.subtract, [sc], [sc])
                    self.tt("dve", bE[sub][:, h:h + 1], sc[:, 5:6], sc[:, 4:5], ALU.subtract, [sc], [bE[sub]])
                    self.ts("dve", s1pp[sub][:, h, :], s1, sc[:, 0:1], ALU.subtract, [sS[sub], sc], [s1pp[sub]])
                self.act(wE[:, :], bE[sub][:, :], AF.Exp, [bE[sub]], [wE])
                for h in range(8):
                    self.ts(["dve", "pool"][h % 2], dg[sub][:, h, :], ib[:, :], wE[:, h:h + 1], ALU.mult, [ib, wE], [dg[sub]])
            NK = 64
            P.barrier()

            def ldblk(blk):
                ub, vb = ubk[blk % 2], vbk[blk % 3]
                P.dma(ub[:, :, :], self.uTs.t[:, blk * 512:(blk + 1) * 512].rearrange("(k p) e -> p k e", p=128),
                      reads=[self.uTs], writes=[ub])
                P.dma(vb[:, :, :], self.vs.t[blk * 512:(blk + 1) * 512, :].rearrange("(c p) d -> p c d", p=128),
                      reads=[self.vs], writes=[vb])

            ldblk(0)

            def S1(k):
                blk, sub = k // 2, k % 2
                ub = ubk[blk % 2]
                if sub == 0 and blk + 1 < 32:
                    ldblk(blk + 1)
                psA = self.next_ps()
                for kc in range(8):
                    self.mm(psA[:, :], xnT[:, kc, sub * 128:(sub + 1) * 128], ub[:, kc, :], kc == 0, kc == 7,
                            [xnT, ub], [psA])
                for hg in range(2):
                    d_, e_, g_ = dl[(k % 2) * 2 + hg], Ee[(k % 2) * 2 + hg], Gh[(k % 2) * 2 + hg]
                    s2v = sS[sub][:, :, :].rearrange("p (h c) n -> p h c n", c=2)[:, hg * 4:(hg + 1) * 4, 1, :]
                    self.tt("dve" if (hg == 1 and k % 2 == 0) else "pool", d_[:, :, :, :],
                            bc(s2v.unsqueeze(2), [128, 4, 4, 128]),
                            bc(s1pp[sub][:, hg * 4:(hg + 1) * 4, blk * 4:(blk + 1) * 4].unsqueeze(3), [128, 4, 4, 128]),
                            ALU.add, [sS[sub], s1pp[sub]], [d_])
                    self.act(e_[:, :, :, :], d_[:, :, :, :], AF.Exp, [d_], [e_])
                    self.stt("dve", g_[:, :, :, :], d_[:, :, :, :], 0.0, e_[:, :, :, :], ALU.is_ge, ALU.mult,
                             [d_, e_], [g_])
                self.psA_k[k % 2] = psA

            def S1b(k):
                psA = self.psA_k[k % 2]
                self.act(gel[k % 3][:, :], psA[:, :], AF.Gelu, [psA], [gel[k % 3]])

            def S2(k):
                sub = k % 2
                psM = self.next_ps()
                for hg in range(2):
                    g_ = Gh[(k % 2) * 2 + hg]
                    for h4 in range(4):
                        h = hg * 4 + h4
                        self.mm(psM[:, :], dg[sub][:, h, :], g_[:, h4, :, :].rearrange("p a b -> p (a b)"),
                                h == 0, h == 7, [dg[sub], g_], [psM])
                self.psM_k[k % 2] = psM

            def S2b(k):
                psM = self.psM_k[k % 2]
                self.tt("dve", Pm[k % 3][:, :], gel[k % 3][:, :], psM[:, :], ALU.mult, [gel[k % 3], psM], [Pm[k % 3]])

            def S3(k):
                pm = Pm[k % 3]
                for ec in range(4):
                    self.tr(pst[:, ec, :], pm[:, ec * 128:(ec + 1) * 128], ib[:, :], [pm, ib], [pst])
                self.copy("act", PTs[k % 3][:, :, :], pst[:, 0:4, :], [pst], [PTs[k % 3]])

            def S4(k):
                blk, sub = k // 2, k % 2
                vb = vbk[blk % 3]
                pts = PTs[k % 3]
                for hf in range(2):
                    for ec in range(4):
                        self.mm(psOut[sub][hf][:, :], pts[:, ec, :], vb[:, ec, hf * 512:(hf + 1) * 512],
                                blk == 0 and ec == 0, blk == 31 and ec == 3, [pts, vb], [psOut[sub][hf]])

            self.psA_k = [None, None]
            self.psM_k = [None, None]
            for r in range(NK + 3):
                if 0 <= r - 3 < NK:
                    S4(r - 3)
                if r < NK:
                    S1(r)
                if 0 <= r - 1 < NK:
                    S2(r - 1)
                if 0 <= r - 2 < NK:
                    S3(r - 2)
                if r < NK:
                    S1b(r)
                if 0 <= r - 1 < NK:
                    S2b(r - 1)
            for sub in range(2):
                r0 = tl * 256 + sub * 128
                xx = x1t[sub]
                for hf in range(2):
                    self.tt("dve", xx[:, hf * 512:(hf + 1) * 512], xx[:, hf * 512:(hf + 1) * 512], psOut[sub][hf][:, :],
                            ALU.add, [xx, psOut[sub][hf]], [xx])
                norm_T(xx, gp, xn2T, slice(0, 128))
                P.dma(pl[:, :], self.inp["p"][r0:r0 + 128, :], writes=[pl])
                self.copy("pool", plb[:, :], pl[:, :], [pl], [plb])
                for c in range(2):
                    self.tr(pst[:, c, :], plb[:, c * 128:(c + 1) * 128], ib[:, :], [plb, ib], [pst])
                self.copy("act", pT[:, :, :], pst[:, 0:2, :], [pst], [pT])
                for hf in range(2):
                    hs = slice(hf * 512, (hf + 1) * 512)
                    ps = self.next_ps()
                    for kc in range(8):
                        self.mm(ps[:, :], xn2T[:, kc, :], wpg[:, kc, hs], kc == 0, kc == 7, [xn2T, wpg], [ps])
                    self.act(gt[:, :], ps[:, :], AF.Sigmoid, [ps], [gt])
                    ps2 = self.next_ps()
                    for c in range(2):
                        self.mm(ps2[:, :], pT[:, c, :], wpp[:, c, hs], c == 0, c == 1, [pT, wpp], [ps2])
                    self.tt("dve", gt[:, :], gt[:, :], ps2[:, :], ALU.mult, [gt, ps2], [gt])
                    self.tt("dve", xx[:, hs], xx[:, hs], gt[:, :], ALU.add, [xx, gt], [xx])
                self.act(junk[:, :], xx[:, :], AF.Square, [xx], [junk, ssd], accum=ssd[:, :])
                self.rsqrt(ssd[:, :], ssd[:, :], 1.0 / D, EPS, [ssd], [ssd])
                self.stt("dve", xx[:, :], xx[:, :], ssd[:, 0:1], gfin[:, :], ALU.mult, ALU.mult, [xx, ssd, gfin], [xx])
                P.dma(self.out[r0:r0 + 128, :], xx[:, :], reads=[xx])
            P.barrier()
    P.barrier()


K.phase_d = _build_phase_d


def _prep_alloc(self, es):
    P = self.P
    self.pp_u32 = [P.sb(es, "ub32_%d" % i, [128, 1024], F32) for i in range(2)]
    self.pp_ucv = [P.sb(es, "ucv%d" % i, [128, 8, 128], BF16) for i in range(2)]
    self.pp_vcv = [P.sb(es, "vcv%d" % i, [128, 1024], BF16) for i in range(2)]
    self.pp_v32 = [P.sb(es, "pv32_%d" % i, [128, 1024], F32) for i in range(2)]


def _prep_step(self):
    P = self.P
    c = self.prep_done
    self.prep_done += 1
    u32 = self.pp_u32[c % 2]
    P.dma(u32[:, :], self.inp["peer_u"][c * 128:(c + 1) * 128, :], writes=[u32])
    uc = self.pp_ucv[c % 2]
    for half in range(2):
        ps = self.next_ps()
        for k4 in range(4):
            kc = half * 4 + k4
            self.tr(ps[:, k4 * 128:(k4 + 1) * 128], u32[:, kc * 128:(kc + 1) * 128], self.ident[:, :],
                    [u32, self.ident], [ps])
        self.copy(self.ev_eng(), uc[:, half * 4:(half + 1) * 4, :],
                  ps[:, :].rearrange("p (k e) -> p k e", k=4), [ps], [uc])
    P.dma(self.uTs.t[:, c * 128:(c + 1) * 128].rearrange("(k p) e -> p k e", p=128), uc[:, :, :],
          reads=[uc], writes=[self.uTs])
    v32 = self.pp_v32[c % 2]
    P.dma(v32[:, :], self.inp["peer_v"][c * 128:(c + 1) * 128, :], writes=[v32])
    vc = self.pp_vcv[c % 2]
    self.copy("pool", vc[:, :], v32[:, :], [v32], [vc])
    P.dma(self.vs.t[c * 128:(c + 1) * 128, :], vc[:, :], reads=[vc], writes=[self.vs])


K.prep_alloc = _prep_alloc
K.prep_step = _prep_step


def _skt(self, stg, skT):
    P = self.P
    for hc in range(16):
        st = stg[hc % 2]
        P.dma(st[:, 0:128], self.inp["peer_sub_keys"][hc], writes=[st])
        ps = self.next_ps()
        self.tr(ps[:, 0:128], st[:, 0:128], self.ident[:, :], [st, self.ident], [ps])
        self.copy("dve", skT[:, hc, :], ps[:, 0:128], [ps], [skT])


K._skt = _skt
```

```python
import math
from contextlib import ExitStack
import numpy as np
import concourse.bass as bass
import concourse.mybir as mybir
from concourse.bass_utils import run_bass_kernel_spmd

F32 = mybir.dt.float32
BF16 = mybir.dt.bfloat16
I32 = mybir.dt.int32
AF = mybir.ActivationFunctionType
ALU = mybir.AluOpType
AX = mybir.AxisListType

D = 1024
NCOL = 4512
NG = 36
EPS = 1e-6
GN_EPS = 64e-5
NDSEM = 12


class Buf:
    __slots__ = ("name", "w", "r")

    def __init__(self, name):
        self.name = name
        self.w = None
        self.r = []


class TT:
    def __init__(self, t, buf):
        self.t = t
        self.buf = buf

    def __getitem__(self, k):
        return self.t[k]


class Prog:
    def __init__(self, nc, es):
        self.nc = nc
        self.es = es
        self.engs = ["pe", "act", "dve", "pool", "sp"]
        self.q = {e: [] for e in self.engs}
        self.cnt = {e: 0 for e in self.engs}
        self.sem = {}
        for e in ["pe", "act", "dve", "pool"]:
            self.sem[e] = es.enter_context(nc.semaphore("s_" + e))
        self.dsem = [es.enter_context(nc.semaphore("d%d" % i)) for i in range(NDSEM)]
        self.dcnt = [0] * NDSEM
        self.dnext = 0
        self.seen = {e: {} for e in self.engs}
        self.nb = 0
        self.epoch = 0
        self.semtab = {(e, 0): self.sem[e] for e in self.sem}

    def buf(self, name=None):
        self.nb += 1
        return Buf(name or "b%d" % self.nb)

    def sb(self, es, name, shape, dt):
        t = es.enter_context(self.nc.sbuf_tensor(name, list(shape), dt))
        return TT(t, self.buf(name))

    def ps(self, es, name, shape, dt=F32):
        t = es.enter_context(self.nc.psum_tensor(name, list(shape), dt))
        return TT(t, self.buf(name))

    def _semobj(self, key):
        return self.semtab[key] if isinstance(key, tuple) else self.dsem[key]

    def _need(self, eng, ev, waits):
        if ev is None:
            return
        key, val, peng = ev
        if eng == "pe" and peng == "pe":
            return
        if isinstance(key, tuple) and key[1] < self.epoch:
            return
        if self.seen[eng].get(key, 0) >= val:
            return
        if waits.get(key, 0) < val:
            waits[key] = val

    def op(self, eng, fn, reads=(), writes=(), dma=False):
        waits = {}
        for b in reads:
            b = b.buf if isinstance(b, TT) else b
            self._need(eng, b.w, waits)
        for b in writes:
            b = b.buf if isinstance(b, TT) else b
            self._need(eng, b.w, waits)
            for ev in b.r:
                self._need(eng, ev, waits)
        if dma:
            j = self.dnext
            self.dnext = (self.dnext + 1) % NDSEM
            if self.dcnt[j] > 0:
                ev = (j, 16 * self.dcnt[j], "dma")
                self._need(eng, ev, waits)
            self.dcnt[j] += 1
            ev = (j, 16 * self.dcnt[j], "dma")
            semo, inc = self.dsem[j], 16
        else:
            self.cnt[eng] += 1
            ev = ((eng, self.epoch), self.cnt[eng], eng)
            semo, inc = self.sem[eng], 1
        for k, v in waits.items():
            self.seen[eng][k] = v
        wl = [(self._semobj(k), v) for k, v in waits.items()]

        def thunk(e, wl=wl, fn=fn, semo=semo, inc=inc):
            for s, v in wl:
                e.wait_ge(s, v)
            fn(e).then_inc(semo, inc)

        self.q[eng].append(thunk)
        for b in reads:
            b = b.buf if isinstance(b, TT) else b
            b.r.append(ev)
            if len(b.r) > 64:
                mx = {}
                for (k, v, pe) in b.r:
                    if k not in mx or mx[k][1] < v:
                        mx[k] = (k, v, pe)
                b.r = list(mx.values())
        for b in writes:
            b = b.buf if isinstance(b, TT) else b
            b.w = ev
            b.r = []
        return ev

    def dma(self, out, in_, reads=(), writes=(), eng="sp", **kw):
        return self.op(eng, lambda e: e.dma_start(out=out, in_=in_, **kw), reads, writes, dma=True)

    def barrier(self):
        tot = {(e, self.epoch): self.cnt[e] for e in ["pe", "act", "dve", "pool"]}
        dt = {j: 16 * self.dcnt[j] for j in range(NDSEM)}
        for eng in self.engs:
            wl = []
            for k, v in list(tot.items()) + list(dt.items()):
                if v > 0 and self.seen[eng].get(k, 0) < v and not (isinstance(k, tuple) and k[0] == eng):
                    wl.append((self._semobj(k), v))
                    self.seen[eng][k] = v

            def thunk(e, wl=wl):
                for s, v in wl:
                    e.wait_ge(s, v)

            self.q[eng].append(thunk)
        if max(self.cnt.values()) > 6000:
            self.epoch += 1
            for e in ["pe", "act", "dve", "pool"]:
                self.sem[e] = self.es.enter_context(self.nc.semaphore("s_%s_%d" % (e, self.epoch)))
                self.semtab[(e, self.epoch)] = self.sem[e]
                self.cnt[e] = 0

    def emit(self):
        nc = self.nc
        with nc.Block() as block:
            @block.tensor
            def _(e):
                for f in self.q["pe"]:
                    f(e)

            @block.scalar
            def _(e):
                for f in self.q["act"]:
                    f(e)

            @block.vector
            def _(e):
                for f in self.q["dve"]:
                    f(e)

            @block.gpsimd
            def _(e):
                for f in self.q["pool"]:
                    f(e)

            @block.sync
            def _(e):
                for f in self.q["sp"]:
                    f(e)


def bc(ap, shape):
    return ap.to_broadcast(list(shape))


class K:
    def __init__(self, S, debug=False):
        self.S = S
        self.debug = debug
        self.nc = bass.Bass("TRN2", target_bir_lowering=False)
        self.inp = {}
        self.rr = 0
        self.split_delta = True
        self.prep_done = 0

    def din(self, name, shape, dt=F32):
        a = self.nc.dram_tensor(name, list(shape), dt, kind="ExternalInput").ap()
        self.inp[name] = a
        return a

    def dscr(self, name, shape, dt=F32):
        return TT(self.nc.dram_tensor(name, list(shape), dt, kind="Internal").ap(), Buf(name))

    def ev_eng(self):
        self.rr += 1
        return ["act", "dve"][self.rr % 2]

    def act(self, out, in_, func, reads, writes, bias=None, scale=1.0, accum=None):
        kw = {}
        if bias is not None:
            kw["bias"] = bias
        if accum is not None:
            kw["accum_out"] = accum
        return self.P.op("act", lambda e: e.activation(out=out, in_=in_, func=func, scale=scale, **kw),
                         reads, writes)

    def tt(self, eng, out, in0, in1, op, reads, writes):
        return self.P.op(eng, lambda e: e.tensor_tensor(out=out, in0=in0, in1=in1, op=op), reads, writes)

    def ts(self, eng, out, in0, s1, op0, reads, writes, s2=None, op1=None):
        if op1 is None:
            return self.P.op(eng, lambda e: e.tensor_scalar(out=out, in0=in0, scalar1=s1, scalar2=None, op0=op0),
                             reads, writes)
        return self.P.op(eng, lambda e: e.tensor_scalar(out=out, in0=in0, scalar1=s1, scalar2=s2, op0=op0, op1=op1),
                         reads, writes)

    def stt(self, eng, out, in0, scalar, in1, op0, op1, reads, writes):
        return self.P.op(eng, lambda e: e.scalar_tensor_tensor(out=out, in0=in0, scalar=scalar, in1=in1,
                                                               op0=op0, op1=op1), reads, writes)

    def copy(self, eng, out, in_, reads, writes):
        if eng == "act":
            return self.act(out, in_, AF.Copy, reads, writes)
        return self.P.op(eng, lambda e: e.tensor_copy(out=out, in_=in_), reads, writes)

    def mm(self, out, lhsT, rhs, start, stop, reads, writes):
        return self.P.op("pe", lambda e: e.matmul(out, lhsT, rhs, start=start, stop=stop), reads, writes)

    def tr(self, out, in_, ident, reads, writes):
        return self.P.op("pe", lambda e: e.transpose(out, in_, ident), reads, writes)

    def memset(self, eng, ap, val, writes):
        return self.P.op(eng, lambda e: e.memset(ap, val), (), writes)

    def rsqrt(self, out, in_, mul, add, reads, writes, eng="dve"):
        self.ts(eng, out, in_, mul, ALU.mult, reads, writes, s2=add, op1=ALU.add)
        self.act(out, out, AF.Sqrt, writes, writes)
        self.P.op("dve", lambda e: e.reciprocal(out=out, in_=out), writes, writes)

    def next_ps(self):
        self.psi = (self.psi + 1) % len(self.pspool)
        return self.pspool[self.psi]


def host_consts():
    c = {}
    c["ident"] = np.eye(128, dtype=np.float32)
    bo = np.zeros((128, 128), np.float32)
    bo[:64, :64] = 1.0
    bo[64:, 64:] = 1.0
    c["blockones"] = bo
    c["ones"] = np.ones((128, 128), np.float32)
    s = np.arange(64)[:, None]
    t = np.arange(64)[None, :]
    m = np.zeros((64, 3, 8, 64), np.float32)
    m[:, 0] = (s < t).astype(np.float32)[:, None, :]
    m[:, 1] = (s <= t).astype(np.float32)[:, None, :]
    m[:, 2] = (t < s).astype(np.float32)[:, None, :]
    c["masks"] = m.reshape(64, 3 * 8 * 64)
    invf = (10000.0 ** (-np.arange(0, 32, 2, dtype=np.float32) / 32)).astype(np.float32)
    rc = np.zeros((128, 4), np.float32)
    rc[64:80, 0] = invf
    rc[80:96, 0] = invf
    rc[64:80, 1] = -1.0
    rc[80:96, 1] = 1.0
    rc[:, 2] = math.pi / 2
    rc[:, 3] = 0.0
    c["ropec"] = rc
    return c


def _build_phase_a(self):
    P, nc, S = self.P, self.nc, self.S
    with ExitStack() as es:
        win = P.sb(es, "win", [128, 8, NCOL], BF16)
        stg = [P.sb(es, "wstg%d" % i, [128, NCOL], F32) for i in range(2)]
        gcol = P.sb(es, "gcol", [128, 8], F32)
        xt = [P.sb(es, "xt%d" % i, [128, D], F32) for i in range(2)]
        junk = P.sb(es, "junk", [128, D], F32)
        hb = [P.sb(es, "hb%d" % i, [128, D], BF16) for i in range(2)]
        ss = [P.sb(es, "ss%d" % i, [128, 1], F32) for i in range(2)]
        hT = [P.sb(es, "hT%d" % i, [128, 8, 512], BF16) for i in range(2)]
        zst = [P.sb(es, "zst%d" % i, [128, 512], F32) for i in range(4)]
        zero = P.sb(es, "zero", [128, 16], F32)
        pst = [P.ps(es, "pst%d" % i, [128, 8, 128], BF16) for i in range(2)]
        psz = [P.ps(es, "psz%d" % i, [128, 512], F32) for i in range(4)]
        w_in = self.inp["w_in"]
        P.dma(gcol[:, :], self.inp["norm_mix"].rearrange("(k p) -> p k", p=128), writes=[gcol],
              allow_slow_non_contiguous=True)
        self.memset("pool", zero[:, :], 0.0, [zero])
        P.dma(self.zT.t[0:14 * 128, 0:1].rearrange("(g p) o -> p (g o)", p=128), zero[:, 0:14],
              reads=[zero], writes=[self.zT], allow_slow_non_contiguous=True)
        for kc in range(8):
            st = stg[kc % 2]
            P.dma(st[:, :], w_in[kc * 128:(kc + 1) * 128, :], writes=[st])
            self.copy(["act", "dve", "pool"][kc % 3], win[:, kc, :], st[:, :], [st], [win])
        nsub = S // 128
        for ti in range(S // 512):
            h_T = hT[ti % 2]
            for sj in range(4):
                si = ti * 4 + sj
                x_t, h_b, s_s, p_t = xt[si % 2], hb[si % 2], ss[si % 2], pst[si % 2]
                P.dma(x_t[:, :], self.inp["x"][si * 128:(si + 1) * 128, :], writes=[x_t])
                self.act(junk[:, :], x_t[:, :], AF.Square, [x_t], [junk, s_s], accum=s_s[:, :])
                self.rsqrt(s_s[:, :], s_s[:, :], 1.0 / D, EPS, [s_s], [s_s])
                self.ts("dve", h_b[:, :], x_t[:, :], s_s[:, 0:1], ALU.mult, [x_t, s_s], [h_b])
                for kc in range(8):
                    self.tr(p_t[:, kc, :], h_b[:, kc * 128:(kc + 1) * 128], self.identb[:, :],
                            [h_b, self.identb], [p_t])
                self.tt("dve", h_T[:, :, sj * 128:(sj + 1) * 128], p_t[:, :, :],
                        bc(gcol[:, :].unsqueeze(2), [128, 8, 128]), ALU.mult, [p_t, gcol], [h_T])
            for g in range(NG):
                c0 = g * 128 if g < 19 else (2432 if g == 19 else 2464 + (g - 20) * 128)
                cw = 32 if g == 19 else 128
                pz = psz[g % 4]
                zs = zst[g % 4]
                for kc in range(8):
                    self.mm(pz[0:cw, :], win[:, kc, c0:c0 + cw], h_T[:, kc, :], kc == 0, kc == 7,
                            [win, h_T], [pz])
                if g >= 20:
                    self.act(zs[0:cw, :], pz[0:cw, :], AF.Sigmoid, [pz], [zs])
                else:
                    self.copy(self.ev_eng(), zs[0:cw, :], pz[0:cw, :], [pz], [zs])
                P.dma(self.zT.t[g * 128:g * 128 + cw, 1 + ti * 512:1 + (ti + 1) * 512], zs[0:cw, :],
                      reads=[zs], writes=[self.zT])
    P.barrier()


K.phase_a = _build_phase_a


def _build(self):
    nc, S = self.nc, self.S
    inp = self.din
    inp("x", [S, D]); inp("p", [S, 256]); inp("pos", [1, S], I32)
    inp("norm_mix", [D]); inp("w_in", [D, NCOL]); inp("rw_mu", [1792]); inp("rw_w0", [512])
    inp("rw_w2", [64, 512]); inp("rw_a0", [512]); inp("rw_a2", [64, 512]); inp("rw_g2", [128, 512])
    inp("rw_k_k", [512]); inp("rw_k_a", [512]); inp("rw_r_k", [512]); inp("rw_gn_w", [512])
    inp("rw_gn_b", [512]); inp("rw_w_o", [512, D]); inp("mla_q_norm", [384]); inp("mla_w_uq", [384, 768])
    inp("mla_kv_norm", [256]); inp("mla_w_ukv", [256, 1024]); inp("mla_w_o", [512, D]); inp("w_out", [D, D])
    inp("norm_ffn", [D]); inp("peer_w_q", [D, 2048]); inp("peer_sub_keys", [16, 128, 128])
    inp("peer_u", [16384, D]); inp("peer_v", [16384, D]); inp("norm_ple", [D]); inp("ple_w_gate", [D, D])
    inp("ple_w_proj", [256, D]); inp("norm_final", [D])
    for k, v in host_consts().items():
        inp("c_" + k, list(v.shape))
    self.out = nc.dram_tensor("out", [S, D], F32, kind="ExternalOutput").ap()
    self.zT = self.dscr("zT", [NG * 128, S + 1])
    self.ygT = self.dscr("ygT", [512, S], BF16)
    self.x1 = self.dscr("x1", [S, D])
    self.oTs = self.dscr("oTs", [512, S], BF16)
    self.wqs = self.dscr("wqs", [1024, 2048], BF16)
    self.uTs = self.dscr("uTs", [1024, 16384], BF16)
    self.vs = self.dscr("vs", [16384, 1024], BF16)
    if self.debug:
        self.dbg = {}
    with ExitStack() as es:
        self.P = P = Prog(nc, es)
        self.ident = P.sb(es, "ident", [128, 128], F32)
        self.identb = P.sb(es, "identb", [128, 128], BF16)
        self.blockones = P.sb(es, "blockones", [128, 128], F32)
        self.ones = P.sb(es, "onesf", [128, 128], F32)
        P.dma(self.ident[:, :], self.inp["c_ident"], writes=[self.ident])
        P.dma(self.blockones[:, :], self.inp["c_blockones"], writes=[self.blockones])
        P.dma(self.ones[:, :], self.inp["c_ones"], writes=[self.ones])
        self.copy("dve", self.identb[:, :], self.ident[:, :], [self.ident], [self.identb])
        P.barrier()
        self.phase_a()
        if self.debug != "a":
            self.phase_b()
        if self.debug not in ("a", "b"):
            self.phase_c()
        if self.debug not in ("a", "b", "c"):
            self.phase_d()
        if self.debug == "c":
            P.dma(self.out[:, :], self.x1.t[:, :], reads=[self.x1])
        if self.debug == "b":
            with ExitStack() as es2:
                d1 = P.sb(es2, "dbg1", [128, 4, 512], BF16)
                d2 = P.sb(es2, "dbg2", [128, 4, 512], F32)
                P.dma(d1[:, :, :], self.ygT.t[0:512, 0:512].rearrange("(g p) t -> p g t", p=128), reads=[self.ygT], writes=[d1])
                self.copy("dve", d2[:, :, :], d1[:, :, :], [d1], [d2])
                P.dma(self.out[0:512, 0:512].rearrange("(g p) t -> p g t", p=128), d2[:, :, :], reads=[d2])
                P.barrier()
        if self.debug == "a":
            P.dma(self.out[0:512, 0:512], self.zT.t[0:512, 1:513], reads=[self.zT], eng="sp")
            P.dma(self.out[0:512, 512:1024], self.zT.t[2560:3072, 1:513], reads=[self.zT], eng="sp")
        P.barrier()
        P.emit()
    return nc


K.build = _build


def make_inputs(S, b, x, p, positions, **w):
    m = {"x": np.ascontiguousarray(x[b]), "p": np.ascontiguousarray(p[0, b]),
         "pos": np.ascontiguousarray(positions[b].reshape(1, S).astype(np.int32))}
    for k, v in w.items():
        a = np.asarray(v)
        if k == "norm_final":
            m[k] = np.ascontiguousarray(a)
        elif k == "rw_r_k":
            m[k] = np.ascontiguousarray(a[0].reshape(512))
        elif k == "peer_sub_keys":
            m[k] = np.ascontiguousarray(a[0].reshape(16, 128, 128))
        else:
            m[k] = np.ascontiguousarray(a[0])
    for k, v in host_consts().items():
        m["c_" + k] = v
    return m


def kernel(x, p, positions, **w):
    x = np.asarray(x); p = np.asarray(p); positions = np.asarray(positions)
    B, S = x.shape[0], x.shape[1]
    kb = K(S)
    nc = kb.build()
    in_maps = [make_inputs(S, b, x, p, positions, **w) for b in range(B)]
    res = run_bass_kernel_spmd(nc, in_maps, core_ids=list(range(B)))
    return np.stack([r["out"] for r in res.results], axis=0).astype(np.float32)


def _col(self, es, name, src, ng):
    t = self.P.sb(es, name, [128, ng], F32)
    self.P.dma(t[:, :], self.inp[src].rearrange("(g p) -> p g", p=128), writes=[t],
               allow_slow_non_contiguous=True)
    return t


K.col = _col


def _build_phase_b(self):
    P, nc, S = self.P, self.nc, self.S
    with ExitStack() as es:
        sb = lambda n, sh, dt=F32: P.sb(es, n, sh, dt)
        mu = self.col(es, "mu", "rw_mu", 14)
        w0 = self.col(es, "w0c", "rw_w0", 4)
        a0 = self.col(es, "a0c", "rw_a0", 4)
        kkc = self.col(es, "kkc", "rw_k_k", 4)
        kac = self.col(es, "kac", "rw_k_a", 4)
        rkc = self.col(es, "rkc", "rw_r_k", 4)
        gnw = self.col(es, "gnw", "rw_gn_w", 4)
        gnb = self.col(es, "gnb", "rw_gn_b", 4)
        w2 = sb("w2", [64, 512]); P.dma(w2[:, :], self.inp["rw_w2"], writes=[w2])
        a2 = sb("a2", [128, 512]); P.dma(a2[64:128, :], self.inp["rw_a2"], writes=[a2])
        g2 = sb("g2", [128, 512]); P.dma(g2[:, :], self.inp["rw_g2"], writes=[g2])
        masks = sb("masks", [64, 3, 8, 64])
        P.dma(masks[:, :, :, :].rearrange("p a h t -> p (a h t)"), self.inp["c_masks"], writes=[masks])
        zin = sb("zin", [128, 14, 513])
        zs = sb("zs", [128, 14, 512])
        big = [sb("big%d" % i, [128, 4, 512]) for i in range(8)]
        tw = sb("tw", [64, 512]); sg = sb("sgz", [128, 512])
        gC = sb("gC", [128, 4, 8])
        Hc = sb("Hc", [128, 4, 64])
        Ht = sb("Ht", [128, 4, 64])
        ygo = sb("ygo", [128, 4, 512], BF16)
        c64 = lambda n: sb(n, [64, 8, 64])
        Np = [c64("Np0"), c64("Np1")]; Ntp = [c64("Ntp0"), c64("Ntp1")]
        AkT = c64("AkT"); ArbT = c64("ArbT"); ArkT = c64("ArkT")
        U = [c64("U0"), c64("U1")]
        Vtm = c64("Vtm"); Btm = sb("Btm", [64, 4, 128]); Ktm = sb("Ktm", [64, 4, 128])
        Ysb = c64("Ysb"); Ysq = c64("Ysq")
        st = sb("st", [64, 4, 8])
        eo = sb("eo", [128, 2])
        self.memset("dve", eo[:, :], 0.0, [eo])
        self.memset("dve", eo[0:64, 0:1], 1.0, [eo])
        self.memset("dve", eo[64:128, 1:2], 1.0, [eo])
        mk = {}
        for nm in ("b", "a", "k", "r"):
            mk[nm] = [sb("mk_%s%d" % (nm, i), [128, 4, 64]) for i in range(2)]
        self.pspool = [P.ps(es, "pb%d" % i, [128, 512], F32) for i in range(8)]
        self.psi = 0
        self.memset("dve", Hc[:, :, :], 0.0, [Hc])
        self.prep_alloc(es)
        prep_per_chunk = -(-128 // (S // 64))
        for ti in range(S // 512):
            t0 = ti * 512
            P.dma(zin[:, :, :], self.zT.t[0:1792, t0:t0 + 513].rearrange("(g p) t -> p g t", p=128),
                  reads=[self.zT], writes=[zin])
            self.tt("dve", zs[:, :, :], zin[:, :, 0:512], zin[:, :, 1:513], ALU.subtract, [zin], [zs])
            self.tt("pool", zs[:, :, :], zs[:, :, :], bc(mu[:, :].unsqueeze(2), [128, 14, 512]), ALU.mult,
                    [zs, mu], [zs])
            self.tt("dve", zs[:, :, :], zs[:, :, :], zin[:, :, 1:513], ALU.add, [zs, zin], [zs])
            r_, k_, v_ = zs[:, 0:4, :], zs[:, 4:8, :], zs[:, 8:12, :]
            A, B, C, Dd, E, Fb, G, H = big
            self.act(tw[:, :], zs[0:64, 12, :], AF.Tanh, [zs], [tw])
            for g in range(4):
                ps = self.next_ps()
                self.mm(ps[:, :], w2[:, g * 128:(g + 1) * 128], tw[:, :], True, True, [w2, tw], [ps])
                self.act(E[:, g, :], ps[:, :], AF.Sigmoid, [ps, w0], [E], bias=w0[:, g:g + 1])
            self.ts("pool", E[:, :, :], E[:, :, :], -math.exp(-0.5), ALU.mult, [E], [E])
            src = E
            pp = [A, B]
            k = 0
            for sh in (1, 2, 4, 8, 16, 32):
                dst = pp[k % 2]
                sv = src[:, :, :].rearrange("p g (c t) -> p (g c) t", t=64)
                dv = dst[:, :, :].rearrange("p g (c t) -> p (g c) t", t=64)
                self.tt("dve", dv[:, :, sh:64], sv[:, :, sh:64], sv[:, :, 0:64 - sh], ALU.add, [src], [dst])
                self.copy("pool", dv[:, :, 0:sh], sv[:, :, 0:sh], [src], [dst])
                src = dst
                k += 1
            X = src
            Y = A if X is B else B
            self.act(C[:, :, :], X[:, :, :], AF.Exp, [X], [C])
            self.act(Dd[:, :, :], X[:, :, :], AF.Exp, [X], [Dd], scale=-1.0)
            self.tt("dve", Y[:, :, :], X[:, :, :], E[:, :, :], ALU.subtract, [X, E], [Y])
            self.act(Y[:, :, :], Y[:, :, :], AF.Exp, [Y], [Y])
            self.copy("pool", gC[:, :, :], C[:, :, :].rearrange("p g (c t) -> p g c t", t=64)[:, :, :, 63],
                      [C], [gC])
            for g in range(4):
                self.ts("dve", Fb[:, g, :], k_[:, g, :], kkc[:, g:g + 1], ALU.mult, [zs, kkc], [Fb])
            self.tt("pool", G[:, :, :], Fb[:, :, :], Fb[:, :, :], ALU.mult, [Fb], [G])
            for g in range(4):
                ps = self.next_ps()
                self.mm(ps[:, :], self.blockones[:, :], G[:, g, :], True, True, [self.blockones, G], [ps])
                self.ts("dve", E[:, g, :], ps[:, :], 1e-24, ALU.add, [ps], [E])
            self.act(E[:, :, :], E[:, :, :], AF.Sqrt, [E], [E])
            P.op("dve", lambda e: e.reciprocal(out=E[:, :, :], in_=E[:, :, :]), [E], [E])
            self.tt("dve", Fb[:, :, :], Fb[:, :, :], E[:, :, :], ALU.mult, [Fb, E], [Fb])
            for g in range(4):
                ps = self.next_ps()
                self.mm(ps[:, :], a2[64:128, g * 128:(g + 1) * 128], zs[64:128, 12, :], True, True, [a2, zs], [ps])
                self.act(G[:, g, :], ps[:, :], AF.Sigmoid, [ps, a0], [G], bias=a0[:, g:g + 1])
            self.stt("dve", Y[:, :, :], Fb[:, :, :], -1.0, Y[:, :, :], ALU.mult, ALU.mult, [Fb, Y], [Y])
            self.tt("pool", H[:, :, :], Fb[:, :, :], G[:, :, :], ALU.mult, [Fb, G], [H])
            self.tt("dve", H[:, :, :], H[:, :, :], Dd[:, :, :], ALU.mult, [H, Dd], [H])
            for g in range(4):
                self.ts("dve", G[:, g, :], G[:, g, :], -1.0, ALU.add, [G, kac], [G], s2=kac[:, g:g + 1], op1=ALU.mult)
            self.stt("dve", G[:, :, :], G[:, :, :], 1.0, k_, ALU.add, ALU.mult, [G, zs], [G])
            self.tt("pool", Fb[:, :, :], r_, G[:, :, :], ALU.mult, [zs, G], [Fb])
            for g in range(4):
                self.ts("dve", Fb[:, g, :], Fb[:, g, :], rkc[:, g:g + 1], ALU.mult, [Fb, rkc], [Fb])
            for g in range(4):
                ps = self.next_ps()
                self.mm(ps[:, :], self.blockones[:, :], Fb[:, g, :], True, True, [self.blockones, Fb], [ps])
                self.tt("dve", E[:, g, :], ps[:, :], v_[:, g, :], ALU.mult, [ps, zs], [E])
            bonus = E
            self.tt("dve", Dd[:, :, :], Dd[:, :, :], G[:, :, :], ALU.mult, [Dd, G], [Dd])
            self.tt("pool", C[:, :, :], C[:, :, :], r_, ALU.mult, [C, zs], [C])
            self.act(sg[:, :], zs[:, 13, :], AF.Sigmoid, [zs], [sg])
            for g in range(4):
                ps = self.next_ps()
                self.mm(ps[:, :], g2[:, g * 128:(g + 1) * 128], sg[:, :], True, True, [g2, sg], [ps])
                self.copy("act", G[:, g, :], ps[:, :], [ps], [G])
            rt, at, bt, kt, ynT = C, Y, H, Dd, X
            for ch in range(8 if getattr(self, "stopb", 9) > 2 else 0):
                cs = slice(ch * 64, (ch + 1) * 64)
                for _ in range(prep_per_chunk):
                    if self.prep_done < 128:
                        self.prep_step()
                for (srcT, dstT, rd) in ((bt, Btm, [bt]), (kt, Ktm, [kt]), (v_, Vtm, [zs])):
                    ps = self.next_ps()
                    for g in range(4):
                        self.tr(ps[0:64, g * 128:(g + 1) * 128], srcT[:, g, cs], self.ident[:, :],
                                rd + [self.ident], [ps])
                    dv = dstT[:, :, :].rearrange("p a b -> p (a b)")
                    self.copy(self.ev_eng(), dv, ps[0:64, :], [ps], [dstT])
                for nm, srcT in (("b", bt), ("a", at), ("k", kt), ("r", rt)):
                    for e2 in range(2):
                        self.ts(["dve", "pool"][e2], mk[nm][e2][:, :, :], srcT[:, :, cs], eo[:, e2:e2 + 1], ALU.mult,
                                [srcT, eo], [mk[nm][e2]])

                def amat(dst, lT, lrd, rT, rrd, mi):
                    ps = self.next_ps()
                    for h in range(8):
                        self.mm(ps[0:64, h * 64:(h + 1) * 64], mk[lT][h % 2][:, h // 2, :], rT[:, h // 2, cs],
                                True, True, [mk[lT][h % 2], rrd], [ps])
                    self.tt("dve", dst[:, :, :], ps[0:64, :].rearrange("p (h t) -> p h t", h=8),
                            masks[:, mi, :, :], ALU.mult, [ps, masks], [dst])
                if getattr(self, "stopb", 9) == 3:
                    continue
                amat(Np[0], "b", bt, at, at, 0)
                amat(Ntp[0], "a", at, bt, bt, 2)
                amat(AkT, "k", kt, at, at, 0)
                amat(ArbT, "b", bt, rt, rt, 1)
                amat(ArkT, "k", kt, rt, rt, 1)
                if getattr(self, "stopb", 9) == 4:
                    continue
                ps = self.next_ps()
                for h in range(8):
                    o = ps[0:64, h * 64:(h + 1) * 64]
                    self.mm(o, mk["a"][h % 2][:, h // 2, :], Hc[:, h // 2, :], True, False, [mk["a"][h % 2], Hc], [ps])
                    self.mm(o, AkT[:, h, :], Vtm[:, h, :], False, True, [AkT, Vtm], [ps])
                self.copy("act", U[0][:, :, :], ps[0:64, :].rearrange("p (h t) -> p h t", h=8), [ps], [U[0]])
                cur = 0
                for lvl in range(6):
                    Nn, Nt = Np[lvl % 2], Ntp[lvl % 2]
                    ps = self.next_ps()
                    for h in range(8):
                        self.mm(ps[0:64, h * 64:(h + 1) * 64], Nn[:, h, :], U[cur][:, h, :], True, True,
                                [Nn, U[cur]], [ps])
                    self.tt("dve", U[1 - cur][:, :, :], U[cur][:, :, :],
                            ps[0:64, :].rearrange("p (h t) -> p h t", h=8), ALU.add, [ps, U[cur]], [U[1 - cur]])
                    cur = 1 - cur
                    if lvl < 5:
                        N2, Nt2 = Np[(lvl + 1) % 2], Ntp[(lvl + 1) % 2]
                        ps = self.next_ps()
                        for h in range(8):
                            self.mm(ps[0:64, h * 64:(h + 1) * 64], Nt[:, h, :], Nn[:, h, :], True, True,
                                    [Nt, Nn], [ps])
                        ps2 = None
                        if lvl < 4:
                            ps2 = self.next_ps()
                            for h in range(8):
                                self.mm(ps2[0:64, h * 64:(h + 1) * 64], Nn[:, h, :], Nt[:, h, :], True, True,
                                        [Nt, Nn], [ps2])
                        self.copy("act", N2[:, :, :], ps[0:64, :].rearrange("p (h t) -> p h t", h=8), [ps], [N2])
                        if ps2 is not None:
                            self.copy("pool" if False else "dve", Nt2[:, :, :],
                                      ps2[0:64, :].rearrange("p (h t) -> p h t", h=8), [ps2], [Nt2])
                Uf = U[cur]
                if getattr(self, "stopb", 9) == 5:
                    continue
                psy = self.next_ps()
                for h in range(8):
                    o = psy[0:64, h * 64:(h + 1) * 64]
                    self.mm(o, mk["r"][h % 2][:, h // 2, :], Hc[:, h // 2, :], True, False, [mk["r"][h % 2], Hc], [psy])
                    self.mm(o, ArbT[:, h, :], Uf[:, h, :], False, False, [ArbT, Uf], [psy])
                    self.mm(o, ArkT[:, h, :], Vtm[:, h, :], False, True, [ArkT, Vtm], [psy])
                psh = self.next_ps()
                for h in range(8):
                    o = psh[:, h * 64:(h + 1) * 64]
                    self.mm(o, Btm[:, h // 2, :], Uf[:, h, :], True, False, [Btm, Uf], [psh])
                    self.mm(o, Ktm[:, h // 2, :], Vtm[:, h, :], False, True, [Ktm, Vtm], [psh])
                phv = psh[:, :].rearrange("p (g e v) -> p g e v", g=4, e=2)
                for e2 in range(2):
                    rs = slice(e2 * 64, e2 * 64 + 64)
                    self.tt("dve", Ht[rs, :, :], Hc[rs, :, :], phv[rs, :, e2, :], ALU.add, [Hc, psh], [Ht])
                    self.tt("dve", Hc[rs, :, :], Ht[rs, :, :], bc(gC[rs, :, ch:ch + 1], [64, 4, 64]), ALU.mult,
                            [Ht, gC], [Hc])
                if getattr(self, "stopb", 9) == 6:
                    continue
                yv = psy[0:64, :].rearrange("p (h t) -> p h t", h=8)
                self.copy("act", Ysb[:, :, :], yv, [psy], [Ysb])
                self.act(Ysq[:, :, :], yv, AF.Square, [psy], [Ysq])
                P.op("dve", lambda e: e.reduce_sum(out=st[:, 0, :], in_=Ysb[:, :, :], axis=AX.X), [Ysb], [st])
                P.op("dve", lambda e: e.reduce_sum(out=st[:, 1, :], in_=Ysq[:, :, :], axis=AX.X), [Ysq], [st])
                self.ts("dve", st[:, 0, :], st[:, 0, :], 1.0 / 64, ALU.mult, [st], [st])
                self.tt("dve", st[:, 2, :], st[:, 0, :], st[:, 0, :], ALU.mult, [st], [st])
                self.stt("dve", st[:, 1, :], st[:, 1, :], 1.0 / 64, st[:, 2, :], ALU.mult, ALU.subtract, [st], [st])
                self.ts("dve", st[:, 1, :], st[:, 1, :], GN_EPS, ALU.add, [st], [st])
                self.act(st[:, 1, :], st[:, 1, :], AF.Sqrt, [st], [st])
                P.op("dve", lambda e: e.reciprocal(out=st[:, 1, :], in_=st[:, 1, :]), [st], [st])
                self.tt("dve", Ysb[:, :, :], Ysb[:, :, :], bc(st[:, 0, :].unsqueeze(2), [64, 8, 64]), ALU.subtract,
                        [Ysb, st], [Ysb])
                self.tt("dve", Ysb[:, :, :], Ysb[:, :, :], bc(st[:, 1, :].unsqueeze(2), [64, 8, 64]), ALU.mult,
                        [Ysb, st], [Ysb])
                ps = self.next_ps()
                yf = Ysb[:, :, :].rearrange("p h v -> p (h v)")
                for g in range(4):
                    self.tr(ps[:, g * 64:(g + 1) * 64], yf[:, g * 128:(g + 1) * 128], self.ident[0:64, 0:64],
                            [Ysb, self.ident], [ps])
                for g in range(4):
                    self.act(ynT[:, g, cs], ps[:, g * 64:(g + 1) * 64], AF.Identity, [ps, gnw, gnb], [ynT],
                             bias=gnb[:, g:g + 1], scale=gnw[:, g:g + 1])
            self.tt("dve", ynT[:, :, :], ynT[:, :, :], bonus[:, :, :], ALU.add, [ynT, bonus], [ynT])
            self.tt("dve", ygo[:, :, :], ynT[:, :, :], G[:, :, :], ALU.mult, [ynT, G], [ygo])
            P.dma(self.ygT.t[:, t0:t0 + 512].rearrange("(g p) t -> p g t", p=128), ygo[:, :, :],
                  reads=[ygo], writes=[self.ygT])
            P.barrier()
    P.barrier()


K.phase_b = _build_phase_b


def _loadw(self, es, name, src_ap, rows, cols, stg):
    P = self.P
    nk = rows // 128
    t = name if isinstance(name, TT) else P.sb(es, name, [128, nk, cols], BF16)
    for kc in range(nk):
        st = stg[kc % 2]
        P.dma(st[:, 0:cols], src_ap[kc * 128:(kc + 1) * 128, :], writes=[st])
        self.copy(["act", "dve", "pool"][kc % 3], t[:, kc, :], st[:, 0:cols], [st], [t])
    return t


K.loadw = _loadw

MAGIC = 12582912.0
TWO_PI_1 = 6.28125
TWO_PI_2 = 2.0 * math.pi - 6.28125


def _build_phase_c(self):
    P, nc, S = self.P, self.nc, self.S
    NT = S // 512
    with ExitStack() as es:
        sb = lambda n, sh, dt=F32: P.sb(es, n, sh, dt)
        stg = [sb("cstg%d" % i, [128, 1024]) for i in range(2)]
        wuq = self.loadw(es, "wuq", self.inp["mla_w_uq"], 384, 768, stg)
        wkv = self.loadw(es, "wkv", self.inp["mla_w_ukv"], 256, 1024, stg)
        wrot = sb("wrot", [128, 3, 8, 96], BF16)
        self.memset("pool", wrot[:, :, :, :], 0.0, [wrot])
        wq4 = wuq[:, :, :].rearrange("p c (h e) -> p c h e", h=8)
        self.copy("dve", wrot[:, :, :, 64:80], wq4[:, :, :, 80:96], [wuq], [wrot])
        self.copy("dve", wrot[:, :, :, 80:96], wq4[:, :, :, 64:80], [wuq], [wrot])
        wk4 = wkv[:, :, :].rearrange("p c (h e) -> p c h e", h=8)
        wv = sb("wv", [128, 2, 8, 64], BF16)
        self.copy("dve", wv[:, :, :, :], wk4[:, :, :, 64:128], [wkv], [wv])
        qn = self.col(es, "qn", "mla_q_norm", 3)
        kvn = self.col(es, "kvn", "mla_kv_norm", 2)
        ropec = sb("ropec", [128, 4]); P.dma(ropec[:, :], self.inp["c_ropec"], writes=[ropec])
        onesb = sb("onesb", [128, 128], BF16)
        self.copy("dve", onesb[:, :], self.ones[:, :], [self.ones], [onesb])
        Kf = [sb("Kf%d" % h, [96, S], BF16) for h in range(8)]
        Vtm = sb("Vtm_a", [128, S // 128, 512], BF16)
        Qf = sb("Qf", [96, 8, 512], BF16)
        zc = sb("zc", [128, 3, 512]); zsq = sb("zsq", [128, 3, 512]); rstd = sb("rstd_c", [128, 512])
        cn = sb("cn", [128, 3, 512], BF16)
        posi = sb("posi", [96, 512], I32); ang = sb("ang", [96, 512]); kq = sb("kq", [96, 512])
        Ct = sb("Ct", [96, 512]); St = sb("St", [96, 512])
        kr = sb("kr", [96, 512]); krot = sb("krot", [96, 512]); krb = sb("krb", [96, 512], BF16)
        qa = sb("qa", [96, 512]); qb = sb("qb", [96, 512])
        Pt = [sb("Pt%d" % i, [128, 512], BF16) for i in range(3)]
        rec = sb("rec", [128, 512])
        oT = sb("oT", [128, 4, 512], BF16)
        self.pspool = [P.ps(es, "pc%d" % i, [128, 512], F32) for i in range(4)]
        self.psi = 0
        psO = [P.ps(es, "pO%d" % i, [128, 512], F32) for i in range(2)]
        psS = [P.ps(es, "pS%d" % i, [128, 512], F32) for i in range(2)]
        scale = 1.0 / math.sqrt(96.0)

        def rmsn(groups, ng, gcolt, nfeat):
            self.tt("pool", zsq[:, 0:ng, :], zc[:, 0:ng, :], zc[:, 0:ng, :], ALU.mult, [zc], [zsq])
            ps = self.next_ps()
            for c in range(ng):
                self.mm(ps[:, :], self.ones[:, :], zsq[:, c, :], c == 0, c == ng - 1, [self.ones, zsq], [ps])
            self.rsqrt(rstd[:, :], ps[:, :], 1.0 / nfeat, EPS, [ps], [rstd])
            for c in range(ng):
                self.stt("dve", cn[:, c, :], zc[:, c, :], gcolt[:, c:c + 1], rstd[:, :], ALU.mult, ALU.mult,
                         [zc, gcolt, rstd], [cn])

        for T in range(NT):
            t0 = T * 512
            P.dma(posi[:, :], self.inp["pos"][0:1, t0:t0 + 512].partition_broadcast(96), writes=[posi])
            self.copy("dve", ang[:, :], posi[:, :], [posi], [ang])
            self.ts("dve", ang[:, :], ang[:, :], ropec[0:96, 0:1], ALU.mult, [ang, ropec], [ang])
            self.ts("dve", kq[:, :], ang[:, :], 1.0 / (2 * math.pi), ALU.mult, [ang], [kq], s2=MAGIC, op1=ALU.add)
            self.ts("dve", kq[:, :], kq[:, :], -MAGIC, ALU.add, [kq], [kq])
            self.stt("dve", ang[:, :], kq[:, :], -TWO_PI_1, ang[:, :], ALU.mult, ALU.add, [kq, ang], [ang])
            self.stt("dve", ang[:, :], kq[:, :], -TWO_PI_2, ang[:, :], ALU.mult, ALU.add, [kq, ang], [ang])
            self.ts("dve", ang[:, :], ang[:, :], 3.14159, ALU.min, [ang], [ang], s2=-3.14159, op1=ALU.max)
            self.act(St[:, :], ang[:, :], AF.Sin, [ang], [St])
            self.ts("dve", St[:, :], St[:, :], ropec[0:96, 1:2], ALU.mult, [St, ropec], [St])
            self.act(kq[:, :], ang[:, :], AF.Abs, [ang], [kq])
            self.act(Ct[:, :], kq[:, :], AF.Sin, [kq, ropec], [Ct], bias=ropec[0:96, 2:3], scale=-1.0)
            P.dma(zc[:, 0:2, :], self.zT.t[17 * 128:19 * 128, 1 + t0:1 + t0 + 512].rearrange("(g p) t -> p g t", p=128),
                  reads=[self.zT], writes=[zc])
            rmsn(2, 2, kvn, 256.0)
            for h in range(8):
                ps = self.next_ps()
                for c in range(2):
                    self.mm(ps[0:64, :], wk4[:, c, h, 0:64], cn[:, c, :], c == 0, c == 1, [wkv, cn], [ps])
                self.copy(self.ev_eng(), Kf[h][0:64, t0:t0 + 512], ps[0:64, :], [ps], [Kf[h]])
            rz = 19 * 128
            P.dma(kr[64:96, :], self.zT.t[rz:rz + 32, 1 + t0:1 + t0 + 512], reads=[self.zT], writes=[kr])
            P.dma(krot[64:80, :], self.zT.t[rz + 16:rz + 32, 1 + t0:1 + t0 + 512], reads=[self.zT], writes=[krot])
            P.dma(krot[80:96, :], self.zT.t[rz:rz + 16, 1 + t0:1 + t0 + 512], reads=[self.zT], writes=[krot])
            self.tt("dve", kr[64:96, :], kr[64:96, :], Ct[64:96, :], ALU.mult, [kr, Ct], [kr])
            self.tt("dve", krot[64:96, :], krot[64:96, :], St[64:96, :], ALU.mult, [krot, St], [krot])
            self.tt("dve", krb[64:96, :], kr[64:96, :], krot[64:96, :], ALU.add, [kr, krot], [krb])
            for h in range(8):
                self.copy(["dve", "pool"][h % 2], Kf[h][64:96, t0:t0 + 512], krb[64:96, :], [krb], [Kf[h]])
            for b4 in range(4):
                ps = self.next_ps()
                for c in range(2):
                    self.mm(ps[:, :], cn[:, c, b4 * 128:(b4 + 1) * 128], wv[:, c, :, :].rearrange("p h e -> p (h e)"),
                            c == 0, c == 1, [cn, wv], [ps])
                self.copy(self.ev_eng(), Vtm[:, T * 4 + b4, :], ps[:, :], [ps], [Vtm])
            P.dma(zc[:, 0:3, :], self.zT.t[14 * 128:17 * 128, 1 + t0:1 + t0 + 512].rearrange("(g p) t -> p g t", p=128),
                  reads=[self.zT], writes=[zc])
            rmsn(3, 3, qn, 384.0)
            for h in range(8):
                ps = self.next_ps()
                ps2 = self.next_ps()
                for c in range(3):
                    self.mm(ps[0:96, :], wuq[:, c, h * 96:(h + 1) * 96], cn[:, c, :], c == 0, c == 2, [wuq, cn], [ps])
                for c in range(3):
                    self.mm(ps2[0:96, :], wrot[:, c, h, :], cn[:, c, :], c == 0, c == 2, [wrot, cn], [ps2])
                self.tt("dve", qa[:, :], ps[0:96, :], Ct[:, :], ALU.mult, [ps, Ct], [qa])
                self.tt("dve", qb[:, :], ps2[0:96, :], St[:, :], ALU.mult, [ps2, St], [qb])
                self.tt("pool", Qf[:, h, :], qa[:, :], qb[:, :], ALU.add, [qa, qb], [Qf])
            pi = 0
            for h in range(8):
                pO, pS = psO[h % 2], psS[h % 2]
                nkb = 4 * T + 4
                for kb in range(nkb):
                    nq0 = max(0, kb - 4 * T)
                    cl = slice(nq0 * 128, 512)
                    ps = self.next_ps()
                    self.mm(ps[:, cl], Kf[h][0:96, kb * 128:(kb + 1) * 128], Qf[0:96, h, cl], True, True,
                            [Kf[h], Qf], [ps])
                    pt = Pt[pi % 3]
                    pi += 1
                    self.act(pt[:, cl], ps[:, cl], AF.Exp, [ps], [pt], scale=scale)
                    if kb >= 4 * T:
                        self.memset("pool", pt[64:128, nq0 * 128:nq0 * 128 + 64], 0.0, [pt])
                    hp2 = (h // 2) * 128
                    self.mm(pO[:, cl], Vtm[:, kb, hp2:hp2 + 128], pt[:, cl], kb == 0, kb == nkb - 1, [Vtm, pt], [pO])
                    self.mm(pS[:, cl], onesb[:, :], pt[:, cl], kb == 0, kb == nkb - 1, [onesb, pt], [pS])
                P.op("dve", lambda e, o=rec[:, :], i=pS[:, :]: e.reciprocal(out=o, in_=i), [pS], [rec])
                rs = slice((h % 2) * 64, (h % 2) * 64 + 64)
                self.tt("dve", oT[rs, h // 2, :], pO[rs, :], rec[rs, :], ALU.mult, [pO, rec], [oT])
            P.dma(self.oTs.t[:, t0:t0 + 512].rearrange("(g p) t -> p g t", p=128), oT[:, :, :],
                  reads=[oT], writes=[self.oTs])
            P.barrier()
    P.barrier()
    with ExitStack() as es:
        sb = lambda n, sh, dt=F32: P.sb(es, n, sh, dt)
        wo_m = P.sb(es, "wo_m", [128, 4, 1024], BF16)
        wo_r = P.sb(es, "wo_r", [128, 4, 1024], BF16)
        w_o = P.sb(es, "w_o", [128, 8, 1024], BF16)
        stg = [sb("c2stg%d" % i, [128, 1024]) for i in range(2)]
        self.loadw(es, wo_m, self.inp["mla_w_o"], 512, 1024, stg)
        self.loadw(es, wo_r, self.inp["rw_w_o"], 512, 1024, stg)
        self.loadw(es, w_o, self.inp["w_out"], 1024, 1024, stg)
        oT = sb("oT2", [128, 4, 512], BF16)
        ygl = sb("ygl", [128, 4, 512], BF16)
        gA = [sb("gA%d" % i, [128, 512]) for i in range(2)]
        gB = [sb("gB%d" % i, [128, 512]) for i in range(2)]
        ta = sb("ta", [128, 512]); tb = sb("tb", [128, 512])
        mix = sb("mix", [128, 8, 512], BF16)
        xl = [sb("xl%d" % i, [128, 1024]) for i in range(2)]
        self.pspool = [P.ps(es, "pc2_%d" % i, [128, 512], F32) for i in range(6)]
        self.psi = 0
        for T in range(NT):
            t0 = T * 512
            P.dma(oT[:, :, :], self.oTs.t[:, t0:t0 + 512].rearrange("(g p) t -> p g t", p=128),
                  reads=[self.oTs], writes=[oT])
            P.dma(ygl[:, :, :], self.ygT.t[:, t0:t0 + 512].rearrange("(g p) t -> p g t", p=128),
                  reads=[self.ygT], writes=[ygl])
            for j in range(8):
                ga, gb = gA[j % 2], gB[j % 2]
                P.dma(ga[:, :], self.zT.t[(20 + j) * 128:(21 + j) * 128, 1 + t0:1 + t0 + 512], reads=[self.zT], writes=[ga])
                P.dma(gb[:, :], self.zT.t[(28 + j) * 128:(29 + j) * 128, 1 + t0:1 + t0 + 512], reads=[self.zT], writes=[gb])
                ps = self.next_ps()
                ps2 = self.next_ps()
                for c in range(4):
                    self.mm(ps[:, :], wo_r[:, c, j * 128:(j + 1) * 128], ygl[:, c, :], c == 0, c == 3, [wo_r, ygl], [ps])
                for c in range(4):
                    self.mm(ps2[:, :], wo_m[:, c, j * 128:(j + 1) * 128], oT[:, c, :], c == 0, c == 3, [wo_m, oT], [ps2])
                self.tt("dve", ta[:, :], ps[:, :], ga[:, :], ALU.mult, [ps, ga], [ta])
                self.tt("dve", tb[:, :], ps2[:, :], gb[:, :], ALU.mult, [ps2, gb], [tb])
                self.tt("pool", mix[:, j, :], ta[:, :], tb[:, :], ALU.add, [ta, tb], [mix])
            for b4 in range(4):
                x_l = xl[b4 % 2]
                r0 = t0 + b4 * 128
                P.dma(x_l[:, :], self.inp["x"][r0:r0 + 128, :], writes=[x_l])
                for hf in range(2):
                    ps = self.next_ps()
                    for j in range(8):
                        self.mm(ps[:, :], mix[:, j, b4 * 128:(b4 + 1) * 128], w_o[:, j, hf * 512:(hf + 1) * 512],
                                j == 0, j == 7, [mix, w_o], [ps])
                    self.tt("dve", x_l[:, hf * 512:(hf + 1) * 512], x_l[:, hf * 512:(hf + 1) * 512], ps[:, :], ALU.add,
                            [x_l, ps], [x_l])
                P.dma(self.x1.t[r0:r0 + 128, :], x_l[:, :], reads=[x_l], writes=[self.x1])
    P.barrier()


K.phase_c = _build_phase_c


def _build_phase_d(self):
    P, nc, S = self.P, self.nc, self.S
    NEG = -1e30
    with ExitStack() as es:
        sb = lambda n, sh, dt=F32: P.sb(es, n, sh, dt)
        stg = [sb("dstg%d" % i, [128, 1024]) for i in range(2)]
        self.pspool = [P.ps(es, "pdp%d" % i, [128, 512], F32) for i in range(4)]
        self.psi = 0
        if self.prep_done < 128:
            self.prep_alloc(es)
            while self.prep_done < 128:
                self.prep_step()
    P.barrier()
    with ExitStack() as es:
        sb = lambda n, sh, dt=F32: P.sb(es, n, sh, dt)
        self.pspool = [P.ps(es, "pd%d" % i, [128, 512], F32) for i in range(3)]
        self.psi = 0
        psOut = [[P.ps(es, "po%d%d" % (a, b), [128, 512], F32) for b in range(2)] for a in range(2)]
        uni = P.sb(es, "uni", [128, 8192], F32)
        wq = TT(uni[:, :].bitcast(BF16).rearrange("p (k c) -> p k c", k=8), P.buf("wqv"))
        wpg = P.sb(es, "wpg", [128, 8, 1024], BF16)
        wpp = P.sb(es, "wpp", [128, 2, 1024], BF16)
        skT = sb("skT", [128, 16, 128], BF16)
        with ExitStack() as es3:
            stg = [P.sb(es3, "dstg2_%d" % i, [128, 2048], F32) for i in range(2)]
            self.loadw(es, wq, self.inp["peer_w_q"], 1024, 2048, stg)
            P.dma(self.wqs.t[:, :].rearrange("(k p) c -> p k c", p=128), wq[:, :, :], reads=[wq], writes=[self.wqs])
            self.loadw(es, wpg, self.inp["ple_w_gate"], 1024, 1024, stg)
            self.loadw(es, wpp, self.inp["ple_w_proj"], 256, 1024, stg)
            self._skt(stg, skT)
            P.barrier()
        for hc in range(0):
            st = stg[hc % 2]
            P.dma(st[:, 0:128], self.inp["peer_sub_keys"][hc], writes=[st])
            ps = self.next_ps()
            self.tr(ps[:, 0:128], st[:, 0:128], self.ident[:, :], [st, self.ident], [ps])
            self.copy("dve", skT[:, hc, :], ps[:, 0:128], [ps], [skT])
        gf = self.col(es, "gf", "norm_ffn", 8)
        gp = self.col(es, "gp", "norm_ple", 8)
        gfin = sb("gfin", [128, 1024])
        P.dma(gfin[:, :], self.inp["norm_final"].rearrange("(o d) -> o d", o=1).partition_broadcast(128), writes=[gfin])
        x1t = [sb("x1t%d" % i, [128, 1024]) for i in range(2)]
        hb = sb("hb_d", [128, 1024], BF16)
        junk = hb
        ssd = sb("ssd", [128, 1])
        xnT = sb("xnT", [128, 8, 256], BF16)
        xn2T = sb("xn2T", [128, 8, 128], BF16)
        qT = sb("qT", [128, 16, 256], BF16)
        sS = [sb("sS%d" % i, [128, 16, 128]) for i in range(2)]
        s1pp = [sb("s1pp%d" % i, [128, 8, 128]) for i in range(2)]
        bE = [sb("bE%d" % i, [128, 8]) for i in range(2)]
        a16 = sb("a16", [128, 16]); b16 = sb("b16", [128, 16]); c16 = sb("c16", [128, 16]); e16 = sb("e16", [128, 16])
        tmpk = sb("tmpk", [128, 128]); cand = sb("cand", [128, 256]); cand2 = sb("cand2", [128, 256])
        sc = sb("scal", [128, 8])
        ubk = [sb("ubk%d" % i, [128, 8, 512], BF16) for i in range(2)]
        vbk = [sb("vbk%d" % i, [128, 4, 1024], BF16) for i in range(3)]
        dl = [TT(uni[:, i * 2048:(i + 1) * 2048].rearrange("p (a b c) -> p a b c", a=4, b=4), P.buf("dl%d" % i))
              for i in range(4)]
        Ee = [sb("Ee%d" % i, [128, 4, 4, 128], BF16) for i in range(4)]
        Gh = [sb("Gh%d" % i, [128, 4, 4, 128], BF16) for i in range(4)]
        dg = [sb("dg%d" % i, [128, 8, 128], BF16) for i in range(2)]
        wE = sb("wE", [128, 8])
        gel = [sb("gel%d" % i, [128, 512], BF16) for i in range(3)]
        Pm = [sb("Pm%d" % i, [128, 512], BF16) for i in range(3)]
        PTs = [sb("PTs%d" % i, [128, 4, 128], BF16) for i in range(3)]
        pst = P.ps(es, "pstd", [128, 8, 128], BF16)
        pl = sb("pl", [128, 256]); plb = sb("plb", [128, 256], BF16); pT = sb("pT", [128, 2, 128], BF16)
        gt = sb("gt", [128, 512])
        ib = self.identb

        def norm_T(xsrc, gcolt, dstT, csl):
            self.act(junk[:, :], xsrc[:, :], AF.Square, [xsrc], [junk, ssd], accum=ssd[:, :])
            self.rsqrt(ssd[:, :], ssd[:, :], 1.0 / D, EPS, [ssd], [ssd])
            self.ts("dve", hb[:, :], xsrc[:, :], ssd[:, 0:1], ALU.mult, [xsrc, ssd], [hb])
            for kc in range(8):
                self.tr(pst[:, kc, :], hb[:, kc * 128:(kc + 1) * 128], ib[:, :], [hb, ib], [pst])
            self.tt("dve", dstT[:, :, csl], pst[:, :, :], bc(gcolt[:, :].unsqueeze(2), [128, 8, 128]), ALU.mult,
                    [pst, gcolt], [dstT])

        def top16(dst, src, srcT, tmp, tmpT):
            P.op("dve", lambda e, o=dst[:, 0:8], i=src: e.max(out=o, in_=i), [dst, srcT], [dst])
            P.op("dve", lambda e, o=tmp, r=dst[:, 0:8], i=src: e.match_replace(out=o, in_to_replace=r, in_values=i,
                                                                                imm_value=NEG), [dst, srcT], [tmpT])
            P.op("dve", lambda e, o=dst[:, 8:16], i=tmp: e.max(out=o, in_=i), [tmpT], [dst])

        gi = 0
        mk = 0
        for tl in range(S // 256):
            P.dma(wq[:, :, :], self.wqs.t[:, :].rearrange("(k p) c -> p k c", p=128), reads=[self.wqs], writes=[wq])
            for sub in range(2):
                r0 = tl * 256 + sub * 128
                P.dma(x1t[sub][:, :], self.x1.t[r0:r0 + 128, :], reads=[self.x1], writes=[x1t[sub]])
                norm_T(x1t[sub], gf, xnT, slice(sub * 128, (sub + 1) * 128))
            for hc in range(16):
                ps = self.next_ps()
                for kc in range(8):
                    self.mm(ps[:, 0:256], wq[:, kc, hc * 128:(hc + 1) * 128], xnT[:, kc, :], kc == 0, kc == 7,
                            [wq, xnT], [ps])
                self.copy(self.ev_eng(), qT[:, hc, :], ps[:, 0:256], [ps], [qT])
            for sub in range(2):
                for g4 in range(4):
                    ps = self.next_ps()
                    for i4 in range(4):
                        hc = g4 * 4 + i4
                        self.mm(ps[:, i4 * 128:(i4 + 1) * 128], qT[:, hc, sub * 128:(sub + 1) * 128], skT[:, hc, :],
                                True, True, [qT, skT], [ps])
                    self.copy(self.ev_eng(), sS[sub][:, g4 * 4:(g4 + 1) * 4, :],
                              ps[:, :].rearrange("p (a n) -> p a n", a=4), [ps], [sS[sub]])
                for h in range(8):
                    s1 = sS[sub][:, 2 * h, :]
                    s2 = sS[sub][:, 2 * h + 1, :]
                    top16(a16, s1, sS[sub], tmpk[:, :], tmpk)
                    top16(b16, s2, sS[sub], tmpk[:, :], tmpk)
                    self.tt("dve", cand[:, :].rearrange("p (a b) -> p a b", a=16),
                            bc(a16[:, :].unsqueeze(2), [128, 16, 16]), bc(b16[:, :].unsqueeze(1), [128, 16, 16]),
                            ALU.add, [a16, b16], [cand])
                    top16(c16, cand[:, :], cand, cand2[:, :], cand2)
                    P.op("dve", lambda e, o=sc[:, 0:1], i=c16[:, :]: e.tensor_reduce(out=o, in_=i, axis=AX.X, op=ALU.min),
                         [c16], [sc])
                    P.op("dve", lambda e, o=sc[:, 1:2], i=c16[:, :]: e.tensor_reduce(out=o, in_=i, axis=AX.X, op=ALU.max),
                         [c16], [sc])
                    self.ts("dve", sc[:, 0:1], sc[:, 0:1], -2e-6, ALU.add, [sc], [sc])
                    self.ts("dve", sc[:, 2:3], sc[:, 1:2], -1.0, ALU.mult, [sc], [sc])
                    self.act(e16[:, :], c16[:, :], AF.Exp, [c16, sc], [e16, sc], bias=sc[:, 2:3], accum=sc[:, 3:4])
                    self.act(sc[:, 4:5], sc[:, 3:4], AF.Ln, [sc], [sc])
                    self.tt("dve", sc[:, 5:6], sc[:, 0:1], sc[:, 1:2], ALU.subtract, [sc], [sc])
                    self.tt("dve", bE[sub][:, h:h + 1], sc[:, 5:6], sc[:, 4:5], ALU.subtract, [sc], [bE[sub]])
                    self.ts("dve", s1pp[sub][:, h, :], s1, sc[:, 0:1], ALU.subtract, [sS[sub], sc], [s1pp[sub]])
                self.act(wE[:, :], bE[sub][:, :], AF.Exp, [bE[sub]], [wE])
                for h in range(8):
                    self.ts(["dve", "pool"][h % 2], dg[sub][:, h, :], ib[:, :], wE[:, h:h + 1], ALU.mult, [ib, wE], [dg[sub]])
            NK = 64
            P.barrier()

            def ldblk(blk):
                ub, vb = ubk[blk % 2], vbk[blk % 3]
                P.dma(ub[:, :, :], self.uTs.t[:, blk * 512:(blk + 1) * 512].rearrange("(k p) e -> p k e", p=128),
                      reads=[self.uTs], writes=[ub])
                P.dma(vb[:, :, :], self.vs.t[blk * 512:(blk + 1) * 512, :].rearrange("(c p) d -> p c d", p=128),
                      reads=[self.vs], writes=[vb])

            ldblk(0)

            def S1(k):
                blk, sub = k // 2, k % 2
                ub = ubk[blk % 2]
                if sub == 0 and blk + 1 < 32:
                    ldblk(blk + 1)
                psA = self.next_ps()
                for kc in range(8):
                    self.mm(psA[:, :], xnT[:, kc, sub * 128:(sub + 1) * 128], ub[:, kc, :], kc == 0, kc == 7,
                            [xnT, ub], [psA])
                for hg in range(2):
                    d_, e_, g_ = dl[(k % 2) * 2 + hg], Ee[(k % 2) * 2 + hg], Gh[(k % 2) * 2 + hg]
                    self.stt("dve", g_[:, :, :, :], d_[:, :, :, :], 0.0, e_[:, :, :, :], ALU.is_ge, ALU.mult,
                             [d_, e_], [g_])
                self.psA_k[k % 2] = psA

            def S0(k):
                blk, sub = k // 2, k % 2
                for hg in range(2):
                    d_ = dl[(k % 2) * 2 + hg]
                    s2v = sS[sub][:, :, :].rearrange("p (h c) n -> p h c n", c=2)[:, hg * 4:(hg + 1) * 4, 1, :]
                    self.tt("dve", d_[:, :, :, :], bc(s2v.unsqueeze(2), [128, 4, 4, 128]),
                            bc(s1pp[sub][:, hg * 4:(hg + 1) * 4, blk * 4:(blk + 1) * 4].unsqueeze(3), [128, 4, 4, 128]),
                            ALU.add, [sS[sub], s1pp[sub]], [d_])
                for hg in range(2):
                    d_, e_ = dl[(k % 2) * 2 + hg], Ee[(k % 2) * 2 + hg]
                    self.act(e_[:, :, :, :], d_[:, :, :, :], AF.Exp, [d_], [e_])

            def S1b(k):
                psA = self.psA_k[k % 2]
                self.act(gel[k % 3][:, :], psA[:, :], AF.Gelu, [psA], [gel[k % 3]])

            def S2(k):
                sub = k % 2
                psM = self.next_ps()
                for hg in range(2):
                    g_ = Gh[(k % 2) * 2 + hg]
                    for h4 in range(4):
                        h = hg * 4 + h4
                        self.mm(psM[:, :], dg[sub][:, h, :], g_[:, h4, :, :].rearrange("p a b -> p (a b)"),
                                h == 0, h == 7, [dg[sub], g_], [psM])
                self.psM_k[k % 2] = psM

            def S2b(k):
                psM = self.psM_k[k % 2]
                self.tt("dve", Pm[k % 3][:, :], gel[k % 3][:, :], psM[:, :], ALU.mult, [gel[k % 3], psM], [Pm[k % 3]])

            def S3(k):
                pm = Pm[k % 3]
                for ec in range(4):
                    self.tr(pst[:, ec, :], pm[:, ec * 128:(ec + 1) * 128], ib[:, :], [pm, ib], [pst])
                self.copy("act", PTs[k % 3][:, :, :], pst[:, 0:4, :], [pst], [PTs[k % 3]])

            def S4(k):
                blk, sub = k // 2, k % 2
                vb = vbk[blk % 3]
                pts = PTs[k % 3]
                for hf in range(2):
                    for ec in range(4):
                        self.mm(psOut[sub][hf][:, :], pts[:, ec, :], vb[:, ec, hf * 512:(hf + 1) * 512],
                                blk == 0 and ec == 0, blk == 31 and ec == 3, [pts, vb], [psOut[sub][hf]])

            self.psA_k = [None, None]
            self.psM_k = [None, None]
            S0(0)
            for r in range(NK + 3):
                if 0 <= r - 3 < NK:
                    S4(r - 3)
                if r + 1 < NK:
                    S0(r + 1)
                if r < NK:
                    S1(r)
                if 0 <= r - 1 < NK:
                    S2(r - 1)
                if 0 <= r - 2 < NK:
                    S3(r - 2)
                if r < NK:
                    S1b(r)
                if 0 <= r - 1 < NK:
                    S2b(r - 1)
            for sub in range(2):
                r0 = tl * 256 + sub * 128
                xx = x1t[sub]
                for hf in range(2):
                    self.tt("dve", xx[:, hf * 512:(hf + 1) * 512], xx[:, hf * 512:(hf + 1) * 512], psOut[sub][hf][:, :],
                            ALU.add, [xx, psOut[sub][hf]], [xx])
                norm_T(xx, gp, xn2T, slice(0, 128))
                P.dma(pl[:, :], self.inp["p"][r0:r0 + 128, :], writes=[pl])
                self.copy("pool", plb[:, :], pl[:, :], [pl], [plb])
                for c in range(2):
                    self.tr(pst[:, c, :], plb[:, c * 128:(c + 1) * 128], ib[:, :], [plb, ib], [pst])
                self.copy("act", pT[:, :, :], pst[:, 0:2, :], [pst], [pT])
                for hf in range(2):
                    hs = slice(hf * 512, (hf + 1) * 512)
                    ps = self.next_ps()
                    for kc in range(8):
                        self.mm(ps[:, :], xn2T[:, kc, :], wpg[:, kc, hs], kc == 0, kc == 7, [xn2T, wpg], [ps])
                    self.act(gt[:, :], ps[:, :], AF.Sigmoid, [ps], [gt])
                    ps2 = self.next_ps()
                    for c in range(2):
                        self.mm(ps2[:, :], pT[:, c, :], wpp[:, c, hs], c == 0, c == 1, [pT, wpp], [ps2])
                    self.tt("dve", gt[:, :], gt[:, :], ps2[:, :], ALU.mult, [gt, ps2], [gt])
                    self.tt("dve", xx[:, hs], xx[:, hs], gt[:, :], ALU.add, [xx, gt], [xx])
                self.act(junk[:, :], xx[:, :], AF.Square, [xx], [junk, ssd], accum=ssd[:, :])
                self.rsqrt(ssd[:, :], ssd[:, :], 1.0 / D, EPS, [ssd], [ssd])
                self.stt("dve", xx[:, :], xx[:, :], ssd[:, 0:1], gfin[:, :], ALU.mult, ALU.mult, [xx, ssd, gfin], [xx])
                P.dma(self.out[r0:r0 + 128, :], xx[:, :], reads=[xx])
            P.barrier()
    P.barrier()


K.phase_d = _build_phase_d


def _prep_alloc(self, es):
    P = self.P
    self.pp_u32 = [P.sb(es, "ub32_%d" % i, [128, 1024], F32) for i in range(2)]
    self.pp_ucv = [P.sb(es, "ucv%d" % i, [128, 8, 128], BF16) for i in range(2)]
    self.pp_vcv = [P.sb(es, "vcv%d" % i, [128, 1024], BF16) for i in range(2)]
    self.pp_v32 = [P.sb(es, "pv32_%d" % i, [128, 1024], F32) for i in range(2)]


def _prep_step(self):
    P = self.P
    c = self.prep_done
    self.prep_done += 1
    u32 = self.pp_u32[c % 2]
    P.dma(u32[:, :], self.inp["peer_u"][c * 128:(c + 1) * 128, :], writes=[u32])
    uc = self.pp_ucv[c % 2]
    for half in range(2):
        ps = self.next_ps()
        for k4 in range(4):
            kc = half * 4 + k4
            self.tr(ps[:, k4 * 128:(k4 + 1) * 128], u32[:, kc * 128:(kc + 1) * 128], self.ident[:, :],
                    [u32, self.ident], [ps])
        self.copy(self.ev_eng(), uc[:, half * 4:(half + 1) * 4, :],
                  ps[:, :].rearrange("p (k e) -> p k e", k=4), [ps], [uc])
    P.dma(self.uTs.t[:, c * 128:(c + 1) * 128].rearrange("(k p) e -> p k e", p=128), uc[:, :, :],
          reads=[uc], writes=[self.uTs])
    v32 = self.pp_v32[c % 2]
    P.dma(v32[:, :], self.inp["peer_v"][c * 128:(c + 1) * 128, :], writes=[v32])
    vc = self.pp_vcv[c % 2]
    self.copy("pool", vc[:, :], v32[:, :], [v32], [vc])
    P.dma(self.vs.t[c * 128:(c + 1) * 128, :], vc[:, :], reads=[vc], writes=[self.vs])


K.prep_alloc = _prep_alloc
K.prep_step = _prep_step


def _skt(self, stg, skT):
    P = self.P
    for hc in range(16):
        st = stg[hc % 2]
        P.dma(st[:, 0:128], self.inp["peer_sub_keys"][hc], writes=[st])
        ps = self.next_ps()
        self.tr(ps[:, 0:128], st[:, 0:128], self.ident[:, :], [st, self.ident], [ps])
        self.copy("dve", skT[:, hc, :], ps[:, 0:128], [ps], [skT])


K._skt = _skt
```

```python
import math
from contextlib import ExitStack
import numpy as np
import concourse.bass as bass
import concourse.mybir as mybir
from concourse.bass_utils import run_bass_kernel_spmd

F32 = mybir.dt.float32
BF16 = mybir.dt.bfloat16
I32 = mybir.dt.int32
AF = mybir.ActivationFunctionType
ALU = mybir.AluOpType
AX = mybir.AxisListType

D = 1024
NCOL = 4512
NG = 36
EPS = 1e-6
GN_EPS = 64e-5
NDSEM = 12


class Buf:
    __slots__ = ("name", "w", "r")

    def __init__(self, name):
        self.name = name
        self.w = None
        self.r = []


class TT:
    def __init__(self, t, buf):
        self.t = t
        self.buf = buf

    def __getitem__(self, k):
        return self.t[k]


class Prog:
    def __init__(self, nc, es):
        self.nc = nc
        self.es = es
        self.engs = ["pe", "act", "dve", "pool", "sp"]
        self.q = {e: [] for e in self.engs}
        self.cnt = {e: 0 for e in self.engs}
        self.sem = {}
        for e in ["pe", "act", "dve", "pool"]:
            self.sem[e] = es.enter_context(nc.semaphore("s_" + e))
        self.dsem = [es.enter_context(nc.semaphore("d%d" % i)) for i in range(NDSEM)]
        self.dcnt = [0] * NDSEM
        self.dnext = 0
        self.seen = {e: {} for e in self.engs}
        self.nb = 0
        self.epoch = 0
        self.semtab = {(e, 0): self.sem[e] for e in self.sem}

    def buf(self, name=None):
        self.nb += 1
        return Buf(name or "b%d" % self.nb)

    def sb(self, es, name, shape, dt):
        t = es.enter_context(self.nc.sbuf_tensor(name, list(shape), dt))
        return TT(t, self.buf(name))

    def ps(self, es, name, shape, dt=F32):
        t = es.enter_context(self.nc.psum_tensor(name, list(shape), dt))
        return TT(t, self.buf(name))

    def _semobj(self, key):
        return self.semtab[key] if isinstance(key, tuple) else self.dsem[key]

    def _need(self, eng, ev, waits):
        if ev is None:
            return
        key, val, peng = ev
        if eng == "pe" and peng == "pe":
            return
        if isinstance(key, tuple) and key[1] < self.epoch:
            return
        if self.seen[eng].get(key, 0) >= val:
            return
        if waits.get(key, 0) < val:
            waits[key] = val

    def op(self, eng, fn, reads=(), writes=(), dma=False):
        waits = {}
        for b in reads:
            b = b.buf if isinstance(b, TT) else b
            self._need(eng, b.w, waits)
        for b in writes:
            b = b.buf if isinstance(b, TT) else b
            self._need(eng, b.w, waits)
            for ev in b.r:
                self._need(eng, ev, waits)
        if dma:
            j = self.dnext
            self.dnext = (self.dnext + 1) % NDSEM
            if self.dcnt[j] > 0:
                ev = (j, 16 * self.dcnt[j], "dma")
                self._need(eng, ev, waits)
            self.dcnt[j] += 1
            ev = (j, 16 * self.dcnt[j], "dma")
            semo, inc = self.dsem[j], 16
        else:
            self.cnt[eng] += 1
            ev = ((eng, self.epoch), self.cnt[eng], eng)
            semo, inc = self.sem[eng], 1
        for k, v in waits.items():
            self.seen[eng][k] = v
        wl = [(self._semobj(k), v) for k, v in waits.items()]

        def thunk(e, wl=wl, fn=fn, semo=semo, inc=inc):
            for s, v in wl:
                e.wait_ge(s, v)
            fn(e).then_inc(semo, inc)

        self.q[eng].append(thunk)
        for b in reads:
            b = b.buf if isinstance(b, TT) else b
            b.r.append(ev)
            if len(b.r) > 64:
                mx = {}
                for (k, v, pe) in b.r:
                    if k not in mx or mx[k][1] < v:
                        mx[k] = (k, v, pe)
                b.r = list(mx.values())
        for b in writes:
            b = b.buf if isinstance(b, TT) else b
            b.w = ev
            b.r = []
        return ev

    def dma(self, out, in_, reads=(), writes=(), eng="sp", **kw):
        return self.op(eng, lambda e: e.dma_start(out=out, in_=in_, **kw), reads, writes, dma=True)

    def barrier(self):
        tot = {(e, self.epoch): self.cnt[e] for e in ["pe", "act", "dve", "pool"]}
        dt = {j: 16 * self.dcnt[j] for j in range(NDSEM)}
        for eng in self.engs:
            wl = []
            for k, v in list(tot.items()) + list(dt.items()):
                if v > 0 and self.seen[eng].get(k, 0) < v and not (isinstance(k, tuple) and k[0] == eng):
                    wl.append((self._semobj(k), v))
                    self.seen[eng][k] = v

            def thunk(e, wl=wl):
                for s, v in wl:
                    e.wait_ge(s, v)

            self.q[eng].append(thunk)
        if max(self.cnt.values()) > 6000:
            self.epoch += 1
            for e in ["pe", "act", "dve", "pool"]:
                self.sem[e] = self.es.enter_context(self.nc.semaphore("s_%s_%d" % (e, self.epoch)))
                self.semtab[(e, self.epoch)] = self.sem[e]
                self.cnt[e] = 0

    def emit(self):
        nc = self.nc
        with nc.Block() as block:
            @block.tensor
            def _(e):
                for f in self.q["pe"]:
                    f(e)

            @block.scalar
            def _(e):
                for f in self.q["act"]:
                    f(e)

            @block.vector
            def _(e):
                for f in self.q["dve"]:
                    f(e)

            @block.gpsimd
            def _(e):
                for f in self.q["pool"]:
                    f(e)

            @block.sync
            def _(e):
                for f in self.q["sp"]:
                    f(e)


def bc(ap, shape):
    return ap.to_broadcast(list(shape))


class K:
    def __init__(self, S, debug=False):
        self.S = S
        self.debug = debug
        self.nc = bass.Bass("TRN2", target_bir_lowering=False)
        self.inp = {}
        self.rr = 0
        self.split_delta = True
        self.prep_done = 0

    def din(self, name, shape, dt=F32):
        a = self.nc.dram_tensor(name, list(shape), dt, kind="ExternalInput").ap()
        self.inp[name] = a
        return a

    def dscr(self, name, shape, dt=F32):
        return TT(self.nc.dram_tensor(name, list(shape), dt, kind="Internal").ap(), Buf(name))

    def ev_eng(self):
        self.rr += 1
        return ["act", "dve"][self.rr % 2]

    def act(self, out, in_, func, reads, writes, bias=None, scale=1.0, accum=None):
        kw = {}
        if bias is not None:
            kw["bias"] = bias
        if accum is not None:
            kw["accum_out"] = accum
        return self.P.op("act", lambda e: e.activation(out=out, in_=in_, func=func, scale=scale, **kw),
                         reads, writes)

    def tt(self, eng, out, in0, in1, op, reads, writes):
        return self.P.op(eng, lambda e: e.tensor_tensor(out=out, in0=in0, in1=in1, op=op), reads, writes)

    def ts(self, eng, out, in0, s1, op0, reads, writes, s2=None, op1=None):
        if op1 is None:
            return self.P.op(eng, lambda e: e.tensor_scalar(out=out, in0=in0, scalar1=s1, scalar2=None, op0=op0),
                             reads, writes)
        return self.P.op(eng, lambda e: e.tensor_scalar(out=out, in0=in0, scalar1=s1, scalar2=s2, op0=op0, op1=op1),
                         reads, writes)

    def stt(self, eng, out, in0, scalar, in1, op0, op1, reads, writes):
        return self.P.op(eng, lambda e: e.scalar_tensor_tensor(out=out, in0=in0, scalar=scalar, in1=in1,
                                                               op0=op0, op1=op1), reads, writes)

    def copy(self, eng, out, in_, reads, writes):
        if eng == "act":
            return self.act(out, in_, AF.Copy, reads, writes)
        return self.P.op(eng, lambda e: e.tensor_copy(out=out, in_=in_), reads, writes)

    def mm(self, out, lhsT, rhs, start, stop, reads, writes):
        return self.P.op("pe", lambda e: e.matmul(out, lhsT, rhs, start=start, stop=stop), reads, writes)

    def tr(self, out, in_, ident, reads, writes):
        return self.P.op("pe", lambda e: e.transpose(out, in_, ident), reads, writes)

    def memset(self, eng, ap, val, writes):
        return self.P.op(eng, lambda e: e.memset(ap, val), (), writes)

    def rsqrt(self, out, in_, mul, add, reads, writes, eng="dve"):
        self.ts(eng, out, in_, mul, ALU.mult, reads, writes, s2=add, op1=ALU.add)
        self.act(out, out, AF.Sqrt, writes, writes)
        self.P.op("dve", lambda e: e.reciprocal(out=out, in_=out), writes, writes)

    def next_ps(self):
        self.psi = (self.psi + 1) % len(self.pspool)
        return self.pspool[self.psi]


def host_consts():
    c = {}
    c["ident"] = np.eye(128, dtype=np.float32)
    bo = np.zeros((128, 128), np.float32)
    bo[:64, :64] = 1.0
    bo[64:, 64:] = 1.0
    c["blockones"] = bo
    c["ones"] = np.ones((128, 128), np.float32)
    s = np.arange(64)[:, None]
    t = np.arange(64)[None, :]
    m = np.zeros((64, 3, 8, 64), np.float32)
    m[:, 0] = (s < t).astype(np.float32)[:, None, :]
    m[:, 1] = (s <= t).astype(np.float32)[:, None, :]
    m[:, 2] = (t < s).astype(np.float32)[:, None, :]
    c["masks"] = m.reshape(64, 3 * 8 * 64)
    invf = (10000.0 ** (-np.arange(0, 32, 2, dtype=np.float32) / 32)).astype(np.float32)
    rc = np.zeros((128, 4), np.float32)
    rc[64:80, 0] = invf
    rc[80:96, 0] = invf
    rc[64:80, 1] = -1.0
    rc[80:96, 1] = 1.0
    rc[:, 2] = math.pi / 2
    rc[:, 3] = 0.0
    c["ropec"] = rc
    return c


def _build_phase_a(self):
    P, nc, S = self.P, self.nc, self.S
    with ExitStack() as es:
        win = P.sb(es, "win", [128, 8, NCOL], BF16)
        stg = [P.sb(es, "wstg%d" % i, [128, NCOL], F32) for i in range(2)]
        gcol = P.sb(es, "gcol", [128, 8], F32)
        xt = [P.sb(es, "xt%d" % i, [128, D], F32) for i in range(2)]
        junk = P.sb(es, "junk", [128, D], F32)
        hb = [P.sb(es, "hb%d" % i, [128, D], BF16) for i in range(2)]
        ss = [P.sb(es, "ss%d" % i, [128, 1], F32) for i in range(2)]
        hT = [P.sb(es, "hT%d" % i, [128, 8, 512], BF16) for i in range(2)]
        zst = [P.sb(es, "zst%d" % i, [128, 512], F32) for i in range(4)]
        zero = P.sb(es, "zero", [128, 16], F32)
        pst = [P.ps(es, "pst%d" % i, [128, 8, 128], BF16) for i in range(2)]
        psz = [P.ps(es, "psz%d" % i, [128, 512], F32) for i in range(4)]
        w_in = self.inp["w_in"]
        P.dma(gcol[:, :], self.inp["norm_mix"].rearrange("(k p) -> p k", p=128), writes=[gcol],
              allow_slow_non_contiguous=True)
        self.memset("pool", zero[:, :], 0.0, [zero])
        P.dma(self.zT.t[0:14 * 128, 0:1].rearrange("(g p) o -> p (g o)", p=128), zero[:, 0:14],
              reads=[zero], writes=[self.zT], allow_slow_non_contiguous=True)
        for kc in range(8):
            st = stg[kc % 2]
            P.dma(st[:, :], w_in[kc * 128:(kc + 1) * 128, :], writes=[st])
            self.copy(["act", "dve", "pool"][kc % 3], win[:, kc, :], st[:, :], [st], [win])
        nsub = S // 128
        for ti in range(S // 512):
            h_T = hT[ti % 2]
            for sj in range(4):
                si = ti * 4 + sj
                x_t, h_b, s_s, p_t = xt[si % 2], hb[si % 2], ss[si % 2], pst[si % 2]
                P.dma(x_t[:, :], self.inp["x"][si * 128:(si + 1) * 128, :], writes=[x_t])
                self.act(junk[:, :], x_t[:, :], AF.Square, [x_t], [junk, s_s], accum=s_s[:, :])
                self.rsqrt(s_s[:, :], s_s[:, :], 1.0 / D, EPS, [s_s], [s_s])
                self.ts("dve", h_b[:, :], x_t[:, :], s_s[:, 0:1], ALU.mult, [x_t, s_s], [h_b])
                for kc in range(8):
                    self.tr(p_t[:, kc, :], h_b[:, kc * 128:(kc + 1) * 128], self.identb[:, :],
                            [h_b, self.identb], [p_t])
                self.tt("dve", h_T[:, :, sj * 128:(sj + 1) * 128], p_t[:, :, :],
                        bc(gcol[:, :].unsqueeze(2), [128, 8, 128]), ALU.mult, [p_t, gcol], [h_T])
            for g in range(NG):
                c0 = g * 128 if g < 19 else (2432 if g == 19 else 2464 + (g - 20) * 128)
                cw = 32 if g == 19 else 128
                pz = psz[g % 4]
                zs = zst[g % 4]
                for kc in range(8):
                    self.mm(pz[0:cw, :], win[:, kc, c0:c0 + cw], h_T[:, kc, :], kc == 0, kc == 7,
                            [win, h_T], [pz])
                if g >= 20:
                    self.act(zs[0:cw, :], pz[0:cw, :], AF.Sigmoid, [pz], [zs])
                else:
                    self.copy(self.ev_eng(), zs[0:cw, :], pz[0:cw, :], [pz], [zs])
                P.dma(self.zT.t[g * 128:g * 128 + cw, 1 + ti * 512:1 + (ti + 1) * 512], zs[0:cw, :],
                      reads=[zs], writes=[self.zT])
    P.barrier()


K.phase_a = _build_phase_a


def _build(self):
    nc, S = self.nc, self.S
    inp = self.din
    inp("x", [S, D]); inp("p", [S, 256]); inp("pos", [1, S], I32)
    inp("norm_mix", [D]); inp("w_in", [D, NCOL]); inp("rw_mu", [1792]); inp("rw_w0", [512])
    inp("rw_w2", [64, 512]); inp("rw_a0", [512]); inp("rw_a2", [64, 512]); inp("rw_g2", [128, 512])
    inp("rw_k_k", [512]); inp("rw_k_a", [512]); inp("rw_r_k", [512]); inp("rw_gn_w", [512])
    inp("rw_gn_b", [512]); inp("rw_w_o", [512, D]); inp("mla_q_norm", [384]); inp("mla_w_uq", [384, 768])
    inp("mla_kv_norm", [256]); inp("mla_w_ukv", [256, 1024]); inp("mla_w_o", [512, D]); inp("w_out", [D, D])
    inp("norm_ffn", [D]); inp("peer_w_q", [D, 2048]); inp("peer_sub_keys", [16, 128, 128])
    inp("peer_u", [16384, D]); inp("peer_v", [16384, D]); inp("norm_ple", [D]); inp("ple_w_gate", [D, D])
    inp("ple_w_proj", [256, D]); inp("norm_final", [D])
    for k, v in host_consts().items():
        inp("c_" + k, list(v.shape))
    self.out = nc.dram_tensor("out", [S, D], F32, kind="ExternalOutput").ap()
    self.zT = self.dscr("zT", [NG * 128, S + 1])
    self.ygT = self.dscr("ygT", [512, S], BF16)
    self.x1 = self.dscr("x1", [S, D])
    self.oTs = self.dscr("oTs", [512, S], BF16)
    self.wqs = self.dscr("wqs", [1024, 2048], BF16)
    self.uTs = self.dscr("uTs", [1024, 16384], BF16)
    self.vs = self.dscr("vs", [16384, 1024], BF16)
    if self.debug:
        self.dbg = {}
    with ExitStack() as es:
        self.P = P = Prog(nc, es)
        self.ident = P.sb(es, "ident", [128, 128], F32)
        self.identb = P.sb(es, "identb", [128, 128], BF16)
        self.blockones = P.sb(es, "blockones", [128, 128], F32)
        self.ones = P.sb(es, "onesf", [128, 128], F32)
        P.dma(self.ident[:, :], self.inp["c_ident"], writes=[self.ident])
        P.dma(self.blockones[:, :], self.inp["c_blockones"], writes=[self.blockones])
        P.dma(self.ones[:, :], self.inp["c_ones"], writes=[self.ones])
        self.copy("dve", self.identb[:, :], self.ident[:, :], [self.ident], [self.identb])
        P.barrier()
        self.phase_a()
        if self.debug != "a":
            self.phase_b()
        if self.debug not in ("a", "b"):
            self.phase_c()
        if self.debug not in ("a", "b", "c"):
            self.phase_d()
        if self.debug == "c":
            P.dma(self.out[:, :], self.x1.t[:, :], reads=[self.x1])
        if self.debug == "b":
            with ExitStack() as es2:
                d1 = P.sb(es2, "dbg1", [128, 4, 512], BF16)
                d2 = P.sb(es2, "dbg2", [128, 4, 512], F32)
                P.dma(d1[:, :, :], self.ygT.t[0:512, 0:512].rearrange("(g p) t -> p g t", p=128), reads=[self.ygT], writes=[d1])
                self.copy("dve", d2[:, :, :], d1[:, :, :], [d1], [d2])
                P.dma(self.out[0:512, 0:512].rearrange("(g p) t -> p g t", p=128), d2[:, :, :], reads=[d2])
                P.barrier()
        if self.debug == "a":
            P.dma(self.out[0:512, 0:512], self.zT.t[0:512, 1:513], reads=[self.zT], eng="sp")
            P.dma(self.out[0:512, 512:1024], self.zT.t[2560:3072, 1:513], reads=[self.zT], eng="sp")
        P.barrier()
        P.emit()
    return nc


K.build = _build


def make_inputs(S, b, x, p, positions, **w):
    m = {"x": np.ascontiguousarray(x[b]), "p": np.ascontiguousarray(p[0, b]),
         "pos": np.ascontiguousarray(positions[b].reshape(1, S).astype(np.int32))}
    for k, v in w.items():
        a = np.asarray(v)
        if k == "norm_final":
            m[k] = np.ascontiguousarray(a)
        elif k == "rw_r_k":
            m[k] = np.ascontiguousarray(a[0].reshape(512))
        elif k == "peer_sub_keys":
            m[k] = np.ascontiguousarray(a[0].reshape(16, 128, 128))
        else:
            m[k] = np.ascontiguousarray(a[0])
    for k, v in host_consts().items():
        m["c_" + k] = v
    return m


def kernel(x, p, positions, **w):
    x = np.asarray(x); p = np.asarray(p); positions = np.asarray(positions)
    B, S = x.shape[0], x.shape[1]
    kb = K(S)
    nc = kb.build()
    in_maps = [make_inputs(S, b, x, p, positions, **w) for b in range(B)]
    res = run_bass_kernel_spmd(nc, in_maps, core_ids=list(range(B)))
    return np.stack([r["out"] for r in res.results], axis=0).astype(np.float32)


def _col(self, es, name, src, ng):
    t = self.P.sb(es, name, [128, ng], F32)
    self.P.dma(t[:, :], self.inp[src].rearrange("(g p) -> p g", p=128), writes=[t],
               allow_slow_non_contiguous=True)
    return t


K.col = _col


def _build_phase_b(self):
    P, nc, S = self.P, self.nc, self.S
    with ExitStack() as es:
        sb = lambda n, sh, dt=F32: P.sb(es, n, sh, dt)
        mu = self.col(es, "mu", "rw_mu", 14)
        w0 = self.col(es, "w0c", "rw_w0", 4)
        a0 = self.col(es, "a0c", "rw_a0", 4)
        kkc = self.col(es, "kkc", "rw_k_k", 4)
        kac = self.col(es, "kac", "rw_k_a", 4)
        rkc = self.col(es, "rkc", "rw_r_k", 4)
        gnw = self.col(es, "gnw", "rw_gn_w", 4)
        gnb = self.col(es, "gnb", "rw_gn_b", 4)
        w2 = sb("w2", [64, 512]); P.dma(w2[:, :], self.inp["rw_w2"], writes=[w2])
        a2 = sb("a2", [128, 512]); P.dma(a2[64:128, :], self.inp["rw_a2"], writes=[a2])
        g2 = sb("g2", [128, 512]); P.dma(g2[:, :], self.inp["rw_g2"], writes=[g2])
        masks = sb("masks", [64, 3, 8, 64])
        P.dma(masks[:, :, :, :].rearrange("p a h t -> p (a h t)"), self.inp["c_masks"], writes=[masks])
        zin = sb("zin", [128, 14, 513])
        zs = sb("zs", [128, 14, 512])
        big = [sb("big%d" % i, [128, 4, 512]) for i in range(8)]
        tw = sb("tw", [64, 512]); sg = sb("sgz", [128, 512])
        gC = sb("gC", [128, 4, 8])
        Hc = sb("Hc", [128, 4, 64])
        Ht = sb("Ht", [128, 4, 64])
        ygo = sb("ygo", [128, 4, 512], BF16)
        c64 = lambda n: sb(n, [64, 8, 64])
        Np = [c64("Np0"), c64("Np1")]; Ntp = [c64("Ntp0"), c64("Ntp1")]
        AkT = c64("AkT"); ArbT = c64("ArbT"); ArkT = c64("ArkT")
        U = [c64("U0"), c64("U1")]
        Vtm = c64("Vtm"); Btm = sb("Btm", [64, 4, 128]); Ktm = sb("Ktm", [64, 4, 128])
        Ysb = c64("Ysb"); Ysq = c64("Ysq")
        st = sb("st", [64, 4, 8])
        eo = sb("eo", [128, 2])
        self.memset("dve", eo[:, :], 0.0, [eo])
        self.memset("dve", eo[0:64, 0:1], 1.0, [eo])
        self.memset("dve", eo[64:128, 1:2], 1.0, [eo])
        mk = {}
        for nm in ("b", "a", "k", "r"):
            mk[nm] = [sb("mk_%s%d" % (nm, i), [128, 4, 64]) for i in range(2)]
        self.pspool = [P.ps(es, "pb%d" % i, [128, 512], F32) for i in range(8)]
        self.psi = 0
        self.memset("dve", Hc[:, :, :], 0.0, [Hc])
        self.prep_alloc(es)
        prep_per_chunk = -(-128 // (S // 64))
        for ti in range(S // 512):
            t0 = ti * 512
            P.dma(zin[:, :, :], self.zT.t[0:1792, t0:t0 + 513].rearrange("(g p) t -> p g t", p=128),
                  reads=[self.zT], writes=[zin])
            self.tt("dve", zs[:, :, :], zin[:, :, 0:512], zin[:, :, 1:513], ALU.subtract, [zin], [zs])
            self.tt("pool", zs[:, :, :], zs[:, :, :], bc(mu[:, :].unsqueeze(2), [128, 14, 512]), ALU.mult,
                    [zs, mu], [zs])
            self.tt("dve", zs[:, :, :], zs[:, :, :], zin[:, :, 1:513], ALU.add, [zs, zin], [zs])
            r_, k_, v_ = zs[:, 0:4, :], zs[:, 4:8, :], zs[:, 8:12, :]
            A, B, C, Dd, E, Fb, G, H = big
            self.act(tw[:, :], zs[0:64, 12, :], AF.Tanh, [zs], [tw])
            for g in range(4):
                ps = self.next_ps()
                self.mm(ps[:, :], w2[:, g * 128:(g + 1) * 128], tw[:, :], True, True, [w2, tw], [ps])
                self.act(E[:, g, :], ps[:, :], AF.Sigmoid, [ps, w0], [E], bias=w0[:, g:g + 1])
            self.ts("pool", E[:, :, :], E[:, :, :], -math.exp(-0.5), ALU.mult, [E], [E])
            src = E
            pp = [A, B]
            k = 0
            for sh in (1, 2, 4, 8, 16, 32):
                dst = pp[k % 2]
                sv = src[:, :, :].rearrange("p g (c t) -> p (g c) t", t=64)
                dv = dst[:, :, :].rearrange("p g (c t) -> p (g c) t", t=64)
                self.tt("dve", dv[:, :, sh:64], sv[:, :, sh:64], sv[:, :, 0:64 - sh], ALU.add, [src], [dst])
                self.copy("pool", dv[:, :, 0:sh], sv[:, :, 0:sh], [src], [dst])
                src = dst
                k += 1
            X = src
            Y = A if X is B else B
            self.act(C[:, :, :], X[:, :, :], AF.Exp, [X], [C])
            self.act(Dd[:, :, :], X[:, :, :], AF.Exp, [X], [Dd], scale=-1.0)
            self.tt("dve", Y[:, :, :], X[:, :, :], E[:, :, :], ALU.subtract, [X, E], [Y])
            self.act(Y[:, :, :], Y[:, :, :], AF.Exp, [Y], [Y])
            self.copy("pool", gC[:, :, :], C[:, :, :].rearrange("p g (c t) -> p g c t", t=64)[:, :, :, 63],
                      [C], [gC])
            for g in range(4):
                self.ts("dve", Fb[:, g, :], k_[:, g, :], kkc[:, g:g + 1], ALU.mult, [zs, kkc], [Fb])
            self.tt("pool", G[:, :, :], Fb[:, :, :], Fb[:, :, :], ALU.mult, [Fb], [G])
            for g in range(4):
                ps = self.next_ps()
                self.mm(ps[:, :], self.blockones[:, :], G[:, g, :], True, True, [self.blockones, G], [ps])
                self.ts("dve", E[:, g, :], ps[:, :], 1e-24, ALU.add, [ps], [E])
            self.act(E[:, :, :], E[:, :, :], AF.Sqrt, [E], [E])
            P.op("dve", lambda e: e.reciprocal(out=E[:, :, :], in_=E[:, :, :]), [E], [E])
            self.tt("dve", Fb[:, :, :], Fb[:, :, :], E[:, :, :], ALU.mult, [Fb, E], [Fb])
            for g in range(4):
                ps = self.next_ps()
                self.mm(ps[:, :], a2[64:128, g * 128:(g + 1) * 128], zs[64:128, 12, :], True, True, [a2, zs], [ps])
                self.act(G[:, g, :], ps[:, :], AF.Sigmoid, [ps, a0], [G], bias=a0[:, g:g + 1])
            self.stt("dve", Y[:, :, :], Fb[:, :, :], -1.0, Y[:, :, :], ALU.mult, ALU.mult, [Fb, Y], [Y])
            self.tt("pool", H[:, :, :], Fb[:, :, :], G[:, :, :], ALU.mult, [Fb, G], [H])
            self.tt("dve", H[:, :, :], H[:, :, :], Dd[:, :, :], ALU.mult, [H, Dd], [H])
            for g in range(4):
                self.ts("dve", G[:, g, :], G[:, g, :], -1.0, ALU.add, [G, kac], [G], s2=kac[:, g:g + 1], op1=ALU.mult)
            self.stt("dve", G[:, :, :], G[:, :, :], 1.0, k_, ALU.add, ALU.mult, [G, zs], [G])
            self.tt("pool", Fb[:, :, :], r_, G[:, :, :], ALU.mult, [zs, G], [Fb])
            for g in range(4):
                self.ts("dve", Fb[:, g, :], Fb[:, g, :], rkc[:, g:g + 1], ALU.mult, [Fb, rkc], [Fb])
            for g in range(4):
                ps = self.next_ps()
                self.mm(ps[:, :], self.blockones[:, :], Fb[:, g, :], True, True, [self.blockones, Fb], [ps])
                self.tt("dve", E[:, g, :], ps[:, :], v_[:, g, :], ALU.mult, [ps, zs], [E])
            bonus = E
            self.tt("dve", Dd[:, :, :], Dd[:, :, :], G[:, :, :], ALU.mult, [Dd, G], [Dd])
            self.tt("pool", C[:, :, :], C[:, :, :], r_, ALU.mult, [C, zs], [C])
            self.act(sg[:, :], zs[:, 13, :], AF.Sigmoid, [zs], [sg])
            for g in range(4):
                ps = self.next_ps()
                self.mm(ps[:, :], g2[:, g * 128:(g + 1) * 128], sg[:, :], True, True, [g2, sg], [ps])
                self.copy("act", G[:, g, :], ps[:, :], [ps], [G])
            rt, at, bt, kt, ynT = C, Y, H, Dd, X
            for ch in range(8 if getattr(self, "stopb", 9) > 2 else 0):
                cs = slice(ch * 64, (ch + 1) * 64)
                for _ in range(prep_per_chunk):
                    if self.prep_done < 128:
                        self.prep_step()
                for (srcT, dstT, rd) in ((bt, Btm, [bt]), (kt, Ktm, [kt]), (v_, Vtm, [zs])):
                    ps = self.next_ps()
                    for g in range(4):
                        self.tr(ps[0:64, g * 128:(g + 1) * 128], srcT[:, g, cs], self.ident[:, :],
                                rd + [self.ident], [ps])
                    dv = dstT[:, :, :].rearrange("p a b -> p (a b)")
                    self.copy(self.ev_eng(), dv, ps[0:64, :], [ps], [dstT])
                for nm, srcT in (("b", bt), ("a", at), ("k", kt), ("r", rt)):
                    for e2 in range(2):
                        self.ts(["dve", "pool"][e2], mk[nm][e2][:, :, :], srcT[:, :, cs], eo[:, e2:e2 + 1], ALU.mult,
                                [srcT, eo], [mk[nm][e2]])

                def amat(dst, lT, lrd, rT, rrd, mi):
                    ps = self.next_ps()
                    for h in range(8):
                        self.mm(ps[0:64, h * 64:(h + 1) * 64], mk[lT][h % 2][:, h // 2, :], rT[:, h // 2, cs],
                                True, True, [mk[lT][h % 2], rrd], [ps])
                    self.tt("dve", dst[:, :, :], ps[0:64, :].rearrange("p (h t) -> p h t", h=8),
                            masks[:, mi, :, :], ALU.mult, [ps, masks], [dst])
                if getattr(self, "stopb", 9) == 3:
                    continue
                amat(Np[0], "b", bt, at, at, 0)
                amat(Ntp[0], "a", at, bt, bt, 2)
                amat(AkT, "k", kt, at, at, 0)
                amat(ArbT, "b", bt, rt, rt, 1)
                amat(ArkT, "k", kt, rt, rt, 1)
                if getattr(self, "stopb", 9) == 4:
                    continue
                ps = self.next_ps()
                for h in range(8):
                    o = ps[0:64, h * 64:(h + 1) * 64]
                    self.mm(o, mk["a"][h % 2][:, h // 2, :], Hc[:, h // 2, :], True, False, [mk["a"][h % 2], Hc], [ps])
                    self.mm(o, AkT[:, h, :], Vtm[:, h, :], False, True, [AkT, Vtm], [ps])
                self.copy("act", U[0][:, :, :], ps[0:64, :].rearrange("p (h t) -> p h t", h=8), [ps], [U[0]])
                cur = 0
                for lvl in range(6):
                    Nn, Nt = Np[lvl % 2], Ntp[lvl % 2]
                    ps = self.next_ps()
                    for h in range(8):
                        self.mm(ps[0:64, h * 64:(h + 1) * 64], Nn[:, h, :], U[cur][:, h, :], True, True,
                                [Nn, U[cur]], [ps])
                    self.tt("dve", U[1 - cur][:, :, :], U[cur][:, :, :],
                            ps[0:64, :].rearrange("p (h t) -> p h t", h=8), ALU.add, [ps, U[cur]], [U[1 - cur]])
                    cur = 1 - cur
                    if lvl < 5:
                        N2, Nt2 = Np[(lvl + 1) % 2], Ntp[(lvl + 1) % 2]
                        ps = self.next_ps()
                        for h in range(8):
                            self.mm(ps[0:64, h * 64:(h + 1) * 64], Nt[:, h, :], Nn[:, h, :], True, True,
                                    [Nt, Nn], [ps])
                        ps2 = None
                        if lvl < 4:
                            ps2 = self.next_ps()
                            for h in range(8):
                                self.mm(ps2[0:64, h * 64:(h + 1) * 64], Nn[:, h, :], Nt[:, h, :], True, True,
                                        [Nt, Nn], [ps2])
                        self.copy("act", N2[:, :, :], ps[0:64, :].rearrange("p (h t) -> p h t", h=8), [ps], [N2])
                        if ps2 is not None:
                            self.copy("pool" if False else "dve", Nt2[:, :, :],
                                      ps2[0:64, :].rearrange("p (h t) -> p h t", h=8), [ps2], [Nt2])
                Uf = U[cur]
                if getattr(self, "stopb", 9) == 5:
                    continue
                psy = self.next_ps()
                for h in range(8):
                    o = psy[0:64, h * 64:(h + 1) * 64]
                    self.mm(o, mk["r"][h % 2][:, h // 2, :], Hc[:, h // 2, :], True, False, [mk["r"][h % 2], Hc], [psy])
                    self.mm(o, ArbT[:, h, :], Uf[:, h, :], False, False, [ArbT, Uf], [psy])
                    self.mm(o, ArkT[:, h, :], Vtm[:, h, :], False, True, [ArkT, Vtm], [psy])
                psh = self.next_ps()
                for h in range(8):
                    o = psh[:, h * 64:(h + 1) * 64]
                    self.mm(o, Btm[:, h // 2, :], Uf[:, h, :], True, False, [Btm, Uf], [psh])
                    self.mm(o, Ktm[:, h // 2, :], Vtm[:, h, :], False, True, [Ktm, Vtm], [psh])
                phv = psh[:, :].rearrange("p (g e v) -> p g e v", g=4, e=2)
                for e2 in range(2):
                    rs = slice(e2 * 64, e2 * 64 + 64)
                    self.tt("dve", Ht[rs, :, :], Hc[rs, :, :], phv[rs, :, e2, :], ALU.add, [Hc, psh], [Ht])
                    self.tt("dve", Hc[rs, :, :], Ht[rs, :, :], bc(gC[rs, :, ch:ch + 1], [64, 4, 64]), ALU.mult,
                            [Ht, gC], [Hc])
                if getattr(self, "stopb", 9) == 6:
                    continue
                yv = psy[0:64, :].rearrange("p (h t) -> p h t", h=8)
                self.copy("act", Ysb[:, :, :], yv, [psy], [Ysb])
                self.act(Ysq[:, :, :], yv, AF.Square, [psy], [Ysq])
                P.op("dve", lambda e: e.reduce_sum(out=st[:, 0, :], in_=Ysb[:, :, :], axis=AX.X), [Ysb], [st])
                P.op("dve", lambda e: e.reduce_sum(out=st[:, 1, :], in_=Ysq[:, :, :], axis=AX.X), [Ysq], [st])
                self.ts("dve", st[:, 0, :], st[:, 0, :], 1.0 / 64, ALU.mult, [st], [st])
                self.tt("dve", st[:, 2, :], st[:, 0, :], st[:, 0, :], ALU.mult, [st], [st])
                self.stt("dve", st[:, 1, :], st[:, 1, :], 1.0 / 64, st[:, 2, :], ALU.mult, ALU.subtract, [st], [st])
                self.ts("dve", st[:, 1, :], st[:, 1, :], GN_EPS, ALU.add, [st], [st])
                self.act(st[:, 1, :], st[:, 1, :], AF.Sqrt, [st], [st])
                P.op("dve", lambda e: e.reciprocal(out=st[:, 1, :], in_=st[:, 1, :]), [st], [st])
                self.tt("dve", Ysb[:, :, :], Ysb[:, :, :], bc(st[:, 0, :].unsqueeze(2), [64, 8, 64]), ALU.subtract,
                        [Ysb, st], [Ysb])
                self.tt("dve", Ysb[:, :, :], Ysb[:, :, :], bc(st[:, 1, :].unsqueeze(2), [64, 8, 64]), ALU.mult,
                        [Ysb, st], [Ysb])
                ps = self.next_ps()
                yf = Ysb[:, :, :].rearrange("p h v -> p (h v)")
                for g in range(4):
                    self.tr(ps[:, g * 64:(g + 1) * 64], yf[:, g * 128:(g + 1) * 128], self.ident[0:64, 0:64],
                            [Ysb, self.ident], [ps])
                for g in range(4):
                    self.act(ynT[:, g, cs], ps[:, g * 64:(g + 1) * 64], AF.Identity, [ps, gnw, gnb], [ynT],
                             bias=gnb[:, g:g + 1], scale=gnw[:, g:g + 1])
            self.tt("dve", ynT[:, :, :], ynT[:, :, :], bonus[:, :, :], ALU.add, [ynT, bonus], [ynT])
            self.tt("dve", ygo[:, :, :], ynT[:, :, :], G[:, :, :], ALU.mult, [ynT, G], [ygo])
            P.dma(self.ygT.t[:, t0:t0 + 512].rearrange("(g p) t -> p g t", p=128), ygo[:, :, :],
                  reads=[ygo], writes=[self.ygT])
            P.barrier()
    P.barrier()


K.phase_b = _build_phase_b


def _loadw(self, es, name, src_ap, rows, cols, stg):
    P = self.P
    nk = rows // 128
    t = name if isinstance(name, TT) else P.sb(es, name, [128, nk, cols], BF16)
    for kc in range(nk):
        st = stg[kc % 2]
        P.dma(st[:, 0:cols], src_ap[kc * 128:(kc + 1) * 128, :], writes=[st])
        self.copy(["act", "dve", "pool"][kc % 3], t[:, kc, :], st[:, 0:cols], [st], [t])
    return t


K.loadw = _loadw

MAGIC = 12582912.0
TWO_PI_1 = 6.28125
TWO_PI_2 = 2.0 * math.pi - 6.28125


def _build_phase_c(self):
    P, nc, S = self.P, self.nc, self.S
    NT = S // 512
    with ExitStack() as es:
        sb = lambda n, sh, dt=F32: P.sb(es, n, sh, dt)
        stg = [sb("cstg%d" % i, [128, 1024]) for i in range(2)]
        wuq = self.loadw(es, "wuq", self.inp["mla_w_uq"], 384, 768, stg)
        wkv = self.loadw(es, "wkv", self.inp["mla_w_ukv"], 256, 1024, stg)
        wrot = sb("wrot", [128, 3, 8, 96], BF16)
        self.memset("pool", wrot[:, :, :, :], 0.0, [wrot])
        wq4 = wuq[:, :, :].rearrange("p c (h e) -> p c h e", h=8)
        self.copy("dve", wrot[:, :, :, 64:80], wq4[:, :, :, 80:96], [wuq], [wrot])
        self.copy("dve", wrot[:, :, :, 80:96], wq4[:, :, :, 64:80], [wuq], [wrot])
        wk4 = wkv[:, :, :].rearrange("p c (h e) -> p c h e", h=8)
        wv = sb("wv", [128, 2, 8, 64], BF16)
        self.copy("dve", wv[:, :, :, :], wk4[:, :, :, 64:128], [wkv], [wv])
        qn = self.col(es, "qn", "mla_q_norm", 3)
        kvn = self.col(es, "kvn", "mla_kv_norm", 2)
        ropec = sb("ropec", [128, 4]); P.dma(ropec[:, :], self.inp["c_ropec"], writes=[ropec])
        onesb = sb("onesb", [128, 128], BF16)
        self.copy("dve", onesb[:, :], self.ones[:, :], [self.ones], [onesb])
        Kf = [sb("Kf%d" % h, [96, S], BF16) for h in range(8)]
        Vtm = sb("Vtm_a", [128, S // 128, 512], BF16)
        Qf = sb("Qf", [96, 8, 512], BF16)
        zc = sb("zc", [128, 3, 512]); zsq = sb("zsq", [128, 3, 512]); rstd = sb("rstd_c", [128, 512])
        cn = sb("cn", [128, 3, 512], BF16)
        posi = sb("posi", [96, 512], I32); ang = sb("ang", [96, 512]); kq = sb("kq", [96, 512])
        Ct = sb("Ct", [96, 512]); St = sb("St", [96, 512])
        kr = sb("kr", [96, 512]); krot = sb("krot", [96, 512]); krb = sb("krb", [96, 512], BF16)
        qa = sb("qa", [96, 512]); qb = sb("qb", [96, 512])
        Pt = [sb("Pt%d" % i, [128, 512], BF16) for i in range(3)]
        rec = sb("rec", [128, 512])
        oT = sb("oT", [128, 4, 512], BF16)
        self.pspool = [P.ps(es, "pc%d" % i, [128, 512], F32) for i in range(4)]
        self.psi = 0
        psO = [P.ps(es, "pO%d" % i, [128, 512], F32) for i in range(2)]
        psS = [P.ps(es, "pS%d" % i, [128, 512], F32) for i in range(2)]
        scale = 1.0 / math.sqrt(96.0)

        def rmsn(groups, ng, gcolt, nfeat):
            self.tt("pool", zsq[:, 0:ng, :], zc[:, 0:ng, :], zc[:, 0:ng, :], ALU.mult, [zc], [zsq])
            ps = self.next_ps()
            for c in range(ng):
                self.mm(ps[:, :], self.ones[:, :], zsq[:, c, :], c == 0, c == ng - 1, [self.ones, zsq], [ps])
            self.rsqrt(rstd[:, :], ps[:, :], 1.0 / nfeat, EPS, [ps], [rstd])
            for c in range(ng):
                self.stt("dve", cn[:, c, :], zc[:, c, :], gcolt[:, c:c + 1], rstd[:, :], ALU.mult, ALU.mult,
                         [zc, gcolt, rstd], [cn])

        for T in range(NT):
            t0 = T * 512
            P.dma(posi[:, :], self.inp["pos"][0:1, t0:t0 + 512].partition_broadcast(96), writes=[posi])
            self.copy("dve", ang[:, :], posi[:, :], [posi], [ang])
            self.ts("dve", ang[:, :], ang[:, :], ropec[0:96, 0:1], ALU.mult, [ang, ropec], [ang])
            self.ts("dve", kq[:, :], ang[:, :], 1.0 / (2 * math.pi), ALU.mult, [ang], [kq], s2=MAGIC, op1=ALU.add)
            self.ts("dve", kq[:, :], kq[:, :], -MAGIC, ALU.add, [kq], [kq])
            self.stt("dve", ang[:, :], kq[:, :], -TWO_PI_1, ang[:, :], ALU.mult, ALU.add, [kq, ang], [ang])
            self.stt("dve", ang[:, :], kq[:, :], -TWO_PI_2, ang[:, :], ALU.mult, ALU.add, [kq, ang], [ang])
            self.ts("dve", ang[:, :], ang[:, :], 3.14159, ALU.min, [ang], [ang], s2=-3.14159, op1=ALU.max)
            self.act(St[:, :], ang[:, :], AF.Sin, [ang], [St])
            self.ts("dve", St[:, :], St[:, :], ropec[0:96, 1:2], ALU.mult, [St, ropec], [St])
            self.act(kq[:, :], ang[:, :], AF.Abs, [ang], [kq])
            self.act(Ct[:, :], kq[:, :], AF.Sin, [kq, ropec], [Ct], bias=ropec[0:96, 2:3], scale=-1.0)
            P.dma(zc[:, 0:2, :], self.zT.t[17 * 128:19 * 128, 1 + t0:1 + t0 + 512].rearrange("(g p) t -> p g t", p=128),
                  reads=[self.zT], writes=[zc])
            rmsn(2, 2, kvn, 256.0)
            for h in range(8):
                ps = self.next_ps()
                for c in range(2):
                    self.mm(ps[0:64, :], wk4[:, c, h, 0:64], cn[:, c, :], c == 0, c == 1, [wkv, cn], [ps])
                self.copy(self.ev_eng(), Kf[h][0:64, t0:t0 + 512], ps[0:64, :], [ps], [Kf[h]])
            rz = 19 * 128
            P.dma(kr[64:96, :], self.zT.t[rz:rz + 32, 1 + t0:1 + t0 + 512], reads=[self.zT], writes=[kr])
            P.dma(krot[64:80, :], self.zT.t[rz + 16:rz + 32, 1 + t0:1 + t0 + 512], reads=[self.zT], writes=[krot])
            P.dma(krot[80:96, :], self.zT.t[rz:rz + 16, 1 + t0:1 + t0 + 512], reads=[self.zT], writes=[krot])
            self.tt("dve", kr[64:96, :], kr[64:96, :], Ct[64:96, :], ALU.mult, [kr, Ct], [kr])
            self.tt("dve", krot[64:96, :], krot[64:96, :], St[64:96, :], ALU.mult, [krot, St], [krot])
            self.tt("dve", krb[64:96, :], kr[64:96, :], krot[64:96, :], ALU.add, [kr, krot], [krb])
            for h in range(8):
                self.copy(["dve", "pool"][h % 2], Kf[h][64:96, t0:t0 + 512], krb[64:96, :], [krb], [Kf[h]])
            for b4 in range(4):
                ps = self.next_ps()
                for c in range(2):
                    self.mm(ps[:, :], cn[:, c, b4 * 128:(b4 + 1) * 128], wv[:, c, :, :].rearrange("p h e -> p (h e)"),
                            c == 0, c == 1, [cn, wv], [ps])
                self.copy(self.ev_eng(), Vtm[:, T * 4 + b4, :], ps[:, :], [ps], [Vtm])
            P.dma(zc[:, 0:3, :], self.zT.t[14 * 128:17 * 128, 1 + t0:1 + t0 + 512].rearrange("(g p) t -> p g t", p=128),
                  reads=[self.zT], writes=[zc])
            rmsn(3, 3, qn, 384.0)
            for h in range(8):
                ps = self.next_ps()
                ps2 = self.next_ps()
                for c in range(3):
                    self.mm(ps[0:96, :], wuq[:, c, h * 96:(h + 1) * 96], cn[:, c, :], c == 0, c == 2, [wuq, cn], [ps])
                for c in range(3):
                    self.mm(ps2[0:96, :], wrot[:, c, h, :], cn[:, c, :], c == 0, c == 2, [wrot, cn], [ps2])
                self.tt("dve", qa[:, :], ps[0:96, :], Ct[:, :], ALU.mult, [ps, Ct], [qa])
                self.tt("dve", qb[:, :], ps2[0:96, :], St[:, :], ALU.mult, [ps2, St], [qb])
                self.tt("pool", Qf[:, h, :], qa[:, :], qb[:, :], ALU.add, [qa, qb], [Qf])
            pi = 0
            for h in range(8):
                pO, pS = psO[h % 2], psS[h % 2]
                nkb = 4 * T + 4
                for kb in range(nkb):
                    nq0 = max(0, kb - 4 * T)
                    cl = slice(nq0 * 128, 512)
                    ps = self.next_ps()
                    self.mm(ps[:, cl], Kf[h][0:96, kb * 128:(kb + 1) * 128], Qf[0:96, h, cl], True, True,
                            [Kf[h], Qf], [ps])
                    pt = Pt[pi % 3]
                    pi += 1
                    self.act(pt[:, cl], ps[:, cl], AF.Exp, [ps], [pt], scale=scale)
                    if kb >= 4 * T:
                        self.memset("pool", pt[64:128, nq0 * 128:nq0 * 128 + 64], 0.0, [pt])
                    hp2 = (h // 2) * 128
                    self.mm(pO[:, cl], Vtm[:, kb, hp2:hp2 + 128], pt[:, cl], kb == 0, kb == nkb - 1, [Vtm, pt], [pO])
                    self.mm(pS[:, cl], onesb[:, :], pt[:, cl], kb == 0, kb == nkb - 1, [onesb, pt], [pS])
                P.op("dve", lambda e, o=rec[:, :], i=pS[:, :]: e.reciprocal(out=o, in_=i), [pS], [rec])
                rs = slice((h % 2) * 64, (h % 2) * 64 + 64)
                self.tt("dve", oT[rs, h // 2, :], pO[rs, :], rec[rs, :], ALU.mult, [pO, rec], [oT])
            P.dma(self.oTs.t[:, t0:t0 + 512].rearrange("(g p) t -> p g t", p=128), oT[:, :, :],
                  reads=[oT], writes=[self.oTs])
            P.barrier()
    P.barrier()
    with ExitStack() as es:
        sb = lambda n, sh, dt=F32: P.sb(es, n, sh, dt)
        wo_m = P.sb(es, "wo_m", [128, 4, 1024], BF16)
        wo_r = P.sb(es, "wo_r", [128, 4, 1024], BF16)
        w_o = P.sb(es, "w_o", [128, 8, 1024], BF16)
        stg = [sb("c2stg%d" % i, [128, 1024]) for i in range(2)]
        self.loadw(es, wo_m, self.inp["mla_w_o"], 512, 1024, stg)
        self.loadw(es, wo_r, self.inp["rw_w_o"], 512, 1024, stg)
        self.loadw(es, w_o, self.inp["w_out"], 1024, 1024, stg)
        oT = sb("oT2", [128, 4, 512], BF16)
        ygl = sb("ygl", [128, 4, 512], BF16)
        gA = [sb("gA%d" % i, [128, 512]) for i in range(2)]
        gB = [sb("gB%d" % i, [128, 512]) for i in range(2)]
        ta = sb("ta", [128, 512]); tb = sb("tb", [128, 512])
        mix = sb("mix", [128, 8, 512], BF16)
        xl = [sb("xl%d" % i, [128, 1024]) for i in range(2)]
        self.pspool = [P.ps(es, "pc2_%d" % i, [128, 512], F32) for i in range(6)]
        self.psi = 0
        for T in range(NT):
            t0 = T * 512
            P.dma(oT[:, :, :], self.oTs.t[:, t0:t0 + 512].rearrange("(g p) t -> p g t", p=128),
                  reads=[self.oTs], writes=[oT])
            P.dma(ygl[:, :, :], self.ygT.t[:, t0:t0 + 512].rearrange("(g p) t -> p g t", p=128),
                  reads=[self.ygT], writes=[ygl])
            for j in range(8):
                ga, gb = gA[j % 2], gB[j % 2]
                P.dma(ga[:, :], self.zT.t[(20 + j) * 128:(21 + j) * 128, 1 + t0:1 + t0 + 512], reads=[self.zT], writes=[ga])
                P.dma(gb[:, :], self.zT.t[(28 + j) * 128:(29 + j) * 128, 1 + t0:1 + t0 + 512], reads=[self.zT], writes=[gb])
                ps = self.next_ps()
                ps2 = self.next_ps()
                for c in range(4):
                    self.mm(ps[:, :], wo_r[:, c, j * 128:(j + 1) * 128], ygl[:, c, :], c == 0, c == 3, [wo_r, ygl], [ps])
                for c in range(4):
                    self.mm(ps2[:, :], wo_m[:, c, j * 128:(j + 1) * 128], oT[:, c, :], c == 0, c == 3, [wo_m, oT], [ps2])
                self.tt("dve", ta[:, :], ps[:, :], ga[:, :], ALU.mult, [ps, ga], [ta])
                self.tt("dve", tb[:, :], ps2[:, :], gb[:, :], ALU.mult, [ps2, gb], [tb])
                self.tt("pool", mix[:, j, :], ta[:, :], tb[:, :], ALU.add, [ta, tb], [mix])
            for b4 in range(4):
                x_l = xl[b4 % 2]
                r0 = t0 + b4 * 128
                P.dma(x_l[:, :], self.inp["x"][r0:r0 + 128, :], writes=[x_l])
                for hf in range(2):
                    ps = self.next_ps()
                    for j in range(8):
                        self.mm(ps[:, :], mix[:, j, b4 * 128:(b4 + 1) * 128], w_o[:, j, hf * 512:(hf + 1) * 512],
                                j == 0, j == 7, [mix, w_o], [ps])
                    self.tt("dve", x_l[:, hf * 512:(hf + 1) * 512], x_l[:, hf * 512:(hf + 1) * 512], ps[:, :], ALU.add,
                            [x_l, ps], [x_l])
                P.dma(self.x1.t[r0:r0 + 128, :], x_l[:, :], reads=[x_l], writes=[self.x1])
    P.barrier()


K.phase_c = _build_phase_c


def _build_phase_d(self):
    P, nc, S = self.P, self.nc, self.S
    NEG = -1e30
    with ExitStack() as es:
        sb = lambda n, sh, dt=F32: P.sb(es, n, sh, dt)
        stg = [sb("dstg%d" % i, [128, 1024]) for i in range(2)]
        self.pspool = [P.ps(es, "pdp%d" % i, [128, 512], F32) for i in range(4)]
        self.psi = 0
        if self.prep_done < 128:
            self.prep_alloc(es)
            while self.prep_done < 128:
                self.prep_step()
    P.barrier()
    with ExitStack() as es:
        sb = lambda n, sh, dt=F32: P.sb(es, n, sh, dt)
        self.pspool = [P.ps(es, "pd%d" % i, [128, 512], F32) for i in range(3)]
        self.psi = 0
        psOut = [[P.ps(es, "po%d%d" % (a, b), [128, 512], F32) for b in range(2)] for a in range(2)]
        uni = P.sb(es, "uni", [128, 8192], F32)
        wq = TT(uni[:, :].bitcast(BF16).rearrange("p (k c) -> p k c", k=8), P.buf("wqv"))
        wpg = P.sb(es, "wpg", [128, 8, 1024], BF16)
        wpp = P.sb(es, "wpp", [128, 2, 1024], BF16)
        skT = sb("skT", [128, 16, 128], BF16)
        with ExitStack() as es3:
            stg = [P.sb(es3, "dstg2_%d" % i, [128, 2048], F32) for i in range(2)]
            self.loadw(es, wq, self.inp["peer_w_q"], 1024, 2048, stg)
            P.dma(self.wqs.t[:, :].rearrange("(k p) c -> p k c", p=128), wq[:, :, :], reads=[wq], writes=[self.wqs])
            self.loadw(es, wpg, self.inp["ple_w_gate"], 1024, 1024, stg)
            self.loadw(es, wpp, self.inp["ple_w_proj"], 256, 1024, stg)
            self._skt(stg, skT)
            P.barrier()
        for hc in range(0):
            st = stg[hc % 2]
            P.dma(st[:, 0:128], self.inp["peer_sub_keys"][hc], writes=[st])
            ps = self.next_ps()
            self.tr(ps[:, 0:128], st[:, 0:128], self.ident[:, :], [st, self.ident], [ps])
            self.copy("dve", skT[:, hc, :], ps[:, 0:128], [ps], [skT])
        gf = self.col(es, "gf", "norm_ffn", 8)
        gp = self.col(es, "gp", "norm_ple", 8)
        gfin = sb("gfin", [128, 1024])
        P.dma(gfin[:, :], self.inp["norm_final"].rearrange("(o d) -> o d", o=1).partition_broadcast(128), writes=[gfin])
        x1t = [sb("x1t%d" % i, [128, 1024]) for i in range(2)]
        hb = sb("hb_d", [128, 1024], BF16)
        junk = hb
        ssd = sb("ssd", [128, 1])
        xnT = sb("xnT", [128, 8, 256], BF16)
        xn2T = sb("xn2T", [128, 8, 128], BF16)
        qT = sb("qT", [128, 16, 256], BF16)
        sS = [sb("sS%d" % i, [128, 16, 128]) for i in range(2)]
        s1pp = [sb("s1pp%d" % i, [128, 8, 128]) for i in range(2)]
        bE = [sb("bE%d" % i, [128, 8]) for i in range(2)]
        a16 = sb("a16", [128, 16]); b16 = sb("b16", [128, 16]); c16 = sb("c16", [128, 16]); e16 = sb("e16", [128, 16])
        tmpk = sb("tmpk", [128, 128]); cand = sb("cand", [128, 256]); cand2 = sb("cand2", [128, 256])
        sc = sb("scal", [128, 8])
        ubk = [sb("ubk%d" % i, [128, 8, 512], BF16) for i in range(2)]
        vbk = [sb("vbk%d" % i, [128, 4, 1024], BF16) for i in range(3)]
        dl = [TT(uni[:, i * 2048:(i + 1) * 2048].rearrange("p (a b c) -> p a b c", a=4, b=4), P.buf("dl%d" % i))
              for i in range(4)]
        dl2 = [TT(uni[:, i * 4096:(i + 1) * 4096].rearrange("p (a b c) -> p a b c", a=8, b=4), P.buf("dl2_%d" % i))
               for i in range(2)]
        Ee2 = [sb("Ee2_%d" % i, [128, 8, 4, 128], BF16) for i in range(2)]
        Gh2 = [sb("Gh2_%d" % i, [128, 8, 4, 128], BF16) for i in range(2)]
        dg = [sb("dg%d" % i, [128, 8, 128], BF16) for i in range(2)]
        wE = sb("wE", [128, 8])
        gel = [sb("gel%d" % i, [128, 512], BF16) for i in range(3)]
        Pm = [sb("Pm%d" % i, [128, 512], BF16) for i in range(3)]
        PTs = [sb("PTs%d" % i, [128, 4, 128], BF16) for i in range(3)]
        pst = P.ps(es, "pstd", [128, 8, 128], BF16)
        pl = sb("pl", [128, 256]); plb = sb("plb", [128, 256], BF16); pT = sb("pT", [128, 2, 128], BF16)
        gt = sb("gt", [128, 512])
        ib = self.identb

        def norm_T(xsrc, gcolt, dstT, csl):
            self.act(junk[:, :], xsrc[:, :], AF.Square, [xsrc], [junk, ssd], accum=ssd[:, :])
            self.rsqrt(ssd[:, :], ssd[:, :], 1.0 / D, EPS, [ssd], [ssd])
            self.ts("dve", hb[:, :], xsrc[:, :], ssd[:, 0:1], ALU.mult, [xsrc, ssd], [hb])
            for kc in range(8):
                self.tr(pst[:, kc, :], hb[:, kc * 128:(kc + 1) * 128], ib[:, :], [hb, ib], [pst])
            self.tt("dve", dstT[:, :, csl], pst[:, :, :], bc(gcolt[:, :].unsqueeze(2), [128, 8, 128]), ALU.mult,
                    [pst, gcolt], [dstT])

        def top16(dst, src, srcT, tmp, tmpT):
            P.op("dve", lambda e, o=dst[:, 0:8], i=src: e.max(out=o, in_=i), [dst, srcT], [dst])
            P.op("dve", lambda e, o=tmp, r=dst[:, 0:8], i=src: e.match_replace(out=o, in_to_replace=r, in_values=i,
                                                                                imm_value=NEG), [dst, srcT], [tmpT])
            P.op("dve", lambda e, o=dst[:, 8:16], i=tmp: e.max(out=o, in_=i), [tmpT], [dst])

        gi = 0
        mk = 0
        for tl in range(S // 256):
            P.dma(wq[:, :, :], self.wqs.t[:, :].rearrange("(k p) c -> p k c", p=128), reads=[self.wqs], writes=[wq])
            for sub in range(2):
                r0 = tl * 256 + sub * 128
                P.dma(x1t[sub][:, :], self.x1.t[r0:r0 + 128, :], reads=[self.x1], writes=[x1t[sub]])
                norm_T(x1t[sub], gf, xnT, slice(sub * 128, (sub + 1) * 128))
            for hc in range(16):
                ps = self.next_ps()
                for kc in range(8):
                    self.mm(ps[:, 0:256], wq[:, kc, hc * 128:(hc + 1) * 128], xnT[:, kc, :], kc == 0, kc == 7,
                            [wq, xnT], [ps])
                self.copy(self.ev_eng(), qT[:, hc, :], ps[:, 0:256], [ps], [qT])
            for sub in range(2):
                for g4 in range(4):
                    ps = self.next_ps()
                    for i4 in range(4):
                        hc = g4 * 4 + i4
                        self.mm(ps[:, i4 * 128:(i4 + 1) * 128], qT[:, hc, sub * 128:(sub + 1) * 128], skT[:, hc, :],
                                True, True, [qT, skT], [ps])
                    self.copy(self.ev_eng(), sS[sub][:, g4 * 4:(g4 + 1) * 4, :],
                              ps[:, :].rearrange("p (a n) -> p a n", a=4), [ps], [sS[sub]])
                for h in range(8):
                    s1 = sS[sub][:, 2 * h, :]
                    s2 = sS[sub][:, 2 * h + 1, :]
                    top16(a16, s1, sS[sub], tmpk[:, :], tmpk)
                    top16(b16, s2, sS[sub], tmpk[:, :], tmpk)
                    self.tt("dve", cand[:, :].rearrange("p (a b) -> p a b", a=16),
                            bc(a16[:, :].unsqueeze(2), [128, 16, 16]), bc(b16[:, :].unsqueeze(1), [128, 16, 16]),
                            ALU.add, [a16, b16], [cand])
                    top16(c16, cand[:, :], cand, cand2[:, :], cand2)
                    P.op("dve", lambda e, o=sc[:, 0:1], i=c16[:, :]: e.tensor_reduce(out=o, in_=i, axis=AX.X, op=ALU.min),
                         [c16], [sc])
                    P.op("dve", lambda e, o=sc[:, 1:2], i=c16[:, :]: e.tensor_reduce(out=o, in_=i, axis=AX.X, op=ALU.max),
                         [c16], [sc])
                    self.ts("dve", sc[:, 0:1], sc[:, 0:1], -2e-6, ALU.add, [sc], [sc])
                    self.ts("dve", sc[:, 2:3], sc[:, 1:2], -1.0, ALU.mult, [sc], [sc])
                    self.act(e16[:, :], c16[:, :], AF.Exp, [c16, sc], [e16, sc], bias=sc[:, 2:3], accum=sc[:, 3:4])
                    self.act(sc[:, 4:5], sc[:, 3:4], AF.Ln, [sc], [sc])
                    self.tt("dve", sc[:, 5:6], sc[:, 0:1], sc[:, 1:2], ALU.subtract, [sc], [sc])
                    self.tt("dve", bE[sub][:, h:h + 1], sc[:, 5:6], sc[:, 4:5], ALU.subtract, [sc], [bE[sub]])
                    self.ts("dve", s1pp[sub][:, h, :], s1, sc[:, 0:1], ALU.subtract, [sS[sub], sc], [s1pp[sub]])
                self.act(wE[:, :], bE[sub][:, :], AF.Exp, [bE[sub]], [wE])
                for h in range(8):
                    self.ts(["dve", "pool"][h % 2], dg[sub][:, h, :], ib[:, :], wE[:, h:h + 1], ALU.mult, [ib, wE], [dg[sub]])
            NK = 64
            P.barrier()

            def ldblk(blk):
                ub, vb = ubk[blk % 2], vbk[blk % 3]
                P.dma(ub[:, :, :], self.uTs.t[:, blk * 512:(blk + 1) * 512].rearrange("(k p) e -> p k e", p=128),
                      reads=[self.uTs], writes=[ub])
                P.dma(vb[:, :, :], self.vs.t[blk * 512:(blk + 1) * 512, :].rearrange("(c p) d -> p c d", p=128),
                      reads=[self.vs], writes=[vb])

            ldblk(0)

            def S1(k):
                blk, sub = k // 2, k % 2
                ub = ubk[blk % 2]
                if sub == 0 and blk + 1 < 32:
                    ldblk(blk + 1)
                psA = self.next_ps()
                for kc in range(8):
                    self.mm(psA[:, :], xnT[:, kc, sub * 128:(sub + 1) * 128], ub[:, kc, :], kc == 0, kc == 7,
                            [xnT, ub], [psA])
                d_, e_, g_ = dl2[k % 2], Ee2[k % 2], Gh2[k % 2]
                self.stt("dve", g_[:, :, :, :], d_[:, :, :, :], 0.0, e_[:, :, :, :], ALU.is_ge, ALU.mult,
                         [d_, e_], [g_])
                self.psA_k[k % 2] = psA

            def S0(k):
                blk, sub = k // 2, k % 2
                d_, e_ = dl2[k % 2], Ee2[k % 2]
                s2v = sS[sub][:, :, :].rearrange("p (h c) n -> p h c n", c=2)[:, :, 1, :]
                self.tt("dve", d_[:, :, :, :], bc(s2v.unsqueeze(2), [128, 8, 4, 128]),
                        bc(s1pp[sub][:, :, blk * 4:(blk + 1) * 4].unsqueeze(3), [128, 8, 4, 128]),
                        ALU.add, [sS[sub], s1pp[sub]], [d_])
                self.act(e_[:, :, :, :], d_[:, :, :, :], AF.Exp, [d_], [e_])

            def S1b(k):
                psA = self.psA_k[k % 2]
                self.act(gel[k % 3][:, :], psA[:, :], AF.Gelu, [psA], [gel[k % 3]])

            def S2(k):
                sub = k % 2
                psM = self.next_ps()
                g_ = Gh2[k % 2]
                for h in range(8):
                    self.mm(psM[:, :], dg[sub][:, h, :], g_[:, h, :, :].rearrange("p a b -> p (a b)"),
                            h == 0, h == 7, [dg[sub], g_], [psM])
                self.psM_k[k % 2] = psM

            def S2b(k):
                psM = self.psM_k[k % 2]
                self.tt("dve", Pm[k % 3][:, :], gel[k % 3][:, :], psM[:, :], ALU.mult, [gel[k % 3], psM], [Pm[k % 3]])

            def S3(k):
                pm = Pm[k % 3]
                for ec in range(4):
                    self.tr(pst[:, ec, :], pm[:, ec * 128:(ec + 1) * 128], ib[:, :], [pm, ib], [pst])
                self.copy("act", PTs[k % 3][:, :, :], pst[:, 0:4, :], [pst], [PTs[k % 3]])

            def S4(k):
                blk, sub = k // 2, k % 2
                vb = vbk[blk % 3]
                pts = PTs[k % 3]
                for hf in range(2):
                    for ec in range(4):
                        self.mm(psOut[sub][hf][:, :], pts[:, ec, :], vb[:, ec, hf * 512:(hf + 1) * 512],
                                blk == 0 and ec == 0, blk == 31 and ec == 3, [pts, vb], [psOut[sub][hf]])

            self.psA_k = [None, None]
            self.psM_k = [None, None]
            S0(0)
            for r in range(NK + 3):
                if 0 <= r - 3 < NK:
                    S4(r - 3)
                if r + 1 < NK:
                    S0(r + 1)
                if r < NK:
                    S1(r)
                if 0 <= r - 1 < NK:
                    S2(r - 1)
                if 0 <= r - 2 < NK:
                    S3(r - 2)
                if r < NK:
                    S1b(r)
                if 0 <= r - 1 < NK:
                    S2b(r - 1)
            for sub in range(2):
                r0 = tl * 256 + sub * 128
                xx = x1t[sub]
                for hf in range(2):
                    self.tt("dve", xx[:, hf * 512:(hf + 1) * 512], xx[:, hf * 512:(hf + 1) * 512], psOut[sub][hf][:, :],
                            ALU.add, [xx, psOut[sub][hf]], [xx])
                norm_T(xx, gp, xn2T, slice(0, 128))
                P.dma(pl[:, :], self.inp["p"][r0:r0 + 128, :], writes=[pl])
                self.copy("pool", plb[:, :], pl[:, :], [pl], [plb])
                for c in range(2):
                    self.tr(pst[:, c, :], plb[:, c * 128:(c + 1) * 128], ib[:, :], [plb, ib], [pst])
                self.copy("act", pT[:, :, :], pst[:, 0:2, :], [pst], [pT])
                for hf in range(2):
                    hs = slice(hf * 512, (hf + 1) * 512)
                    ps = self.next_ps()
                    for kc in range(8):
                        self.mm(ps[:, :], xn2T[:, kc, :], wpg[:, kc, hs], kc == 0, kc == 7, [xn2T, wpg], [ps])
                    self.act(gt[:, :], ps[:, :], AF.Sigmoid, [ps], [gt])
                    ps2 = self.next_ps()
                    for c in range(2):
                        self.mm(ps2[:, :], pT[:, c, :], wpp[:, c, hs], c == 0, c == 1, [pT, wpp], [ps2])
                    self.tt("dve", gt[:, :], gt[:, :], ps2[:, :], ALU.mult, [gt, ps2], [gt])
                    self.tt("dve", xx[:, hs], xx[:, hs], gt[:, :], ALU.add, [xx, gt], [xx])
                self.act(junk[:, :], xx[:, :], AF.Square, [xx], [junk, ssd], accum=ssd[:, :])
                self.rsqrt(ssd[:, :], ssd[:, :], 1.0 / D, EPS, [ssd], [ssd])
                self.stt("dve", xx[:, :], xx[:, :], ssd[:, 0:1], gfin[:, :], ALU.mult, ALU.mult, [xx, ssd, gfin], [xx])
                P.dma(self.out[r0:r0 + 128, :], xx[:, :], reads=[xx])
            P.barrier()
    P.barrier()


K.phase_d = _build_phase_d


def _prep_alloc(self, es):
    P = self.P
    self.pp_u32 = [P.sb(es, "ub32_%d" % i, [128, 1024], F32) for i in range(2)]
    self.pp_ucv = [P.sb(es, "ucv%d" % i, [128, 8, 128], BF16) for i in range(2)]
    self.pp_vcv = [P.sb(es, "vcv%d" % i, [128, 1024], BF16) for i in range(2)]
    self.pp_v32 = [P.sb(es, "pv32_%d" % i, [128, 1024], F32) for i in range(2)]


def _prep_step(self):
    P = self.P
    c = self.prep_done
    self.prep_done += 1
    u32 = self.pp_u32[c % 2]
    P.dma(u32[:, :], self.inp["peer_u"][c * 128:(c + 1) * 128, :], writes=[u32])
    uc = self.pp_ucv[c % 2]
    for half in range(2):
        ps = self.next_ps()
        for k4 in range(4):
            kc = half * 4 + k4
            self.tr(ps[:, k4 * 128:(k4 + 1) * 128], u32[:, kc * 128:(kc + 1) * 128], self.ident[:, :],
                    [u32, self.ident], [ps])
        self.copy(self.ev_eng(), uc[:, half * 4:(half + 1) * 4, :],
                  ps[:, :].rearrange("p (k e) -> p k e", k=4), [ps], [uc])
    P.dma(self.uTs.t[:, c * 128:(c + 1) * 128].rearrange("(k p) e -> p k e", p=128), uc[:, :, :],
          reads=[uc], writes=[self.uTs])
    v32 = self.pp_v32[c % 2]
    P.dma(v32[:, :], self.inp["peer_v"][c * 128:(c + 1) * 128, :], writes=[v32])
    vc = self.pp_vcv[c % 2]
    self.copy("pool", vc[:, :], v32[:, :], [v32], [vc])
    P.dma(self.vs.t[c * 128:(c + 1) * 128, :], vc[:, :], reads=[vc], writes=[self.vs])


K.prep_alloc = _prep_alloc
K.prep_step = _prep_step


def _skt(self, stg, skT):
    P = self.P
    for hc in range(16):
        st = stg[hc % 2]
        P.dma(st[:, 0:128], self.inp["peer_sub_keys"][hc], writes=[st])
        ps = self.next_ps()
        self.tr(ps[:, 0:128], st[:, 0:128], self.ident[:, :], [st, self.ident], [ps])
        self.copy("dve", skT[:, hc, :], ps[:, 0:128], [ps], [skT])


K._skt = _skt
```

```python
import math
from contextlib import ExitStack
import numpy as np
import concourse.bass as bass
import concourse.mybir as mybir
from concourse.bass_utils import run_bass_kernel_spmd

F32 = mybir.dt.float32
BF16 = mybir.dt.bfloat16
I32 = mybir.dt.int32
AF = mybir.ActivationFunctionType
ALU = mybir.AluOpType
AX = mybir.AxisListType

D = 1024
NCOL = 4512
NG = 36
EPS = 1e-6
GN_EPS = 64e-5
NDSEM = 12


class Buf:
    __slots__ = ("name", "w", "r")

    def __init__(self, name):
        self.name = name
        self.w = None
        self.r = []


class TT:
    def __init__(self, t, buf):
        self.t = t
        self.buf = buf

    def __getitem__(self, k):
        return self.t[k]


class Prog:
    def __init__(self, nc, es):
        self.nc = nc
        self.es = es
        self.engs = ["pe", "act", "dve", "pool", "sp"]
        self.q = {e: [] for e in self.engs}
        self.cnt = {e: 0 for e in self.engs}
        self.sem = {}
        for e in ["pe", "act", "dve", "pool"]:
            self.sem[e] = es.enter_context(nc.semaphore("s_" + e))
        self.dsem = [es.enter_context(nc.semaphore("d%d" % i)) for i in range(NDSEM)]
        self.dcnt = [0] * NDSEM
        self.dnext = 0
        self.seen = {e: {} for e in self.engs}
        self.nb = 0
        self.epoch = 0
        self.semtab = {(e, 0): self.sem[e] for e in self.sem}

    def buf(self, name=None):
        self.nb += 1
        return Buf(name or "b%d" % self.nb)

    def sb(self, es, name, shape, dt):
        t = es.enter_context(self.nc.sbuf_tensor(name, list(shape), dt))
        return TT(t, self.buf(name))

    def ps(self, es, name, shape, dt=F32):
        t = es.enter_context(self.nc.psum_tensor(name, list(shape), dt))
        return TT(t, self.buf(name))

    def _semobj(self, key):
        return self.semtab[key] if isinstance(key, tuple) else self.dsem[key]

    def _need(self, eng, ev, waits):
        if ev is None:
            return
        key, val, peng = ev
        if eng == "pe" and peng == "pe":
            return
        if isinstance(key, tuple) and key[1] < self.epoch:
            return
        if self.seen[eng].get(key, 0) >= val:
            return
        if waits.get(key, 0) < val:
            waits[key] = val

    def op(self, eng, fn, reads=(), writes=(), dma=False):
        waits = {}
        for b in reads:
            b = b.buf if isinstance(b, TT) else b
            self._need(eng, b.w, waits)
        for b in writes:
            b = b.buf if isinstance(b, TT) else b
            self._need(eng, b.w, waits)
            for ev in b.r:
                self._need(eng, ev, waits)
        if dma:
            j = self.dnext
            self.dnext = (self.dnext + 1) % NDSEM
            if self.dcnt[j] > 0:
                ev = (j, 16 * self.dcnt[j], "dma")
                self._need(eng, ev, waits)
            self.dcnt[j] += 1
            ev = (j, 16 * self.dcnt[j], "dma")
            semo, inc = self.dsem[j], 16
        else:
            self.cnt[eng] += 1
            ev = ((eng, self.epoch), self.cnt[eng], eng)
            semo, inc = self.sem[eng], 1
        for k, v in waits.items():
            self.seen[eng][k] = v
        wl = [(self._semobj(k), v) for k, v in waits.items()]

        def thunk(e, wl=wl, fn=fn, semo=semo, inc=inc):
            for s, v in wl:
                e.wait_ge(s, v)
            fn(e).then_inc(semo, inc)

        self.q[eng].append(thunk)
        for b in reads:
            b = b.buf if isinstance(b, TT) else b
            b.r.append(ev)
            if len(b.r) > 64:
                mx = {}
                for (k, v, pe) in b.r:
                    if k not in mx or mx[k][1] < v:
                        mx[k] = (k, v, pe)
                b.r = list(mx.values())
        for b in writes:
            b = b.buf if isinstance(b, TT) else b
            b.w = ev
            b.r = []
        return ev

    def dma(self, out, in_, reads=(), writes=(), eng="sp", **kw):
        return self.op(eng, lambda e: e.dma_start(out=out, in_=in_, **kw), reads, writes, dma=True)

    def barrier(self):
        tot = {(e, self.epoch): self.cnt[e] for e in ["pe", "act", "dve", "pool"]}
        dt = {j: 16 * self.dcnt[j] for j in range(NDSEM)}
        for eng in self.engs:
            wl = []
            for k, v in list(tot.items()) + list(dt.items()):
                if v > 0 and self.seen[eng].get(k, 0) < v and not (isinstance(k, tuple) and k[0] == eng):
                    wl.append((self._semobj(k), v))
                    self.seen[eng][k] = v

            def thunk(e, wl=wl):
                for s, v in wl:
                    e.wait_ge(s, v)

            self.q[eng].append(thunk)
        if max(self.cnt.values()) > 6000:
            self.epoch += 1
            for e in ["pe", "act", "dve", "pool"]:
                self.sem[e] = self.es.enter_context(self.nc.semaphore("s_%s_%d" % (e, self.epoch)))
                self.semtab[(e, self.epoch)] = self.sem[e]
                self.cnt[e] = 0

    def emit(self):
        nc = self.nc
        with nc.Block() as block:
            @block.tensor
            def _(e):
                for f in self.q["pe"]:
                    f(e)

            @block.scalar
            def _(e):
                for f in self.q["act"]:
                    f(e)

            @block.vector
            def _(e):
                for f in self.q["dve"]:
                    f(e)

            @block.gpsimd
            def _(e):
                for f in self.q["pool"]:
                    f(e)

            @block.sync
            def _(e):
                for f in self.q["sp"]:
                    f(e)


def bc(ap, shape):
    return ap.to_broadcast(list(shape))


class K:
    def __init__(self, S, debug=False):
        self.S = S
        self.debug = debug
        self.nc = bass.Bass("TRN2", target_bir_lowering=False)
        self.inp = {}
        self.rr = 0
        self.split_delta = True
        self.prep_done = 0

    def din(self, name, shape, dt=F32):
        a = self.nc.dram_tensor(name, list(shape), dt, kind="ExternalInput").ap()
        self.inp[name] = a
        return a

    def dscr(self, name, shape, dt=F32):
        return TT(self.nc.dram_tensor(name, list(shape), dt, kind="Internal").ap(), Buf(name))

    def ev_eng(self):
        self.rr += 1
        return ["act", "dve"][self.rr % 2]

    def act(self, out, in_, func, reads, writes, bias=None, scale=1.0, accum=None):
        kw = {}
        if bias is not None:
            kw["bias"] = bias
        if accum is not None:
            kw["accum_out"] = accum
        return self.P.op("act", lambda e: e.activation(out=out, in_=in_, func=func, scale=scale, **kw),
                         reads, writes)

    def tt(self, eng, out, in0, in1, op, reads, writes):
        return self.P.op(eng, lambda e: e.tensor_tensor(out=out, in0=in0, in1=in1, op=op), reads, writes)

    def ts(self, eng, out, in0, s1, op0, reads, writes, s2=None, op1=None):
        if op1 is None:
            return self.P.op(eng, lambda e: e.tensor_scalar(out=out, in0=in0, scalar1=s1, scalar2=None, op0=op0),
                             reads, writes)
        return self.P.op(eng, lambda e: e.tensor_scalar(out=out, in0=in0, scalar1=s1, scalar2=s2, op0=op0, op1=op1),
                         reads, writes)

    def stt(self, eng, out, in0, scalar, in1, op0, op1, reads, writes):
        return self.P.op(eng, lambda e: e.scalar_tensor_tensor(out=out, in0=in0, scalar=scalar, in1=in1,
                                                               op0=op0, op1=op1), reads, writes)

    def copy(self, eng, out, in_, reads, writes):
        if eng == "act":
            return self.act(out, in_, AF.Copy, reads, writes)
        return self.P.op(eng, lambda e: e.tensor_copy(out=out, in_=in_), reads, writes)

    def mm(self, out, lhsT, rhs, start, stop, reads, writes):
        return self.P.op("pe", lambda e: e.matmul(out, lhsT, rhs, start=start, stop=stop), reads, writes)

    def tr(self, out, in_, ident, reads, writes):
        return self.P.op("pe", lambda e: e.transpose(out, in_, ident), reads, writes)

    def memset(self, eng, ap, val, writes):
        return self.P.op(eng, lambda e: e.memset(ap, val), (), writes)

    def rsqrt(self, out, in_, mul, add, reads, writes, eng="dve"):
        self.ts(eng, out, in_, mul, ALU.mult, reads, writes, s2=add, op1=ALU.add)
        self.act(out, out, AF.Sqrt, writes, writes)
        self.P.op("dve", lambda e: e.reciprocal(out=out, in_=out), writes, writes)

    def next_ps(self):
        self.psi = (self.psi + 1) % len(self.pspool)
        return self.pspool[self.psi]


def host_consts():
    c = {}
    c["ident"] = np.eye(128, dtype=np.float32)
    bo = np.zeros((128, 128), np.float32)
    bo[:64, :64] = 1.0
    bo[64:, 64:] = 1.0
    c["blockones"] = bo
    c["ones"] = np.ones((128, 128), np.float32)
    s = np.arange(64)[:, None]
    t = np.arange(64)[None, :]
    m = np.zeros((64, 3, 8, 64), np.float32)
    m[:, 0] = (s < t).astype(np.float32)[:, None, :]
    m[:, 1] = (s <= t).astype(np.float32)[:, None, :]
    m[:, 2] = (t < s).astype(np.float32)[:, None, :]
    c["masks"] = m.reshape(64, 3 * 8 * 64)
    invf = (10000.0 ** (-np.arange(0, 32, 2, dtype=np.float32) / 32)).astype(np.float32)
    rc = np.zeros((128, 4), np.float32)
    rc[64:80, 0] = invf
    rc[80:96, 0] = invf
    rc[64:80, 1] = -1.0
    rc[80:96, 1] = 1.0
    rc[:, 2] = math.pi / 2
    rc[:, 3] = 0.0
    c["ropec"] = rc
    return c


def _build_phase_a(self):
    P, nc, S = self.P, self.nc, self.S
    with ExitStack() as es:
        win = P.sb(es, "win", [128, 8, NCOL], BF16)
        stg = [P.sb(es, "wstg%d" % i, [128, NCOL], F32) for i in range(2)]
        gcol = P.sb(es, "gcol", [128, 8], F32)
        xt = [P.sb(es, "xt%d" % i, [128, D], F32) for i in range(2)]
        junk = P.sb(es, "junk", [128, D], F32)
        hb = [P.sb(es, "hb%d" % i, [128, D], BF16) for i in range(2)]
        ss = [P.sb(es, "ss%d" % i, [128, 1], F32) for i in range(2)]
        hT = [P.sb(es, "hT%d" % i, [128, 8, 512], BF16) for i in range(2)]
        zst = [P.sb(es, "zst%d" % i, [128, 512], F32) for i in range(4)]
        zero = P.sb(es, "zero", [128, 16], F32)
        pst = [P.ps(es, "pst%d" % i, [128, 8, 128], BF16) for i in range(2)]
        psz = [P.ps(es, "psz%d" % i, [128, 512], F32) for i in range(4)]
        w_in = self.inp["w_in"]
        P.dma(gcol[:, :], self.inp["norm_mix"].rearrange("(k p) -> p k", p=128), writes=[gcol],
              allow_slow_non_contiguous=True)
        self.memset("pool", zero[:, :], 0.0, [zero])
        P.dma(self.zT.t[0:14 * 128, 0:1].rearrange("(g p) o -> p (g o)", p=128), zero[:, 0:14],
              reads=[zero], writes=[self.zT], allow_slow_non_contiguous=True)
        for kc in range(8):
            st = stg[kc % 2]
            P.dma(st[:, :], w_in[kc * 128:(kc + 1) * 128, :], writes=[st])
            self.copy(["act", "dve", "pool"][kc % 3], win[:, kc, :], st[:, :], [st], [win])
        nsub = S // 128
        for ti in range(S // 512):
            h_T = hT[ti % 2]
            for sj in range(4):
                si = ti * 4 + sj
                x_t, h_b, s_s, p_t = xt[si % 2], hb[si % 2], ss[si % 2], pst[si % 2]
                P.dma(x_t[:, :], self.inp["x"][si * 128:(si + 1) * 128, :], writes=[x_t])
                self.act(junk[:, :], x_t[:, :], AF.Square, [x_t], [junk, s_s], accum=s_s[:, :])
                self.rsqrt(s_s[:, :], s_s[:, :], 1.0 / D, EPS, [s_s], [s_s])
                self.ts("dve", h_b[:, :], x_t[:, :], s_s[:, 0:1], ALU.mult, [x_t, s_s], [h_b])
                for kc in range(8):
                    self.tr(p_t[:, kc, :], h_b[:, kc * 128:(kc + 1) * 128], self.identb[:, :],
                            [h_b, self.identb], [p_t])
                self.tt("dve", h_T[:, :, sj * 128:(sj + 1) * 128], p_t[:, :, :],
                        bc(gcol[:, :].unsqueeze(2), [128, 8, 128]), ALU.mult, [p_t, gcol], [h_T])
            for g in range(NG):
                c0 = g * 128 if g < 19 else (2432 if g == 19 else 2464 + (g - 20) * 128)
                cw = 32 if g == 19 else 128
                pz = psz[g % 4]
                zs = zst[g % 4]
                for kc in range(8):
                    self.mm(pz[0:cw, :], win[:, kc, c0:c0 + cw], h_T[:, kc, :], kc == 0, kc == 7,
                            [win, h_T], [pz])
                if g >= 20:
                    self.act(zs[0:cw, :], pz[0:cw, :], AF.Sigmoid, [pz], [zs])
                else:
                    self.copy(self.ev_eng(), zs[0:cw, :], pz[0:cw, :], [pz], [zs])
                P.dma(self.zT.t[g * 128:g * 128 + cw, 1 + ti * 512:1 + (ti + 1) * 512], zs[0:cw, :],
                      reads=[zs], writes=[self.zT])
    P.barrier()


K.phase_a = _build_phase_a


def _build(self):
    nc, S = self.nc, self.S
    inp = self.din
    inp("x", [S, D]); inp("p", [S, 256]); inp("pos", [1, S], I32)
    inp("norm_mix", [D]); inp("w_in", [D, NCOL]); inp("rw_mu", [1792]); inp("rw_w0", [512])
    inp("rw_w2", [64, 512]); inp("rw_a0", [512]); inp("rw_a2", [64, 512]); inp("rw_g2", [128, 512])
    inp("rw_k_k", [512]); inp("rw_k_a", [512]); inp("rw_r_k", [512]); inp("rw_gn_w", [512])
    inp("rw_gn_b", [512]); inp("rw_w_o", [512, D]); inp("mla_q_norm", [384]); inp("mla_w_uq", [384, 768])
    inp("mla_kv_norm", [256]); inp("mla_w_ukv", [256, 1024]); inp("mla_w_o", [512, D]); inp("w_out", [D, D])
    inp("norm_ffn", [D]); inp("peer_w_q", [D, 2048]); inp("peer_sub_keys", [16, 128, 128])
    inp("peer_u", [16384, D]); inp("peer_v", [16384, D]); inp("norm_ple", [D]); inp("ple_w_gate", [D, D])
    inp("ple_w_proj", [256, D]); inp("norm_final", [D])
    for k, v in host_consts().items():
        inp("c_" + k, list(v.shape))
    self.out = nc.dram_tensor("out", [S, D], F32, kind="ExternalOutput").ap()
    self.zT = self.dscr("zT", [NG * 128, S + 1])
    self.ygT = self.dscr("ygT", [512, S], BF16)
    self.x1 = self.dscr("x1", [S, D])
    self.oTs = self.dscr("oTs", [512, S], BF16)
    self.wqs = self.dscr("wqs", [1024, 2048], BF16)
    self.uTs = self.dscr("uTs", [1024, 16384], BF16)
    self.vs = self.dscr("vs", [16384, 1024], BF16)
    if self.debug:
        self.dbg = {}
    with ExitStack() as es:
        self.P = P = Prog(nc, es)
        self.ident = P.sb(es, "ident", [128, 128], F32)
        self.identb = P.sb(es, "identb", [128, 128], BF16)
        self.blockones = P.sb(es, "blockones", [128, 128], F32)
        self.ones = P.sb(es, "onesf", [128, 128], F32)
        P.dma(self.ident[:, :], self.inp["c_ident"], writes=[self.ident])
        P.dma(self.blockones[:, :], self.inp["c_blockones"], writes=[self.blockones])
        P.dma(self.ones[:, :], self.inp["c_ones"], writes=[self.ones])
        self.copy("dve", self.identb[:, :], self.ident[:, :], [self.ident], [self.identb])
        P.barrier()
        self.phase_a()
        if self.debug != "a":
            self.phase_b()
        if self.debug not in ("a", "b"):
            self.phase_c()
        if self.debug not in ("a", "b", "c"):
            self.phase_d()
        if self.debug == "c":
            P.dma(self.out[:, :], self.x1.t[:, :], reads=[self.x1])
        if self.debug == "b":
            with ExitStack() as es2:
                d1 = P.sb(es2, "dbg1", [128, 4, 512], BF16)
                d2 = P.sb(es2, "dbg2", [128, 4, 512], F32)
                P.dma(d1[:, :, :], self.ygT.t[0:512, 0:512].rearrange("(g p) t -> p g t", p=128), reads=[self.ygT], writes=[d1])
                self.copy("dve", d2[:, :, :], d1[:, :, :], [d1], [d2])
                P.dma(self.out[0:512, 0:512].rearrange("(g p) t -> p g t", p=128), d2[:, :, :], reads=[d2])
                P.barrier()
        if self.debug == "a":
            P.dma(self.out[0:512, 0:512], self.zT.t[0:512, 1:513], reads=[self.zT], eng="sp")
            P.dma(self.out[0:512, 512:1024], self.zT.t[2560:3072, 1:513], reads=[self.zT], eng="sp")
        P.barrier()
        P.emit()
    return nc


K.build = _build


def make_inputs(S, b, x, p, positions, **w):
    m = {"x": np.ascontiguousarray(x[b]), "p": np.ascontiguousarray(p[0, b]),
         "pos": np.ascontiguousarray(positions[b].reshape(1, S).astype(np.int32))}
    for k, v in w.items():
        a = np.asarray(v)
        if k == "norm_final":
            m[k] = np.ascontiguousarray(a)
        elif k == "rw_r_k":
            m[k] = np.ascontiguousarray(a[0].reshape(512))
        elif k == "peer_sub_keys":
            m[k] = np.ascontiguousarray(a[0].reshape(16, 128, 128))
        else:
            m[k] = np.ascontiguousarray(a[0])
    for k, v in host_consts().items():
        m["c_" + k] = v
    return m


def kernel(x, p, positions, **w):
    x = np.asarray(x); p = np.asarray(p); positions = np.asarray(positions)
    B, S = x.shape[0], x.shape[1]
    kb = K(S)
    nc = kb.build()
    in_maps = [make_inputs(S, b, x, p, positions, **w) for b in range(B)]
    res = run_bass_kernel_spmd(nc, in_maps, core_ids=list(range(B)))
    return np.stack([r["out"] for r in res.results], axis=0).astype(np.float32)


def _col(self, es, name, src, ng):
    t = self.P.sb(es, name, [128, ng], F32)
    self.P.dma(t[:, :], self.inp[src].rearrange("(g p) -> p g", p=128), writes=[t],
               allow_slow_non_contiguous=True)
    return t


K.col = _col


def _build_phase_b(self):
    P, nc, S = self.P, self.nc, self.S
    with ExitStack() as es:
        sb = lambda n, sh, dt=F32: P.sb(es, n, sh, dt)
        mu = self.col(es, "mu", "rw_mu", 14)
        w0 = self.col(es, "w0c", "rw_w0", 4)
        a0 = self.col(es, "a0c", "rw_a0", 4)
        kkc = self.col(es, "kkc", "rw_k_k", 4)
        kac = self.col(es, "kac", "rw_k_a", 4)
        rkc = self.col(es, "rkc", "rw_r_k", 4)
        gnw = self.col(es, "gnw", "rw_gn_w", 4)
        gnb = self.col(es, "gnb", "rw_gn_b", 4)
        w2 = sb("w2", [64, 512]); P.dma(w2[:, :], self.inp["rw_w2"], writes=[w2])
        a2 = sb("a2", [128, 512]); P.dma(a2[64:128, :], self.inp["rw_a2"], writes=[a2])
        g2 = sb("g2", [128, 512]); P.dma(g2[:, :], self.inp["rw_g2"], writes=[g2])
        masks = sb("masks", [64, 3, 8, 64])
        P.dma(masks[:, :, :, :].rearrange("p a h t -> p (a h t)"), self.inp["c_masks"], writes=[masks])
        zin = sb("zin", [128, 14, 513])
        zs = sb("zs", [128, 14, 512])
        big = [sb("big%d" % i, [128, 4, 512]) for i in range(8)]
        tw = sb("tw", [64, 512]); sg = sb("sgz", [128, 512])
        gC = sb("gC", [128, 4, 8])
        Hc = sb("Hc", [128, 4, 64])
        Ht = sb("Ht", [128, 4, 64])
        ygo = sb("ygo", [128, 4, 512], BF16)
        c64 = lambda n: sb(n, [64, 8, 64])
        Np = [c64("Np0"), c64("Np1")]; Ntp = [c64("Ntp0"), c64("Ntp1")]
        AkT = c64("AkT"); ArbT = c64("ArbT"); ArkT = c64("ArkT")
        U = [c64("U0"), c64("U1")]
        Vtm = c64("Vtm"); Btm = sb("Btm", [64, 4, 128]); Ktm = sb("Ktm", [64, 4, 128])
        Ysb = c64("Ysb"); Ysq = c64("Ysq")
        st = sb("st", [64, 4, 8])
        eo = sb("eo", [128, 2])
        self.memset("dve", eo[:, :], 0.0, [eo])
        self.memset("dve", eo[0:64, 0:1], 1.0, [eo])
        self.memset("dve", eo[64:128, 1:2], 1.0, [eo])
        mk = {}
        for nm in ("b", "a", "k", "r"):
            mk[nm] = [sb("mk_%s%d" % (nm, i), [128, 4, 64]) for i in range(2)]
        self.pspool = [P.ps(es, "pb%d" % i, [128, 512], F32) for i in range(8)]
        self.psi = 0
        self.memset("dve", Hc[:, :, :], 0.0, [Hc])
        self.prep_alloc(es)
        prep_per_chunk = -(-128 // (S // 64))
        for ti in range(S // 512):
            t0 = ti * 512
            P.dma(zin[:, :, :], self.zT.t[0:1792, t0:t0 + 513].rearrange("(g p) t -> p g t", p=128),
                  reads=[self.zT], writes=[zin])
            self.tt("dve", zs[:, :, :], zin[:, :, 0:512], zin[:, :, 1:513], ALU.subtract, [zin], [zs])
            self.tt("pool", zs[:, :, :], zs[:, :, :], bc(mu[:, :].unsqueeze(2), [128, 14, 512]), ALU.mult,
                    [zs, mu], [zs])
            self.tt("dve", zs[:, :, :], zs[:, :, :], zin[:, :, 1:513], ALU.add, [zs, zin], [zs])
            r_, k_, v_ = zs[:, 0:4, :], zs[:, 4:8, :], zs[:, 8:12, :]
            A, B, C, Dd, E, Fb, G, H = big
            self.act(tw[:, :], zs[0:64, 12, :], AF.Tanh, [zs], [tw])
            for g in range(4):
                ps = self.next_ps()
                self.mm(ps[:, :], w2[:, g * 128:(g + 1) * 128], tw[:, :], True, True, [w2, tw], [ps])
                self.act(E[:, g, :], ps[:, :], AF.Sigmoid, [ps, w0], [E], bias=w0[:, g:g + 1])
            self.ts("pool", E[:, :, :], E[:, :, :], -math.exp(-0.5), ALU.mult, [E], [E])
            src = E
            pp = [A, B]
            k = 0
            for sh in (1, 2, 4, 8, 16, 32):
                dst = pp[k % 2]
                sv = src[:, :, :].rearrange("p g (c t) -> p (g c) t", t=64)
                dv = dst[:, :, :].rearrange("p g (c t) -> p (g c) t", t=64)
                self.tt("dve", dv[:, :, sh:64], sv[:, :, sh:64], sv[:, :, 0:64 - sh], ALU.add, [src], [dst])
                self.copy("pool", dv[:, :, 0:sh], sv[:, :, 0:sh], [src], [dst])
                src = dst
                k += 1
            X = src
            Y = A if X is B else B
            self.act(C[:, :, :], X[:, :, :], AF.Exp, [X], [C])
            self.act(Dd[:, :, :], X[:, :, :], AF.Exp, [X], [Dd], scale=-1.0)
            self.tt("dve", Y[:, :, :], X[:, :, :], E[:, :, :], ALU.subtract, [X, E], [Y])
            self.act(Y[:, :, :], Y[:, :, :], AF.Exp, [Y], [Y])
            self.copy("pool", gC[:, :, :], C[:, :, :].rearrange("p g (c t) -> p g c t", t=64)[:, :, :, 63],
                      [C], [gC])
            for g in range(4):
                self.ts("dve", Fb[:, g, :], k_[:, g, :], kkc[:, g:g + 1], ALU.mult, [zs, kkc], [Fb])
            self.tt("pool", G[:, :, :], Fb[:, :, :], Fb[:, :, :], ALU.mult, [Fb], [G])
            for g in range(4):
                ps = self.next_ps()
                self.mm(ps[:, :], self.blockones[:, :], G[:, g, :], True, True, [self.blockones, G], [ps])
                self.ts("dve", E[:, g, :], ps[:, :], 1e-24, ALU.add, [ps], [E])
            self.act(E[:, :, :], E[:, :, :], AF.Sqrt, [E], [E])
            P.op("dve", lambda e: e.reciprocal(out=E[:, :, :], in_=E[:, :, :]), [E], [E])
            self.tt("dve", Fb[:, :, :], Fb[:, :, :], E[:, :, :], ALU.mult, [Fb, E], [Fb])
            for g in range(4):
                ps = self.next_ps()
                self.mm(ps[:, :], a2[64:128, g * 128:(g + 1) * 128], zs[64:128, 12, :], True, True, [a2, zs], [ps])
                self.act(G[:, g, :], ps[:, :], AF.Sigmoid, [ps, a0], [G], bias=a0[:, g:g + 1])
            self.stt("dve", Y[:, :, :], Fb[:, :, :], -1.0, Y[:, :, :], ALU.mult, ALU.mult, [Fb, Y], [Y])
            self.tt("pool", H[:, :, :], Fb[:, :, :], G[:, :, :], ALU.mult, [Fb, G], [H])
            self.tt("dve", H[:, :, :], H[:, :, :], Dd[:, :, :], ALU.mult, [H, Dd], [H])
            for g in range(4):
                self.ts("dve", G[:, g, :], G[:, g, :], -1.0, ALU.add, [G, kac], [G], s2=kac[:, g:g + 1], op1=ALU.mult)
            self.stt("dve", G[:, :, :], G[:, :, :], 1.0, k_, ALU.add, ALU.mult, [G, zs], [G])
            self.tt("pool", Fb[:, :, :], r_, G[:, :, :], ALU.mult, [zs, G], [Fb])
            for g in range(4):
                self.ts("dve", Fb[:, g, :], Fb[:, g, :], rkc[:, g:g + 1], ALU.mult, [Fb, rkc], [Fb])
            for g in range(4):
                ps = self.next_ps()
                self.mm(ps[:, :], self.blockones[:, :], Fb[:, g, :], True, True, [self.blockones, Fb], [ps])
                self.tt("dve", E[:, g, :], ps[:, :], v_[:, g, :], ALU.mult, [ps, zs], [E])
            bonus = E
            self.tt("dve", Dd[:, :, :], Dd[:, :, :], G[:, :, :], ALU.mult, [Dd, G], [Dd])
            self.tt("pool", C[:, :, :], C[:, :, :], r_, ALU.mult, [C, zs], [C])
            self.act(sg[:, :], zs[:, 13, :], AF.Sigmoid, [zs], [sg])
            for g in range(4):
                ps = self.next_ps()
                self.mm(ps[:, :], g2[:, g * 128:(g + 1) * 128], sg[:, :], True, True, [g2, sg], [ps])
                self.copy("act", G[:, g, :], ps[:, :], [ps], [G])
            rt, at, bt, kt, ynT = C, Y, H, Dd, X
            for ch in range(8 if getattr(self, "stopb", 9) > 2 else 0):
                cs = slice(ch * 64, (ch + 1) * 64)
                for _ in range(prep_per_chunk):
                    if self.prep_done < 128:
                        self.prep_step()
                for (srcT, dstT, rd) in ((bt, Btm, [bt]), (kt, Ktm, [kt]), (v_, Vtm, [zs])):
                    ps = self.next_ps()
                    for g in range(4):
                        self.tr(ps[0:64, g * 128:(g + 1) * 128], srcT[:, g, cs], self.ident[:, :],
                                rd + [self.ident], [ps])
                    dv = dstT[:, :, :].rearrange("p a b -> p (a b)")
                    self.copy(self.ev_eng(), dv, ps[0:64, :], [ps], [dstT])
                for nm, srcT in (("b", bt), ("a", at), ("k", kt), ("r", rt)):
                    for e2 in range(2):
                        self.ts(["dve", "pool"][e2], mk[nm][e2][:, :, :], srcT[:, :, cs], eo[:, e2:e2 + 1], ALU.mult,
                                [srcT, eo], [mk[nm][e2]])

                def amat(dst, lT, lrd, rT, rrd, mi):
                    ps = self.next_ps()
                    for h in range(8):
                        self.mm(ps[0:64, h * 64:(h + 1) * 64], mk[lT][h % 2][:, h // 2, :], rT[:, h // 2, cs],
                                True, True, [mk[lT][h % 2], rrd], [ps])
                    self.tt("dve", dst[:, :, :], ps[0:64, :].rearrange("p (h t) -> p h t", h=8),
                            masks[:, mi, :, :], ALU.mult, [ps, masks], [dst])
                if getattr(self, "stopb", 9) == 3:
                    continue
                amat(Np[0], "b", bt, at, at, 0)
                amat(Ntp[0], "a", at, bt, bt, 2)
                amat(AkT, "k", kt, at, at, 0)
                amat(ArbT, "b", bt, rt, rt, 1)
                amat(ArkT, "k", kt, rt, rt, 1)
                if getattr(self, "stopb", 9) == 4:
                    continue
                ps = self.next_ps()
                for h in range(8):
                    o = ps[0:64, h * 64:(h + 1) * 64]
                    self.mm(o, mk["a"][h % 2][:, h // 2, :], Hc[:, h // 2, :], True, False, [mk["a"][h % 2], Hc], [ps])
                    self.mm(o, AkT[:, h, :], Vtm[:, h, :], False, True, [AkT, Vtm], [ps])
                self.copy("act", U[0][:, :, :], ps[0:64, :].rearrange("p (h t) -> p h t", h=8), [ps], [U[0]])
                cur = 0
                for lvl in range(6):
                    Nn, Nt = Np[lvl % 2], Ntp[lvl % 2]
                    ps = self.next_ps()
                    for h in range(8):
                        self.mm(ps[0:64, h * 64:(h + 1) * 64], Nn[:, h, :], U[cur][:, h, :], True, True,
                                [Nn, U[cur]], [ps])
                    self.tt("dve", U[1 - cur][:, :, :], U[cur][:, :, :],
                            ps[0:64, :].rearrange("p (h t) -> p h t", h=8), ALU.add, [ps, U[cur]], [U[1 - cur]])
                    cur = 1 - cur
                    if lvl < 5:
                        N2, Nt2 = Np[(lvl + 1) % 2], Ntp[(lvl + 1) % 2]
                        ps = self.next_ps()
                        for h in range(8):
                            self.mm(ps[0:64, h * 64:(h + 1) * 64], Nt[:, h, :], Nn[:, h, :], True, True,
                                    [Nt, Nn], [ps])
                        ps2 = None
                        if lvl < 4:
                            ps2 = self.next_ps()
                            for h in range(8):
                                self.mm(ps2[0:64, h * 64:(h + 1) * 64], Nn[:, h, :], Nt[:, h, :], True, True,
                                        [Nt, Nn], [ps2])
                        self.copy("act", N2[:, :, :], ps[0:64, :].rearrange("p (h t) -> p h t", h=8), [ps], [N2])
                        if ps2 is not None:
                            self.copy("pool" if False else "dve", Nt2[:, :, :],
                                      ps2[0:64, :].rearrange("p (h t) -> p h t", h=8), [ps2], [Nt2])
                Uf = U[cur]
                if getattr(self, "stopb", 9) == 5:
                    continue
                psy = self.next_ps()
                for h in range(8):
                    o = psy[0:64, h * 64:(h + 1) * 64]
                    self.mm(o, mk["r"][h % 2][:, h // 2, :], Hc[:, h // 2, :], True, False, [mk["r"][h % 2], Hc], [psy])
                    self.mm(o, ArbT[:, h, :], Uf[:, h, :], False, False, [ArbT, Uf], [psy])
                    self.mm(o, ArkT[:, h, :], Vtm[:, h, :], False, True, [ArkT, Vtm], [psy])
                psh = self.next_ps()
                for h in range(8):
                    o = psh[:, h * 64:(h + 1) * 64]
                    self.mm(o, Btm[:, h // 2, :], Uf[:, h, :], True, False, [Btm, Uf], [psh])
                    self.mm(o, Ktm[:, h // 2, :], Vtm[:, h, :], False, True, [Ktm, Vtm], [psh])
                phv = psh[:, :].rearrange("p (g e v) -> p g e v", g=4, e=2)
                for e2 in range(2):
                    rs = slice(e2 * 64, e2 * 64 + 64)
                    self.tt("dve", Ht[rs, :, :], Hc[rs, :, :], phv[rs, :, e2, :], ALU.add, [Hc, psh], [Ht])
                    self.tt("dve", Hc[rs, :, :], Ht[rs, :, :], bc(gC[rs, :, ch:ch + 1], [64, 4, 64]), ALU.mult,
                            [Ht, gC], [Hc])
                if getattr(self, "stopb", 9) == 6:
                    continue
                yv = psy[0:64, :].rearrange("p (h t) -> p h t", h=8)
                self.copy("act", Ysb[:, :, :], yv, [psy], [Ysb])
                self.act(Ysq[:, :, :], yv, AF.Square, [psy], [Ysq])
                P.op("dve", lambda e: e.reduce_sum(out=st[:, 0, :], in_=Ysb[:, :, :], axis=AX.X), [Ysb], [st])
                P.op("dve", lambda e: e.reduce_sum(out=st[:, 1, :], in_=Ysq[:, :, :], axis=AX.X), [Ysq], [st])
                self.ts("dve", st[:, 0, :], st[:, 0, :], 1.0 / 64, ALU.mult, [st], [st])
                self.tt("dve", st[:, 2, :], st[:, 0, :], st[:, 0, :], ALU.mult, [st], [st])
                self.stt("dve", st[:, 1, :], st[:, 1, :], 1.0 / 64, st[:, 2, :], ALU.mult, ALU.subtract, [st], [st])
                self.ts("dve", st[:, 1, :], st[:, 1, :], GN_EPS, ALU.add, [st], [st])
                self.act(st[:, 1, :], st[:, 1, :], AF.Sqrt, [st], [st])
                P.op("dve", lambda e: e.reciprocal(out=st[:, 1, :], in_=st[:, 1, :]), [st], [st])
                self.tt("dve", Ysb[:, :, :], Ysb[:, :, :], bc(st[:, 0, :].unsqueeze(2), [64, 8, 64]), ALU.subtract,
                        [Ysb, st], [Ysb])
                self.tt("dve", Ysb[:, :, :], Ysb[:, :, :], bc(st[:, 1, :].unsqueeze(2), [64, 8, 64]), ALU.mult,
                        [Ysb, st], [Ysb])
                ps = self.next_ps()
                yf = Ysb[:, :, :].rearrange("p h v -> p (h v)")
                for g in range(4):
                    self.tr(ps[:, g * 64:(g + 1) * 64], yf[:, g * 128:(g + 1) * 128], self.ident[0:64, 0:64],
                            [Ysb, self.ident], [ps])
                for g in range(4):
                    self.act(ynT[:, g, cs], ps[:, g * 64:(g + 1) * 64], AF.Identity, [ps, gnw, gnb], [ynT],
                             bias=gnb[:, g:g + 1], scale=gnw[:, g:g + 1])
            self.tt("dve", ynT[:, :, :], ynT[:, :, :], bonus[:, :, :], ALU.add, [ynT, bonus], [ynT])
            self.tt("dve", ygo[:, :, :], ynT[:, :, :], G[:, :, :], ALU.mult, [ynT, G], [ygo])
            P.dma(self.ygT.t[:, t0:t0 + 512].rearrange("(g p) t -> p g t", p=128), ygo[:, :, :],
                  reads=[ygo], writes=[self.ygT])
            P.barrier()
    P.barrier()


K.phase_b = _build_phase_b


def _loadw(self, es, name, src_ap, rows, cols, stg):
    P = self.P
    nk = rows // 128
    t = name if isinstance(name, TT) else P.sb(es, name, [128, nk, cols], BF16)
    for kc in range(nk):
        st = stg[kc % 2]
        P.dma(st[:, 0:cols], src_ap[kc * 128:(kc + 1) * 128, :], writes=[st])
        self.copy(["act", "dve", "pool"][kc % 3], t[:, kc, :], st[:, 0:cols], [st], [t])
    return t


K.loadw = _loadw

MAGIC = 12582912.0
TWO_PI_1 = 6.28125
TWO_PI_2 = 2.0 * math.pi - 6.28125


def _build_phase_c(self):
    P, nc, S = self.P, self.nc, self.S
    NT = S // 512
    with ExitStack() as es:
        sb = lambda n, sh, dt=F32: P.sb(es, n, sh, dt)
        stg = [sb("cstg%d" % i, [128, 1024]) for i in range(2)]
        wuq = self.loadw(es, "wuq", self.inp["mla_w_uq"], 384, 768, stg)
        wkv = self.loadw(es, "wkv", self.inp["mla_w_ukv"], 256, 1024, stg)
        wrot = sb("wrot", [128, 3, 8, 96], BF16)
        self.memset("pool", wrot[:, :, :, :], 0.0, [wrot])
        wq4 = wuq[:, :, :].rearrange("p c (h e) -> p c h e", h=8)
        self.copy("dve", wrot[:, :, :, 64:80], wq4[:, :, :, 80:96], [wuq], [wrot])
        self.copy("dve", wrot[:, :, :, 80:96], wq4[:, :, :, 64:80], [wuq], [wrot])
        wk4 = wkv[:, :, :].rearrange("p c (h e) -> p c h e", h=8)
        wv = sb("wv", [128, 2, 8, 64], BF16)
        self.copy("dve", wv[:, :, :, :], wk4[:, :, :, 64:128], [wkv], [wv])
        qn = self.col(es, "qn", "mla_q_norm", 3)
        kvn = self.col(es, "kvn", "mla_kv_norm", 2)
        ropec = sb("ropec", [128, 4]); P.dma(ropec[:, :], self.inp["c_ropec"], writes=[ropec])
        onesb = sb("onesb", [128, 128], BF16)
        self.copy("dve", onesb[:, :], self.ones[:, :], [self.ones], [onesb])
        Kf = [sb("Kf%d" % h, [96, S], BF16) for h in range(8)]
        Vtm = sb("Vtm_a", [128, S // 128, 512], BF16)
        Qf = sb("Qf", [96, 8, 512], BF16)
        zc = sb("zc", [128, 3, 512]); zsq = sb("zsq", [128, 3, 512]); rstd = sb("rstd_c", [128, 512])
        cn = sb("cn", [128, 3, 512], BF16)
        posi = sb("posi", [96, 512], I32); ang = sb("ang", [96, 512]); kq = sb("kq", [96, 512])
        Ct = sb("Ct", [96, 512]); St = sb("St", [96, 512])
        kr = sb("kr", [96, 512]); krot = sb("krot", [96, 512]); krb = sb("krb", [96, 512], BF16)
        qa = sb("qa", [96, 512]); qb = sb("qb", [96, 512])
        Pt = [sb("Pt%d" % i, [128, 512], BF16) for i in range(3)]
        rec = sb("rec", [128, 512])
        oT = sb("oT", [128, 4, 512], BF16)
        self.pspool = [P.ps(es, "pc%d" % i, [128, 512], F32) for i in range(4)]
        self.psi = 0
        psO = [P.ps(es, "pO%d" % i, [128, 512], F32) for i in range(2)]
        psS = [P.ps(es, "pS%d" % i, [128, 512], F32) for i in range(2)]
        scale = 1.0 / math.sqrt(96.0)

        def rmsn(groups, ng, gcolt, nfeat):
            self.tt("pool", zsq[:, 0:ng, :], zc[:, 0:ng, :], zc[:, 0:ng, :], ALU.mult, [zc], [zsq])
            ps = self.next_ps()
            for c in range(ng):
                self.mm(ps[:, :], self.ones[:, :], zsq[:, c, :], c == 0, c == ng - 1, [self.ones, zsq], [ps])
            self.rsqrt(rstd[:, :], ps[:, :], 1.0 / nfeat, EPS, [ps], [rstd])
            for c in range(ng):
                self.stt("dve", cn[:, c, :], zc[:, c, :], gcolt[:, c:c + 1], rstd[:, :], ALU.mult, ALU.mult,
                         [zc, gcolt, rstd], [cn])

        for T in range(NT):
            t0 = T * 512
            P.dma(posi[:, :], self.inp["pos"][0:1, t0:t0 + 512].partition_broadcast(96), writes=[posi])
            self.copy("dve", ang[:, :], posi[:, :], [posi], [ang])
            self.ts("dve", ang[:, :], ang[:, :], ropec[0:96, 0:1], ALU.mult, [ang, ropec], [ang])
            self.ts("dve", kq[:, :], ang[:, :], 1.0 / (2 * math.pi), ALU.mult, [ang], [kq], s2=MAGIC, op1=ALU.add)
            self.ts("dve", kq[:, :], kq[:, :], -MAGIC, ALU.add, [kq], [kq])
            self.stt("dve", ang[:, :], kq[:, :], -TWO_PI_1, ang[:, :], ALU.mult, ALU.add, [kq, ang], [ang])
            self.stt("dve", ang[:, :], kq[:, :], -TWO_PI_2, ang[:, :], ALU.mult, ALU.add, [kq, ang], [ang])
            self.ts("dve", ang[:, :], ang[:, :], 3.14159, ALU.min, [ang], [ang], s2=-3.14159, op1=ALU.max)
            self.act(St[:, :], ang[:, :], AF.Sin, [ang], [St])
            self.ts("dve", St[:, :], St[:, :], ropec[0:96, 1:2], ALU.mult, [St, ropec], [St])
            self.act(kq[:, :], ang[:, :], AF.Abs, [ang], [kq])
            self.act(Ct[:, :], kq[:, :], AF.Sin, [kq, ropec], [Ct], bias=ropec[0:96, 2:3], scale=-1.0)
            P.dma(zc[:, 0:2, :], self.zT.t[17 * 128:19 * 128, 1 + t0:1 + t0 + 512].rearrange("(g p) t -> p g t", p=128),
                  reads=[self.zT], writes=[zc])
            rmsn(2, 2, kvn, 256.0)
            for h in range(8):
                ps = self.next_ps()
                for c in range(2):
                    self.mm(ps[0:64, :], wk4[:, c, h, 0:64], cn[:, c, :], c == 0, c == 1, [wkv, cn], [ps])
                self.copy(self.ev_eng(), Kf[h][0:64, t0:t0 + 512], ps[0:64, :], [ps], [Kf[h]])
            rz = 19 * 128
            P.dma(kr[64:96, :], self.zT.t[rz:rz + 32, 1 + t0:1 + t0 + 512], reads=[self.zT], writes=[kr])
            P.dma(krot[64:80, :], self.zT.t[rz + 16:rz + 32, 1 + t0:1 + t0 + 512], reads=[self.zT], writes=[krot])
            P.dma(krot[80:96, :], self.zT.t[rz:rz + 16, 1 + t0:1 + t0 + 512], reads=[self.zT], writes=[krot])
            self.tt("dve", kr[64:96, :], kr[64:96, :], Ct[64:96, :], ALU.mult, [kr, Ct], [kr])
            self.tt("dve", krot[64:96, :], krot[64:96, :], St[64:96, :], ALU.mult, [krot, St], [krot])
            self.tt("dve", krb[64:96, :], kr[64:96, :], krot[64:96, :], ALU.add, [kr, krot], [krb])
            for h in range(8):
                self.copy(["dve", "pool"][h % 2], Kf[h][64:96, t0:t0 + 512], krb[64:96, :], [krb], [Kf[h]])
            for b4 in range(4):
                ps = self.next_ps()
                for c in range(2):
                    self.mm(ps[:, :], cn[:, c, b4 * 128:(b4 + 1) * 128], wv[:, c, :, :].rearrange("p h e -> p (h e)"),
                            c == 0, c == 1, [cn, wv], [ps])
                self.copy(self.ev_eng(), Vtm[:, T * 4 + b4, :], ps[:, :], [ps], [Vtm])
            P.dma(zc[:, 0:3, :], self.zT.t[14 * 128:17 * 128, 1 + t0:1 + t0 + 512].rearrange("(g p) t -> p g t", p=128),
                  reads=[self.zT], writes=[zc])
            rmsn(3, 3, qn, 384.0)
            for h in range(8):
                ps = self.next_ps()
                ps2 = self.next_ps()
                for c in range(3):
                    self.mm(ps[0:96, :], wuq[:, c, h * 96:(h + 1) * 96], cn[:, c, :], c == 0, c == 2, [wuq, cn], [ps])
                for c in range(3):
                    self.mm(ps2[0:96, :], wrot[:, c, h, :], cn[:, c, :], c == 0, c == 2, [wrot, cn], [ps2])
                self.tt("dve", qa[:, :], ps[0:96, :], Ct[:, :], ALU.mult, [ps, Ct], [qa])
                self.tt("dve", qb[:, :], ps2[0:96, :], St[:, :], ALU.mult, [ps2, St], [qb])
                self.tt("pool", Qf[:, h, :], qa[:, :], qb[:, :], ALU.add, [qa, qb], [Qf])
            pi = 0
            for h in range(8):
                pO, pS = psO[h % 2], psS[h % 2]
                nkb = 4 * T + 4
                for kb in range(nkb):
                    nq0 = max(0, kb - 4 * T)
                    cl = slice(nq0 * 128, 512)
                    ps = self.next_ps()
                    self.mm(ps[:, cl], Kf[h][0:96, kb * 128:(kb + 1) * 128], Qf[0:96, h, cl], True, True,
                            [Kf[h], Qf], [ps])
                    pt = Pt[pi % 3]
                    pi += 1
                    self.act(pt[:, cl], ps[:, cl], AF.Exp, [ps], [pt], scale=scale)
                    if kb >= 4 * T:
                        self.memset("pool", pt[64:128, nq0 * 128:nq0 * 128 + 64], 0.0, [pt])
                    hp2 = (h // 2) * 128
                    self.mm(pO[:, cl], Vtm[:, kb, hp2:hp2 + 128], pt[:, cl], kb == 0, kb == nkb - 1, [Vtm, pt], [pO])
                    self.mm(pS[:, cl], onesb[:, :], pt[:, cl], kb == 0, kb == nkb - 1, [onesb, pt], [pS])
                P.op("dve", lambda e, o=rec[:, :], i=pS[:, :]: e.reciprocal(out=o, in_=i), [pS], [rec])
                rs = slice((h % 2) * 64, (h % 2) * 64 + 64)
                self.tt("dve", oT[rs, h // 2, :], pO[rs, :], rec[rs, :], ALU.mult, [pO, rec], [oT])
            P.dma(self.oTs.t[:, t0:t0 + 512].rearrange("(g p) t -> p g t", p=128), oT[:, :, :],
                  reads=[oT], writes=[self.oTs])
            P.barrier()
    P.barrier()
    with ExitStack() as es:
        sb = lambda n, sh, dt=F32: P.sb(es, n, sh, dt)
        wo_m = P.sb(es, "wo_m", [128, 4, 1024], BF16)
        wo_r = P.sb(es, "wo_r", [128, 4, 1024], BF16)
        w_o = P.sb(es, "w_o", [128, 8, 1024], BF16)
        stg = [sb("c2stg%d" % i, [128, 1024]) for i in range(2)]
        self.loadw(es, wo_m, self.inp["mla_w_o"], 512, 1024, stg)
        self.loadw(es, wo_r, self.inp["rw_w_o"], 512, 1024, stg)
        self.loadw(es, w_o, self.inp["w_out"], 1024, 1024, stg)
        oT = sb("oT2", [128, 4, 512], BF16)
        ygl = sb("ygl", [128, 4, 512], BF16)
        gA = [sb("gA%d" % i, [128, 512]) for i in range(2)]
        gB = [sb("gB%d" % i, [128, 512]) for i in range(2)]
        ta = sb("ta", [128, 512]); tb = sb("tb", [128, 512])
        mix = sb("mix", [128, 8, 512], BF16)
        xl = [sb("xl%d" % i, [128, 1024]) for i in range(2)]
        self.pspool = [P.ps(es, "pc2_%d" % i, [128, 512], F32) for i in range(6)]
        self.psi = 0
        for T in range(NT):
            t0 = T * 512
            P.dma(oT[:, :, :], self.oTs.t[:, t0:t0 + 512].rearrange("(g p) t -> p g t", p=128),
                  reads=[self.oTs], writes=[oT])
            P.dma(ygl[:, :, :], self.ygT.t[:, t0:t0 + 512].rearrange("(g p) t -> p g t", p=128),
                  reads=[self.ygT], writes=[ygl])
            for j in range(8):
                ga, gb = gA[j % 2], gB[j % 2]
                P.dma(ga[:, :], self.zT.t[(20 + j) * 128:(21 + j) * 128, 1 + t0:1 + t0 + 512], reads=[self.zT], writes=[ga])
                P.dma(gb[:, :], self.zT.t[(28 + j) * 128:(29 + j) * 128, 1 + t0:1 + t0 + 512], reads=[self.zT], writes=[gb])
                ps = self.next_ps()
                ps2 = self.next_ps()
                for c in range(4):
                    self.mm(ps[:, :], wo_r[:, c, j * 128:(j + 1) * 128], ygl[:, c, :], c == 0, c == 3, [wo_r, ygl], [ps])
                for c in range(4):
                    self.mm(ps2[:, :], wo_m[:, c, j * 128:(j + 1) * 128], oT[:, c, :], c == 0, c == 3, [wo_m, oT], [ps2])
                self.tt("dve", ta[:, :], ps[:, :], ga[:, :], ALU.mult, [ps, ga], [ta])
                self.tt("dve", tb[:, :], ps2[:, :], gb[:, :], ALU.mult, [ps2, gb], [tb])
                self.tt("pool", mix[:, j, :], ta[:, :], tb[:, :], ALU.add, [ta, tb], [mix])
            for b4 in range(4):
                x_l = xl[b4 % 2]
                r0 = t0 + b4 * 128
                P.dma(x_l[:, :], self.inp["x"][r0:r0 + 128, :], writes=[x_l])
                for hf in range(2):
                    ps = self.next_ps()
                    for j in range(8):
                        self.mm(ps[:, :], mix[:, j, b4 * 128:(b4 + 1) * 128], w_o[:, j, hf * 512:(hf + 1) * 512],
                                j == 0, j == 7, [mix, w_o], [ps])
                    self.tt("dve", x_l[:, hf * 512:(hf + 1) * 512], x_l[:, hf * 512:(hf + 1) * 512], ps[:, :], ALU.add,
                            [x_l, ps], [x_l])
                P.dma(self.x1.t[r0:r0 + 128, :], x_l[:, :], reads=[x_l], writes=[self.x1])
    P.barrier()


K.phase_c = _build_phase_c


def _build_phase_d(self):
    P, nc, S = self.P, self.nc, self.S
    NEG = -1e30
    with ExitStack() as es:
        sb = lambda n, sh, dt=F32: P.sb(es, n, sh, dt)
        stg = [sb("dstg%d" % i, [128, 1024]) for i in range(2)]
        self.pspool = [P.ps(es, "pdp%d" % i, [128, 512], F32) for i in range(4)]
        self.psi = 0
        if self.prep_done < 128:
            self.prep_alloc(es)
            while self.prep_done < 128:
                self.prep_step()
    P.barrier()
    with ExitStack() as es:
        sb = lambda n, sh, dt=F32: P.sb(es, n, sh, dt)
        self.pspool = [P.ps(es, "pd%d" % i, [128, 512], F32) for i in range(3)]
        self.psi = 0
        psOut = [[P.ps(es, "po%d%d" % (a, b), [128, 512], F32) for b in range(2)] for a in range(2)]
        uni = P.sb(es, "uni", [128, 8192], F32)
        wq = TT(uni[:, :].bitcast(BF16).rearrange("p (k c) -> p k c", k=8), P.buf("wqv"))
        wpg = P.sb(es, "wpg", [128, 8, 1024], BF16)
        wpp = P.sb(es, "wpp", [128, 2, 1024], BF16)
        skT = sb("skT", [128, 16, 128], BF16)
        with ExitStack() as es3:
            stg = [P.sb(es3, "dstg2_%d" % i, [128, 2048], F32) for i in range(2)]
            self.loadw(es, wq, self.inp["peer_w_q"], 1024, 2048, stg)
            P.dma(self.wqs.t[:, :].rearrange("(k p) c -> p k c", p=128), wq[:, :, :], reads=[wq], writes=[self.wqs])
            self.loadw(es, wpg, self.inp["ple_w_gate"], 1024, 1024, stg)
            self.loadw(es, wpp, self.inp["ple_w_proj"], 256, 1024, stg)
            self._skt(stg, skT)
            P.barrier()
        for hc in range(0):
            st = stg[hc % 2]
            P.dma(st[:, 0:128], self.inp["peer_sub_keys"][hc], writes=[st])
            ps = self.next_ps()
            self.tr(ps[:, 0:128], st[:, 0:128], self.ident[:, :], [st, self.ident], [ps])
            self.copy("dve", skT[:, hc, :], ps[:, 0:128], [ps], [skT])
        gf = self.col(es, "gf", "norm_ffn", 8)
        gp = self.col(es, "gp", "norm_ple", 8)
        gfin = sb("gfin", [128, 1024])
        P.dma(gfin[:, :], self.inp["norm_final"].rearrange("(o d) -> o d", o=1).partition_broadcast(128), writes=[gfin])
        x1t = [sb("x1t%d" % i, [128, 1024]) for i in range(2)]
        hb = sb("hb_d", [128, 1024], BF16)
        junk = hb
        ssd = sb("ssd", [128, 1])
        xnT = sb("xnT", [128, 8, 256], BF16)
        xn2T = sb("xn2T", [128, 8, 128], BF16)
        qT = sb("qT", [128, 16, 256], BF16)
        sS = [sb("sS%d" % i, [128, 16, 128]) for i in range(2)]
        s1pp = [sb("s1pp%d" % i, [128, 8, 128]) for i in range(2)]
        bE = [sb("bE%d" % i, [128, 8]) for i in range(2)]
        a16 = sb("a16", [128, 16]); b16 = sb("b16", [128, 16]); c16 = sb("c16", [128, 16]); e16 = sb("e16", [128, 16])
        tmpk = sb("tmpk", [128, 128]); cand = sb("cand", [128, 256]); cand2 = sb("cand2", [128, 256])
        sc = sb("scal", [128, 8])
        ubk = [sb("ubk%d" % i, [128, 8, 512], BF16) for i in range(2)]
        vbk = [sb("vbk%d" % i, [128, 4, 1024], BF16) for i in range(3)]
        dl = [TT(uni[:, i * 2048:(i + 1) * 2048].rearrange("p (a b c) -> p a b c", a=4, b=4), P.buf("dl%d" % i))
              for i in range(4)]
        Ee = [sb("Ee%d" % i, [128, 4, 4, 128], BF16) for i in range(4)]
        Gh = [sb("Gh%d" % i, [128, 4, 4, 128], BF16) for i in range(4)]
        dg = [sb("dg%d" % i, [128, 8, 128], BF16) for i in range(2)]
        wE = sb("wE", [128, 8])
        gel = [sb("gel%d" % i, [128, 512], BF16) for i in range(3)]
        Pm = [sb("Pm%d" % i, [128, 512], BF16) for i in range(3)]
        PTs = [sb("PTs%d" % i, [128, 4, 128], BF16) for i in range(3)]
        pst = P.ps(es, "pstd", [128, 8, 128], BF16)
        pl = sb("pl", [128, 256]); plb = sb("plb", [128, 256], BF16); pT = sb("pT", [128, 2, 128], BF16)
        gt = sb("gt", [128, 512])
        ib = self.identb

        def norm_T(xsrc, gcolt, dstT, csl):
            self.act(junk[:, :], xsrc[:, :], AF.Square, [xsrc], [junk, ssd], accum=ssd[:, :])
            self.rsqrt(ssd[:, :], ssd[:, :], 1.0 / D, EPS, [ssd], [ssd])
            self.ts("dve", hb[:, :], xsrc[:, :], ssd[:, 0:1], ALU.mult, [xsrc, ssd], [hb])
            for kc in range(8):
                self.tr(pst[:, kc, :], hb[:, kc * 128:(kc + 1) * 128], ib[:, :], [hb, ib], [pst])
            self.tt("dve", dstT[:, :, csl], pst[:, :, :], bc(gcolt[:, :].unsqueeze(2), [128, 8, 128]), ALU.mult,
                    [pst, gcolt], [dstT])

        def top16(dst, src, srcT, tmp, tmpT):
            P.op("dve", lambda e, o=dst[:, 0:8], i=src: e.max(out=o, in_=i), [dst, srcT], [dst])
            P.op("dve", lambda e, o=tmp, r=dst[:, 0:8], i=src: e.match_replace(out=o, in_to_replace=r, in_values=i,
                                                                                imm_value=NEG), [dst, srcT], [tmpT])
            P.op("dve", lambda e, o=dst[:, 8:16], i=tmp: e.max(out=o, in_=i), [tmpT], [dst])

        gi = 0
        mk = 0
        for tl in range(S // 256):
            for sub in range(2):
                r0 = tl * 256 + sub * 128
                P.dma(x1t[sub][:, :], self.x1.t[r0:r0 + 128, :], reads=[self.x1], writes=[x1t[sub]])
            P.dma(wq[:, :, :], self.wqs.t[:, :].rearrange("(k p) c -> p k c", p=128), reads=[self.wqs], writes=[wq])
            for sub in range(2):
                norm_T(x1t[sub], gf, xnT, slice(sub * 128, (sub + 1) * 128))
            for hc in range(16):
                ps = self.next_ps()
                for kc in range(8):
                    self.mm(ps[:, 0:256], wq[:, kc, hc * 128:(hc + 1) * 128], xnT[:, kc, :], kc == 0, kc == 7,
                            [wq, xnT], [ps])
                self.copy(self.ev_eng(), qT[:, hc, :], ps[:, 0:256], [ps], [qT])
            for sub in range(2):
                for g4 in range(4):
                    ps = self.next_ps()
                    for i4 in range(4):
                        hc = g4 * 4 + i4
                        self.mm(ps[:, i4 * 128:(i4 + 1) * 128], qT[:, hc, sub * 128:(sub + 1) * 128], skT[:, hc, :],
                                True, True, [qT, skT], [ps])
                    self.copy(self.ev_eng(), sS[sub][:, g4 * 4:(g4 + 1) * 4, :],
                              ps[:, :].rearrange("p (a n) -> p a n", a=4), [ps], [sS[sub]])
                for h in range(8):
                    s1 = sS[sub][:, 2 * h, :]
                    s2 = sS[sub][:, 2 * h + 1, :]
                    top16(a16, s1, sS[sub], tmpk[:, :], tmpk)
                    top16(b16, s2, sS[sub], tmpk[:, :], tmpk)
                    self.tt("dve", cand[:, :].rearrange("p (a b) -> p a b", a=16),
                            bc(a16[:, :].unsqueeze(2), [128, 16, 16]), bc(b16[:, :].unsqueeze(1), [128, 16, 16]),
                            ALU.add, [a16, b16], [cand])
                    top16(c16, cand[:, :], cand, cand2[:, :], cand2)
                    P.op("dve", lambda e, o=sc[:, 0:1], i=c16[:, :]: e.tensor_reduce(out=o, in_=i, axis=AX.X, op=ALU.min),
                         [c16], [sc])
                    P.op("dve", lambda e, o=sc[:, 1:2], i=c16[:, :]: e.tensor_reduce(out=o, in_=i, axis=AX.X, op=ALU.max),
                         [c16], [sc])
                    self.ts("dve", sc[:, 0:1], sc[:, 0:1], -2e-6, ALU.add, [sc], [sc])
                    self.ts("dve", sc[:, 2:3], sc[:, 1:2], -1.0, ALU.mult, [sc], [sc])
                    self.act(e16[:, :], c16[:, :], AF.Exp, [c16, sc], [e16, sc], bias=sc[:, 2:3], accum=sc[:, 3:4])
                    self.act(sc[:, 4:5], sc[:, 3:4], AF.Ln, [sc], [sc])
                    self.tt("dve", sc[:, 5:6], sc[:, 0:1], sc[:, 1:2], ALU.subtract, [sc], [sc])
                    self.tt("dve", bE[sub][:, h:h + 1], sc[:, 5:6], sc[:, 4:5], ALU.subtract, [sc], [bE[sub]])
                    self.ts("dve", s1pp[sub][:, h, :], s1, sc[:, 0:1], ALU.subtract, [sS[sub], sc], [s1pp[sub]])
                self.act(wE[:, :], bE[sub][:, :], AF.Exp, [bE[sub]], [wE])
                self.tt("dve", dg[sub][:, :, :], bc(ib[:, :].unsqueeze(1), [128, 8, 128]),
                        bc(wE[:, :].unsqueeze(2), [128, 8, 128]), ALU.mult, [ib, wE], [dg[sub]])
            NK = 64
            P.barrier()

            def ldblk(blk):
                ub, vb = ubk[blk % 2], vbk[blk % 3]
                P.dma(ub[:, :, :], self.uTs.t[:, blk * 512:(blk + 1) * 512].rearrange("(k p) e -> p k e", p=128),
                      reads=[self.uTs], writes=[ub])
                P.dma(vb[:, :, :], self.vs.t[blk * 512:(blk + 1) * 512, :].rearrange("(c p) d -> p c d", p=128),
                      reads=[self.vs], writes=[vb])

            ldblk(0)

            def S1(k):
                blk, sub = k // 2, k % 2
                ub = ubk[blk % 2]
                if sub == 0 and blk + 1 < 32:
                    ldblk(blk + 1)
                psA = self.next_ps()
                for kc in range(8):
                    self.mm(psA[:, :], xnT[:, kc, sub * 128:(sub + 1) * 128], ub[:, kc, :], kc == 0, kc == 7,
                            [xnT, ub], [psA])
                for hg in range(2):
                    d_, e_, g_ = dl[(k % 2) * 2 + hg], Ee[(k % 2) * 2 + hg], Gh[(k % 2) * 2 + hg]
                    self.stt("dve", g_[:, :, :, :], d_[:, :, :, :], 0.0, e_[:, :, :, :], ALU.is_ge, ALU.mult,
                             [d_, e_], [g_])
                self.psA_k[k % 2] = psA

            def S0(k):
                blk, sub = k // 2, k % 2
                for hg in range(2):
                    d_ = dl[(k % 2) * 2 + hg]
                    s2v = sS[sub][:, :, :].rearrange("p (h c) n -> p h c n", c=2)[:, hg * 4:(hg + 1) * 4, 1, :]
                    self.tt("dve", d_[:, :, :, :], bc(s2v.unsqueeze(2), [128, 4, 4, 128]),
                            bc(s1pp[sub][:, hg * 4:(hg + 1) * 4, blk * 4:(blk + 1) * 4].unsqueeze(3), [128, 4, 4, 128]),
                            ALU.add, [sS[sub], s1pp[sub]], [d_])
                for hg in range(2):
                    d_, e_ = dl[(k % 2) * 2 + hg], Ee[(k % 2) * 2 + hg]
                    self.act(e_[:, :, :, :], d_[:, :, :, :], AF.Exp, [d_], [e_])

            def S1b(k):
                psA = self.psA_k[k % 2]
                self.act(gel[k % 3][:, :], psA[:, :], AF.Gelu, [psA], [gel[k % 3]])

            def S2(k):
                sub = k % 2
                psM = self.next_ps()
                for hg in range(2):
                    g_ = Gh[(k % 2) * 2 + hg]
                    for h4 in range(4):
                        h = hg * 4 + h4
                        self.mm(psM[:, :], dg[sub][:, h, :], g_[:, h4, :, :].rearrange("p a b -> p (a b)"),
                                h == 0, h == 7, [dg[sub], g_], [psM])
                self.psM_k[k % 2] = psM

            def S2b(k):
                psM = self.psM_k[k % 2]
                self.tt("dve", Pm[k % 3][:, :], gel[k % 3][:, :], psM[:, :], ALU.mult, [gel[k % 3], psM], [Pm[k % 3]])

            def S3(k):
                pm = Pm[k % 3]
                for ec in range(4):
                    self.tr(pst[:, ec, :], pm[:, ec * 128:(ec + 1) * 128], ib[:, :], [pm, ib], [pst])
                self.copy("act", PTs[k % 3][:, :, :], pst[:, 0:4, :], [pst], [PTs[k % 3]])

            def S4(k):
                blk, sub = k // 2, k % 2
                vb = vbk[blk % 3]
                pts = PTs[k % 3]
                for hf in range(2):
                    for ec in range(4):
                        self.mm(psOut[sub][hf][:, :], pts[:, ec, :], vb[:, ec, hf * 512:(hf + 1) * 512],
                                blk == 0 and ec == 0, blk == 31 and ec == 3, [pts, vb], [psOut[sub][hf]])

            self.psA_k = [None, None]
            self.psM_k = [None, None]
            S0(0)
            for r in range(NK + 3):
                if 0 <= r - 3 < NK:
                    S4(r - 3)
                if r + 1 < NK:
                    S0(r + 1)
                if r < NK:
                    S1(r)
                if 0 <= r - 1 < NK:
                    S2(r - 1)
                if 0 <= r - 2 < NK:
                    S3(r - 2)
                if r < NK:
                    S1b(r)
                if 0 <= r - 1 < NK:
                    S2b(r - 1)
            for sub in range(2):
                r0 = tl * 256 + sub * 128
                xx = x1t[sub]
                for hf in range(2):
                    self.tt("dve", xx[:, hf * 512:(hf + 1) * 512], xx[:, hf * 512:(hf + 1) * 512], psOut[sub][hf][:, :],
                            ALU.add, [xx, psOut[sub][hf]], [xx])
                norm_T(xx, gp, xn2T, slice(0, 128))
                P.dma(pl[:, :], self.inp["p"][r0:r0 + 128, :], writes=[pl])
                self.copy("pool", plb[:, :], pl[:, :], [pl], [plb])
                for c in range(2):
                    self.tr(pst[:, c, :], plb[:, c * 128:(c + 1) * 128], ib[:, :], [plb, ib], [pst])
                self.copy("act", pT[:, :, :], pst[:, 0:2, :], [pst], [pT])
                for hf in range(2):
                    hs = slice(hf * 512, (hf + 1) * 512)
                    ps = self.next_ps()
                    for kc in range(8):
                        self.mm(ps[:, :], xn2T[:, kc, :], wpg[:, kc, hs], kc == 0, kc == 7, [xn2T, wpg], [ps])
                    self.act(gt[:, :], ps[:, :], AF.Sigmoid, [ps], [gt])
                    ps2 = self.next_ps()
                    for c in range(2):
                        self.mm(ps2[:, :], pT[:, c, :], wpp[:, c, hs], c == 0, c == 1, [pT, wpp], [ps2])
                    self.tt("dve", gt[:, :], gt[:, :], ps2[:, :], ALU.mult, [gt, ps2], [gt])
                    self.tt("dve", xx[:, hs], xx[:, hs], gt[:, :], ALU.add, [xx, gt], [xx])
                self.act(junk[:, :], xx[:, :], AF.Square, [xx], [junk, ssd], accum=ssd[:, :])
                self.rsqrt(ssd[:, :], ssd[:, :], 1.0 / D, EPS, [ssd], [ssd])
                self.stt("dve", xx[:, :], xx[:, :], ssd[:, 0:1], gfin[:, :], ALU.mult, ALU.mult, [xx, ssd, gfin], [xx])
                P.dma(self.out[r0:r0 + 128, :], xx[:, :], reads=[xx])
            P.barrier()
    P.barrier()


K.phase_d = _build_phase_d


def _prep_alloc(self, es):
    P = self.P
    self.pp_u32 = [P.sb(es, "ub32_%d" % i, [128, 1024], F32) for i in range(2)]
    self.pp_ucv = [P.sb(es, "ucv%d" % i, [128, 8, 128], BF16) for i in range(2)]
    self.pp_vcv = [P.sb(es, "vcv%d" % i, [128, 1024], BF16) for i in range(2)]
    self.pp_v32 = [P.sb(es, "pv32_%d" % i, [128, 1024], F32) for i in range(2)]


def _prep_step(self):
    P = self.P
    c = self.prep_done
    self.prep_done += 1
    u32 = self.pp_u32[c % 2]
    P.dma(u32[:, :], self.inp["peer_u"][c * 128:(c + 1) * 128, :], writes=[u32])
    uc = self.pp_ucv[c % 2]
    for half in range(2):
        ps = self.next_ps()
        for k4 in range(4):
            kc = half * 4 + k4
            self.tr(ps[:, k4 * 128:(k4 + 1) * 128], u32[:, kc * 128:(kc + 1) * 128], self.ident[:, :],
                    [u32, self.ident], [ps])
        self.copy(self.ev_eng(), uc[:, half * 4:(half + 1) * 4, :],
                  ps[:, :].rearrange("p (k e) -> p k e", k=4), [ps], [uc])
    P.dma(self.uTs.t[:, c * 128:(c + 1) * 128].rearrange("(k p) e -> p k e", p=128), uc[:, :, :],
          reads=[uc], writes=[self.uTs])
    v32 = self.pp_v32[c % 2]
    P.dma(v32[:, :], self.inp["peer_v"][c * 128:(c + 1) * 128, :], writes=[v32])
    vc = self.pp_vcv[c % 2]
    self.copy("pool", vc[:, :], v32[:, :], [v32], [vc])
    P.dma(self.vs.t[c * 128:(c + 1) * 128, :], vc[:, :], reads=[vc], writes=[self.vs])


K.prep_alloc = _prep_alloc
K.prep_step = _prep_step


def _skt(self, stg, skT):
    P = self.P
    for hc in range(16):
        st = stg[hc % 2]
        P.dma(st[:, 0:128], self.inp["peer_sub_keys"][hc], writes=[st])
        ps = self.next_ps()
        self.tr(ps[:, 0:128], st[:, 0:128], self.ident[:, :], [st, self.ident], [ps])
        self.copy("dve", skT[:, hc, :], ps[:, 0:128], [ps], [skT])


K._skt = _skt
```
